# Optimizing a Trainium2 kernel written in Bass

```python
import math
import jax, jax.numpy as jnp
from jax import lax
import numpy as np

D_MODEL = 1024
BATCH = 32
SEQ = 2048
DEPTH = 1

N_HEADS = 8
HEAD_DIM = 64
ATT_WIDTH = N_HEADS * HEAD_DIM
PATTERNS = ((128, 1), (512, 4), (2048, 16))
SSM_GROUPS = 16
SSM_GROUP_CH = 16
SSM_WIDTH = SSM_GROUPS * SSM_GROUP_CH
SSM_STATE = 64
D_FF = 2048
CONV_W = 3
IN_WIDTH = 3 * ATT_WIDTH + SSM_WIDTH + 2 * D_MODEL
EPS = 1e-6
NEG_INF = -1e30

kernel_name = "gated_dilated_attn_s5_hybrid_block"


def _rmsnorm(x, g):
    x32 = x.astype(jnp.float32)
    y = x32 * lax.rsqrt(jnp.mean(x32 * x32, axis=-1, keepdims=True) + EPS)
    return y.astype(x.dtype) * g


def _modulate(x, g, shift, scale):
    return _rmsnorm(x, g) * (1 + scale[:, None, :]) + shift[:, None, :]


def _alibi_slopes():
    return np.array([2.0 ** (-8.0 * (h + 1) / N_HEADS) for h in range(N_HEADS)], dtype=np.float32)


def _dilated_window_attention(q, k, v, slopes, window, dilation):
    b, s, h, hd = q.shape
    w = window // dilation
    L = s // dilation
    nb = -(-L // w)
    Lp = nb * w
    X = b * dilation

    def to_sub(t):
        t = t.reshape(b, L, dilation, h, hd).transpose(0, 2, 3, 1, 4)
        return t.reshape(X, h, L, hd)

    qs, ks, vs = to_sub(q), to_sub(k), to_sub(v)
    qb = jnp.pad(qs, ((0, 0), (0, 0), (0, Lp - L), (0, 0))).reshape(X, h, nb, w, hd)
    kp = jnp.pad(ks, ((0, 0), (0, 0), (w, Lp - L), (0, 0)))
    vp = jnp.pad(vs, ((0, 0), (0, 0), (w, Lp - L), (0, 0)))
    kb = jnp.concatenate([kp[:, :, :Lp].reshape(X, h, nb, w, hd),
                          kp[:, :, w:].reshape(X, h, nb, w, hd)], axis=3)
    vb = jnp.concatenate([vp[:, :, :Lp].reshape(X, h, nb, w, hd),
                          vp[:, :, w:].reshape(X, h, nb, w, hd)], axis=3)

    a_idx = np.arange(w)[:, None]
    j_idx = np.arange(2 * w)[None, :]
    dist = (w + a_idx - j_idx).astype(np.float32)
    kpos = np.arange(nb)[:, None, None] * w - w + j_idx[None]
    valid = (dist[None] >= 0) & (dist[None] <= w) & (kpos >= 0)
    bias = -(slopes[:, None, None] * dilation) * dist[None]

    scale = HEAD_DIM ** -0.5
    sc = jnp.einsum('xhnqd,xhnkd->xhnqk', qb, kb).astype(jnp.float32) * scale
    sc = jnp.where(valid, sc + bias[:, None], NEG_INF)
    m = jnp.max(sc, axis=-1, keepdims=True)
    p = jnp.exp(sc - m)
    den = jnp.sum(p, axis=-1, keepdims=True)
    o = jnp.einsum('xhnqk,xhnkd->xhnqd', p, vb.astype(jnp.float32)) / den
    lse = (m + jnp.log(den))[..., 0]

    o = o.reshape(X, h, Lp, hd)[:, :, :L].reshape(b, dilation, h, L, hd)
    o = o.transpose(0, 3, 1, 2, 4).reshape(b, s, h, hd)
    lse = lse.reshape(X, h, Lp)[:, :, :L].reshape(b, dilation, h, L)
    lse = lse.transpose(0, 3, 1, 2).reshape(b, s, h)
    return o, lse


def _s5_branch(u, a_re, a_im, log_dt, b_re, b_im, c_re, c_im, d_skip, w_glu, b_glu):
    f32 = jnp.float32
    bsz, s, _ = u.shape
    lr, li = a_re.astype(f32), a_im.astype(f32)
    dt = jnp.exp(log_dt.astype(f32))[:, None]
    mag = jnp.exp(lr * dt)
    ang = li * dt
    ab_re, ab_im = mag * jnp.cos(ang), mag * jnp.sin(ang)
    nr, ni = ab_re - 1.0, ab_im
    den = lr * lr + li * li
    f_re = (nr * lr + ni * li) / den
    f_im = (ni * lr - nr * li) / den
    br, bi = b_re.astype(f32), b_im.astype(f32)
    bb_re = f_re[..., None] * br - f_im[..., None] * bi
    bb_im = f_re[..., None] * bi + f_im[..., None] * br

    ug = u.astype(f32).reshape(bsz, s, SSM_GROUPS, SSM_GROUP_CH)
    bu_re = jnp.einsum('bsgc,gnc->bsgn', ug, bb_re)
    bu_im = jnp.einsum('bsgc,gnc->bsgn', ug, bb_im)
    a_re_t = jnp.broadcast_to(ab_re, bu_re.shape)
    a_im_t = jnp.broadcast_to(ab_im, bu_re.shape)

    def combine(left, right):
        ar1, ai1, xr1, xi1 = left
        ar2, ai2, xr2, xi2 = right
        return (ar2 * ar1 - ai2 * ai1,
                ar2 * ai1 + ai2 * ar1,
                ar2 * xr1 - ai2 * xi1 + xr2,
                ar2 * xi1 + ai2 * xr1 + xi2)

    _, _, xr, xi = lax.associative_scan(combine, (a_re_t, a_im_t, bu_re, bu_im), axis=1)
    y = (jnp.einsum('bsgn,gcn->bsgc', xr, c_re.astype(f32))
         - jnp.einsum('bsgn,gcn->bsgc', xi, c_im.astype(f32))
         + d_skip.astype(f32).reshape(SSM_GROUPS, SSM_GROUP_CH) * ug)
    y = y.reshape(bsz, s, SSM_WIDTH).astype(u.dtype)
    y = jax.nn.gelu(y)
    return y * jax.nn.sigmoid(y @ w_glu + b_glu)


def _hybrid_mixer(u, w_in, b_gate, a_re, a_im, log_dt, b_re, b_im, c_re, c_im,
                  d_skip, w_glu, b_glu, w_proj_att, w_proj_ssm, w_out):
    bsz, s, _ = u.shape
    proj = u @ w_in
    q, k, v, us, g_att, g_ssm = jnp.split(
        proj, [ATT_WIDTH, 2 * ATT_WIDTH, 3 * ATT_WIDTH, 3 * ATT_WIDTH + SSM_WIDTH,
               3 * ATT_WIDTH + SSM_WIDTH + D_MODEL], axis=-1)
    q = q.reshape(bsz, s, N_HEADS, HEAD_DIM)
    k = k.reshape(bsz, s, N_HEADS, HEAD_DIM)
    v = v.reshape(bsz, s, N_HEADS, HEAD_DIM)

    slopes = _alibi_slopes()
    outs, lses = [], []
    for window, dilation in PATTERNS:
        o, lse = _dilated_window_attention(q, k, v, slopes, window, dilation)
        outs.append(o)
        lses.append(lse)
    wts = jax.nn.softmax(jnp.stack(lses, axis=0), axis=0)
    o_att = jnp.sum(wts[..., None] * jnp.stack(outs, axis=0), axis=0)
    o_att = o_att.reshape(bsz, s, ATT_WIDTH).astype(u.dtype)
    y_att = o_att @ w_proj_att

    y_ssm = _s5_branch(us, a_re, a_im, log_dt, b_re, b_im, c_re, c_im,
                       d_skip, w_glu, b_glu) @ w_proj_ssm

    gb_att, gb_ssm = jnp.split(b_gate, 2, axis=-1)
    merged = jax.nn.sigmoid(g_att + gb_att) * y_att + jax.nn.sigmoid(g_ssm + gb_ssm) * y_ssm
    return merged @ w_out


def _conv_ffn(u, w_up, w_conv, b_conv, w_down):
    s = u.shape[1]
    a, val = jnp.split(u @ w_up, 2, axis=-1)
    ap = jnp.pad(a, ((0, 0), (CONV_W - 1, 0), (0, 0)))
    conv = b_conv
    for j in range(CONV_W):
        conv = conv + w_conv[j] * ap[:, CONV_W - 1 - j:CONV_W - 1 - j + s]
    return (jax.nn.silu(conv) * val) @ w_down


def setup_inputs(seed: int = 0) -> dict:
    key = jax.random.key(seed)
    ks = jax.random.split(key, 32)
    f32 = jnp.float32
    L, D, G, N, C = DEPTH, D_MODEL, SSM_GROUPS, SSM_STATE, SSM_GROUP_CH
    nrm = lambda k, shape, sc: jax.random.normal(k, shape, f32) * sc
    inp = {}
    inp["x"] = nrm(ks[0], (BATCH, SEQ, D), 1.0)
    inp["c"] = nrm(ks[1], (BATCH, D), 1.0)
    inp["w_ada"] = nrm(ks[2], (L, D, 6 * D), 0.5 * D ** -0.5)
    inp["b_ada"] = nrm(ks[3], (L, 6 * D), 0.02)
    inp["g_mix"] = 1.0 + nrm(ks[4], (L, D), 0.02)
    inp["w_in"] = nrm(ks[5], (L, D, IN_WIDTH), D ** -0.5)
    inp["b_gate"] = nrm(ks[6], (L, 2 * D), 0.02)
    inp["a_re"] = -0.5 + nrm(ks[7], (L, G, N), 0.01)
    inp["a_im"] = jnp.pi * jnp.arange(N, dtype=f32)[None, None, :] + nrm(ks[8], (L, G, N), 0.01)
    inp["log_dt"] = jax.random.uniform(ks[9], (L, G), f32, math.log(1e-3), math.log(1e-1))
    inp["b_re"] = nrm(ks[10], (L, G, N, C), (2 * C) ** -0.5)
    inp["b_im"] = nrm(ks[11], (L, G, N, C), (2 * C) ** -0.5)
    inp["c_re"] = nrm(ks[12], (L, G, C, N), (2 * N) ** -0.5)
    inp["c_im"] = nrm(ks[13], (L, G, C, N), (2 * N) ** -0.5)
    inp["d_skip"] = nrm(ks[14], (L, SSM_WIDTH), 1.0)
    inp["w_glu"] = nrm(ks[15], (L, SSM_WIDTH, SSM_WIDTH), SSM_WIDTH ** -0.5)
    inp["b_glu"] = nrm(ks[16], (L, SSM_WIDTH), 0.02)
    inp["w_proj_att"] = nrm(ks[17], (L, ATT_WIDTH, D), ATT_WIDTH ** -0.5)
    inp["w_proj_ssm"] = nrm(ks[18], (L, SSM_WIDTH, D), SSM_WIDTH ** -0.5)
    inp["w_out"] = nrm(ks[19], (L, D, D), D ** -0.5)
    inp["g_ffn"] = 1.0 + nrm(ks[20], (L, D), 0.02)
    inp["w_up"] = nrm(ks[21], (L, D, 2 * D_FF), D ** -0.5)
    inp["w_conv"] = nrm(ks[22], (L, CONV_W, D_FF), CONV_W ** -0.5)
    inp["b_conv"] = nrm(ks[23], (L, D_FF), 0.02)
    inp["w_down"] = nrm(ks[24], (L, D_FF, D), D_FF ** -0.5)
    inp["g_final"] = 1.0 + nrm(ks[25], (D,), 0.02)
    return inp


def reference(x, c, w_ada, b_ada, g_mix, w_in, b_gate, a_re, a_im, log_dt, b_re, b_im,
              c_re, c_im, d_skip, w_glu, b_glu, w_proj_att, w_proj_ssm, w_out,
              g_ffn, w_up, w_conv, b_conv, w_down, g_final):
    h = x
    c_act = jax.nn.silu(c)
    for l in range(DEPTH):
        mod = c_act @ w_ada[l] + b_ada[l]
        sh1, sc1, gt1, sh2, sc2, gt2 = jnp.split(mod, 6, axis=-1)
        u = _modulate(h, g_mix[l], sh1, sc1)
        h = h + gt1[:, None, :] * _hybrid_mixer(
            u, w_in[l], b_gate[l], a_re[l], a_im[l], log_dt[l], b_re[l], b_im[l],
            c_re[l], c_im[l], d_skip[l], w_glu[l], b_glu[l],
            w_proj_att[l], w_proj_ssm[l], w_out[l])
        u = _modulate(h, g_ffn[l], sh2, sc2)
        h = h + gt2[:, None, :] * _conv_ffn(u, w_up[l], w_conv[l], b_conv[l], w_down[l])
    return _rmsnorm(h, g_final)
```

```python
import math
from contextlib import ExitStack

import numpy as np
import concourse.bass as bass
import concourse.mybir as mybir
from concourse.bass_utils import run_bass_kernel_spmd

F32 = mybir.dt.float32
BF16 = mybir.dt.bfloat16
I32 = mybir.dt.int32
AF = mybir.ActivationFunctionType
ALU = mybir.AluOpType

NCORES = 8
D = 1024
S = 2048
NT = 4
EPS = 1e-6
SHIFT = 8.0
TWO_PI = 2.0 * math.pi
PI_LO = 3.1415925
ENGS = ("pe", "act", "dve", "pool", "sp")

V_GMIX, V_GFFN, V_GFIN, V_BGATE, V_BGLU, V_WCONV, V_BCONV, V_ARE, V_AIM, V_LDT, V_BADA = (
    0, 8, 16, 24, 40, 42, 90, 106, 114, 122, 130)
NV = 178


class Prog:
    def __init__(self, nc, stack, n_dma_sems=8):
        self.nc = nc
        self.ops = {e: [] for e in ENGS}
        self.cnt = {e: 0 for e in ENGS}
        self.semobj = {}
        for e in ENGS:
            self.semobj[("c", e)] = stack.enter_context(nc.semaphore("c_" + e))
        self.dval = {q: [0] * n_dma_sems for q in ("sp", "pool")}
        self.dnext = {q: 0 for q in ("sp", "pool")}
        for q in ("sp", "pool"):
            for i in range(n_dma_sems):
                self.semobj[("d", q, i)] = stack.enter_context(nc.semaphore("d_%s%d" % (q, i)))
        self.waited = {e: {} for e in ENGS}
        self.regions = {}
        self.fence = []
        self.arena_toks = {}

    def _need(self, eng, waits, tok):
        if tok is None:
            return
        sid, val = tok
        if self.waited[eng].get(sid, 0) >= val:
            return
        if waits.get(sid, 0) < val:
            waits[sid] = val

    def emit(self, eng, fn, reads=(), writes=(), dma=False, arena=False):
        waits = {}
        own = ("c", eng)
        if dma and arena:
            for t in self.fence:
                self._need(eng, waits, t)
        for key in reads:
            r = self.regions.get(key)
            if r is not None:
                self._need(eng, waits, r["w"])
        for key in writes:
            r = self.regions.get(key)
            if r is not None:
                w = r["w"]
                if w is not None and not (w[0] == own and not dma):
                    self._need(eng, waits, w)
                for sid, val in r["r"].items():
                    if sid == own and not dma:
                        continue
                    self._need(eng, waits, (sid, val))
        if dma:
            i = self.dnext[eng]
            self.dnext[eng] = (i + 1) % len(self.dval[eng])
            sid = ("d", eng, i)
            if self.dval[eng][i] > 0:
                self._need(eng, waits, (sid, self.dval[eng][i]))
            self.dval[eng][i] += 16
            tok = (sid, self.dval[eng][i])
            amt = 16
            if arena:
                self.arena_toks[sid] = tok[1]
        else:
            self.cnt[eng] += 1
            tok = (own, self.cnt[eng])
            amt = 1
        for sid, val in waits.items():
            self.waited[eng][sid] = val
        for key in reads:
            r = self.regions.setdefault(key, {"w": None, "r": {}})
            if r["r"].get(tok[0], 0) < tok[1]:
                r["r"][tok[0]] = tok[1]
        for key in writes:
            self.regions[key] = {"w": tok, "r": {}}
        self.ops[eng].append((list(waits.items()), fn, tok, amt))
        return tok

    def _all_toks(self):
        toks = []
        for e in ENGS:
            if self.cnt[e] > 0:
                toks.append((("c", e), self.cnt[e]))
        for q in ("sp", "pool"):
            for i, v in enumerate(self.dval[q]):
                if v > 0:
                    toks.append((("d", q, i), v))
        return toks

    def barrier(self):
        toks = self._all_toks()
        for e in ENGS:
            waits = {}
            for t in toks:
                if t[0] == ("c", e):
                    continue
                self._need(e, waits, t)
            for sid, val in waits.items():
                self.waited[e][sid] = val
            if waits:
                self.ops[e].append((list(waits.items()), None, None, 0))

    def final_wait(self, eng="sp"):
        waits = {}
        for t in self._all_toks():
            if t[0] == ("c", eng):
                continue
            self._need(eng, waits, t)
        self.ops[eng].append((list(waits.items()), None, None, 0))

    def build(self):
        nc = self.nc
        with nc.Block() as block:
            def mk(e):
                def body(h):
                    for waits, fn, tok, amt in self.ops[e]:
                        for sid, val in waits:
                            h.wait_ge(self.semobj[sid], val)
                        if fn is None:
                            continue
                        ins = fn(h)
                        ins.then_inc(self.semobj[tok[0]], amt)
                return body
            block.tensor(mk("pe"))
            block.scalar(mk("act"))
            block.vector(mk("dve"))
            block.gpsimd(mk("pool"))
            block.sync(mk("sp"))


def MM(lst):
    def fn(e):
        ins = None
        for (out, lhsT, rhs, st, sp) in lst:
            ins = e.matmul(out, lhsT=lhsT, rhs=rhs, start=st, stop=sp)
        return ins
    return fn


def ACTF(out, in_, func, bias=None, scale=None):
    def fn(e):
        kw = {}
        if bias is not None:
            kw["bias"] = bias
        if scale is not None:
            kw["scale"] = scale
        return e.activation(out=out, in_=in_, func=func, **kw)
    return fn


def TT(out, in0, in1, op):
    return lambda e: e.tensor_tensor(out=out, in0=in0, in1=in1, op=op)


def TS(out, in0, s1, s2, op0, op1=None):
    if op1 is None:
        return lambda e: e.tensor_scalar(out=out, in0=in0, scalar1=s1, scalar2=None, op0=op0)
    return lambda e: e.tensor_scalar(out=out, in0=in0, scalar1=s1, scalar2=s2, op0=op0, op1=op1)


def STT(out, in0, scalar, in1, op0, op1):
    return lambda e: e.scalar_tensor_tensor(out=out, in0=in0, scalar=scalar, in1=in1, op0=op0, op1=op1)


def CP(out, in_):
    return lambda e: e.tensor_copy(out=out, in_=in_)


def ACP(out, in_):
    return lambda e: e.activation(out=out, in_=in_, func=AF.Copy)


def DMA(out, in_):
    return lambda e: e.dma_start(out=out, in_=in_)


def SCAN(out, d0, d1, init):
    return lambda e: e.tensor_tensor_scan(out=out, data0=d0, data1=d1, initial=init, op0=ALU.mult, op1=ALU.add)


def RECIP(out, in_):
    return lambda e: e.reciprocal(out=out, in_=in_)


def MEMSET(ap, v):
    return lambda e: e.memset(ap, v)


def TRANSP(out, in_, ident):
    return lambda e: e.transpose(out=out, in_=in_, identity=ident)


def xap(base, start, dims):
    return bass.AP(base.tensor, base.offset + start, [list(base.ap[0])] + [list(d) for d in dims])


def build_nc(NB):
    nc = bass.Bass("TRN2", target_bir_lowering=False)

    def din(name, shape, dtype=F32):
        return nc.dram_tensor(name, list(shape), dtype, kind="ExternalInput").ap()

    xT = din("xT", [NB, D, S])
    cT = din("cT", [128, 8, NB])
    w_ada = din("w_ada", [D, 6 * D])
    w_in = din("w_in_p", [D, 3840])
    vec = din("vec", [128, NV])
    bre_x = din("bre_x", [128, 8, 128])
    bim_x = din("bim_x", [128, 8, 128])
    cre_x = din("cre_x", [128, 8, 128])
    cim_x = din("cim_x", [128, 8, 128])
    d_x = din("d_x", [128, 2, 128])
    ident_d = din("ident", [128, 128])
    iota_d = din("iota", [128, 512])
    E_d = din("E", [128, 48, 128])
    w_glu = din("w_glu", [256, 256])
    w_pa = din("w_proj_att", [512, D])
    w_ps = din("w_proj_ssm", [256, D])
    w_out = din("w_out", [D, D])
    w_up = din("w_up", [D, 4096])
    w_down = din("w_down", [2048, D])
    outT = nc.dram_tensor("outT", [NB, D, S], F32, kind="ExternalOutput").ap()
    trig_scr = nc.dram_tensor("trig_scr", [128, 2, 8, 512], F32).ap()
    btab_scr = nc.dram_tensor("btab_scr", [128, 8, 2, 128], BF16).ap()

    with ExitStack() as st:
        P = Prog(nc, st)
        sb = lambda name, shape, dt: st.enter_context(nc.sbuf_tensor(name, list(shape), dt))
        A_h = sb("A_h", [128, 16384], F32)
        A_u = sb("A_u", [128, 16384], BF16)
        A_m = sb("A_m", [128, 16384], BF16)
        oatt = sb("oatt", [128, 4, S], BF16)
        ssmo = sb("ssmo", [128, 2, S], BF16)
        usT = sb("usT", [128, 2, S], BF16)
        NSLOT = 4
        slots = [sb("slot%d" % i, [128, 4096], BF16) for i in range(NSLOT)]
        p3buf = sb("p3buf", [128, 4, 514], F32)
        sm = sb("sm", [128, 768], F32)
        csb = sb("csb", [128, 8, NB], BF16)
        onesb = sb("onesb", [128, 128], BF16)
        ps = [st.enter_context(nc.psum_tensor("ps%d" % i, [128, 512], F32)) for i in range(8)]

        hT = A_h[:, :].rearrange("p (k t) -> p k t", k=8)
        uT = A_u[:, :].rearrange("p (k t) -> p k t", k=8)
        mT = A_m[:, :].rearrange("p (k t) -> p k t", k=8)
        Ahb = A_h[:, :].bitcast(BF16)
        Amf = A_m[:, :].bitcast(F32)

        SM_VEC = 0
        SM_MOD = 192
        SM_GS1 = SM_MOD + 48 * NB
        SM_GS2 = SM_GS1 + 8 * NB
        SM_SSM = SM_GS2 + 8 * NB
        assert SM_SSM + 26 <= 768
        eps_col = sm[:, SM_SSM + 24: SM_SSM + 25]
        nshift_col = sm[:, SM_SSM + 25: SM_SSM + 26]
        smv = lambda off, n=1: sm[:, off:off + n]
        mod_col = lambda chunk, b: sm[:, SM_MOD + chunk * NB + b: SM_MOD + chunk * NB + b + 1]
        gs1_col = lambda k, b: sm[:, SM_GS1 + k * NB + b: SM_GS1 + k * NB + b + 1]
        gs2_col = lambda k, b: sm[:, SM_GS2 + k * NB + b: SM_GS2 + k * NB + b + 1]
        mag_col = lambda sc: sm[:, SM_SSM + sc: SM_SSM + sc + 1]
        carr_col = lambda sc: sm[:, SM_SSM + 8 + sc: SM_SSM + 9 + sc]
        cari_col = lambda sc: sm[:, SM_SSM + 16 + sc: SM_SSM + 17 + sc]

        state = {"bank": 0, "slot": 0, "ev": 0}

        def nb():
            i = state["bank"]
            state["bank"] = (i + 1) % 6
            return i

        def bk(i):
            return ("ps", i)

        def ev():
            state["ev"] ^= 1
            return "act" if state["ev"] else "dve"

        def evac(out, in_, reads, writes):
            e = ev()
            P.emit(e, ACP(out, in_) if e == "act" else CP(out, in_), reads=reads, writes=writes)

        def load_w(src2d, nk, ncols, slot=None, off=0):
            if slot is None:
                i = state["slot"]
                state["slot"] = (i + 1) % NSLOT
            else:
                i = slot
            n = nk * ncols
            keys = [("slot", i, h) for h in range(2) if off < (h + 1) * 2048 and off + n > h * 2048]
            view = slots[i][:, off:off + n].rearrange("p (k c) -> p k c", k=nk)
            P.emit("pool", DMA(view, src2d.rearrange("(k p) c -> p k c", p=128)), writes=keys, dma=True)
            return keys, view

        tl = lambda t: slice(t * 512, (t + 1) * 512)

        P.emit("sp", DMA(sm[:, 0:NV], vec[:, :]), writes=["vec"], dma=True, arena=True)
        P.emit("dve", MEMSET(onesb[:, :], 1.0), writes=["ones"])
        P.emit("dve", MEMSET(eps_col, EPS), writes=["consts"])
        P.emit("dve", MEMSET(nshift_col, -SHIFT), writes=["consts"])
        ctmp = A_h[:, 0:8 * NB].rearrange("p (k b) -> p k b", k=8)
        P.emit("sp", DMA(ctmp, cT[:, :, :]), writes=["ctmp"], dma=True, arena=True)
        P.emit("act", ACTF(csb[:, :, :], ctmp, AF.Silu), reads=["ctmp"], writes=["csb"])
        for jb in range(12):
            key, wv = load_w(w_ada[:, jb * 512:(jb + 1) * 512], 8, 512)
            for jj in range(4):
                j = jb * 4 + jj
                b_ = nb()
                P.emit("pe", MM([(ps[b_][:, 0:NB], wv[:, k, jj * 128:(jj + 1) * 128], csb[:, k, :], k == 0, k == 7)
                                 for k in range(8)]), reads=key + ["csb"], writes=[bk(b_)])
                P.emit("dve", TS(sm[:, SM_MOD + j * NB: SM_MOD + (j + 1) * NB], ps[b_][:, 0:NB],
                                 smv(V_BADA + j), None, ALU.add), reads=[bk(b_), "vec"], writes=["mod"])
        for k in range(8):
            P.emit("dve", TS(sm[:, SM_GS1 + k * NB: SM_GS1 + (k + 1) * NB],
                             sm[:, SM_MOD + (8 + k) * NB: SM_MOD + (9 + k) * NB], 1.0, smv(V_GMIX + k), ALU.add, ALU.mult),
                   reads=["mod", "vec"], writes=["gs"])
            P.emit("dve", TS(sm[:, SM_GS2 + k * NB: SM_GS2 + (k + 1) * NB],
                             sm[:, SM_MOD + (32 + k) * NB: SM_MOD + (33 + k) * NB], 1.0, smv(V_GFFN + k), ALU.add, ALU.mult),
                   reads=["mod", "vec"], writes=["gs"])

        pt = lambda i: A_h[:, 512 + 8 * i: 512 + 8 * (i + 1)]
        Xre = A_h[:, 1024:2048].rearrange("p (s c) -> p s c", s=8)
        Xim = A_h[:, 2048:3072].rearrange("p (s c) -> p s c", s=8)
        bre = A_h[:, 3072:4096].rearrange("p (s c) -> p s c", s=8)
        bim = A_h[:, 4096:5120].rearrange("p (s c) -> p s c", s=8)
        iota = A_h[:, 5120:5632]
        ident = A_h[:, 5632:5760]
        P.emit("sp", DMA(bre, bre_x[:, :, :]), writes=["bre"], dma=True, arena=True)
        P.emit("sp", DMA(bim, bim_x[:, :, :]), writes=["bim"], dma=True, arena=True)
        P.emit("sp", DMA(iota, iota_d[:, :]), writes=["iota"], dma=True, arena=True)
        P.emit("sp", DMA(ident, ident_d[:, :]), writes=["ident"], dma=True, arena=True)
        lr, li, ldt = smv(V_ARE, 8), smv(V_AIM, 8), smv(V_LDT, 8)
        T = {}

        def pe_(name, eng, fn, reads):
            P.emit(eng, fn, reads=reads + ["vec"], writes=[name])

        names = ["dt", "lrdt", "th", "ki", "kf", "thr", "sin", "ab", "cos", "abr", "abi", "nr", "t1", "t2", "den",
                 "rden", "u1", "u2", "fre", "fim", "nfim", "u3", "u4"]
        for i, n in enumerate(names):
            T[n] = pt(i)
        T["ki"] = pt(names.index("ki")).bitcast(I32)
        pe_("dt", "act", ACTF(T["dt"], ldt, AF.Exp), [])
        pe_("lrdt", "dve", TT(T["lrdt"], lr, T["dt"], ALU.mult), ["dt"])
        pe_("mag", "act", ACTF(smv(SM_SSM, 8), T["lrdt"], AF.Exp), ["lrdt"])
        pe_("th", "dve", TT(T["th"], li, T["dt"], ALU.mult), ["dt"])
        pe_("ki", "dve", TS(T["ki"], T["th"], 1.0 / TWO_PI, None, ALU.mult), ["th"])
        pe_("kf", "dve", CP(T["kf"], T["ki"]), ["ki"])
        pe_("thr", "dve", STT(T["thr"], T["kf"], -TWO_PI, T["th"], ALU.mult, ALU.add), ["kf", "th"])
        pe_("thr", "dve", TS(T["thr"], T["thr"], PI_LO, -PI_LO, ALU.min, ALU.max), ["thr"])
        pe_("sin", "act", ACTF(T["sin"], T["thr"], AF.Sin), ["thr"])
        pe_("ab", "act", ACTF(T["ab"], T["thr"], AF.Abs), ["thr"])
        pe_("ab2", "dve", TS(T["u4"], T["ab"], -1.0, math.pi / 2, ALU.mult, ALU.add), ["ab"])
        pe_("cos", "act", ACTF(T["cos"], T["u4"], AF.Sin), ["ab2"])
        pe_("abr", "dve", TT(T["abr"], smv(SM_SSM, 8), T["cos"], ALU.mult), ["mag", "cos"])
        pe_("abi", "dve", TT(T["abi"], smv(SM_SSM, 8), T["sin"], ALU.mult), ["mag", "sin"])
        pe_("nr", "dve", TS(T["nr"], T["abr"], -1.0, None, ALU.add), ["abr"])
        pe_("t1", "dve", TT(T["t1"], lr, lr, ALU.mult), [])
        pe_("t2", "dve", TT(T["t2"], li, li, ALU.mult), [])
        pe_("den", "dve", TT(T["den"], T["t1"], T["t2"], ALU.add), ["t1", "t2"])
        pe_("rden", "dve", RECIP(T["rden"], T["den"]), ["den"])
        pe_("u1", "dve", TT(T["u1"], T["nr"], lr, ALU.mult), ["nr"])
        pe_("u2", "dve", TT(T["u2"], T["abi"], li, ALU.mult), ["abi"])
        pe_("u3", "dve", TT(T["u3"], T["u1"], T["u2"], ALU.add), ["u1", "u2"])
        pe_("fre", "dve", TT(T["fre"], T["u3"], T["rden"], ALU.mult), ["u3", "rden"])
        pe_("u1b", "dve", TT(T["u1"], T["abi"], lr, ALU.mult), ["abi", "u3"])
        pe_("u2b", "dve", TT(T["u2"], T["nr"], li, ALU.mult), ["nr", "u3"])
        pe_("u3b", "dve", TT(T["u3"], T["u1"], T["u2"], ALU.subtract), ["u1b", "u2b", "fre"])
        pe_("fim", "dve", TT(T["fim"], T["u3"], T["rden"], ALU.mult), ["u3b", "rden"])
        pe_("nfim", "dve", TS(T["nfim"], T["fim"], -1.0, None, ALU.mult), ["fim"])
        btab = Ahb[:, 2 * 13312: 2 * 13312 + 2048].rearrange("p (s r c) -> p s r c", s=8, r=2)
        for sc in range(8):
            c1 = lambda t, sc=sc: t[:, sc:sc + 1]
            P.emit("dve", TS(Xre[:, sc, :], bre[:, sc, :], c1(T["fre"]), None, ALU.mult), reads=["bre", "fre"], writes=[("Xre", sc)])
            P.emit("dve", STT(Xre[:, sc, :], bim[:, sc, :], c1(T["nfim"]), Xre[:, sc, :], ALU.mult, ALU.add),
                   reads=["bim", "nfim", ("Xre", sc)], writes=[("Xre", sc)])
            P.emit("dve", TS(Xim[:, sc, :], bre[:, sc, :], c1(T["fim"]), None, ALU.mult), reads=["bre", "fim"], writes=[("Xim", sc)])
            P.emit("dve", STT(Xim[:, sc, :], bim[:, sc, :], c1(T["fre"]), Xim[:, sc, :], ALU.mult, ALU.add),
                   reads=["bim", "fre", ("Xim", sc)], writes=[("Xim", sc)])
            for ri, X in enumerate((Xre, Xim)):
                b_ = nb()
                P.emit("pe", TRANSP(ps[b_][:, 0:128], X[:, sc, :], ident), reads=[("Xre" if ri == 0 else "Xim", sc), "ident"],
                       writes=[bk(b_)])
                evac(btab[:, sc, ri, :], ps[b_][:, 0:128], [bk(b_)], [("btab", sc, ri)])
        P.emit("sp", DMA(btab_scr[:, :, :, :], btab), reads=[("btab", sc, ri) for sc in range(8) for ri in range(2)],
               writes=["btab_scr"], dma=True, arena=True)
        for sc in range(8):
            base = 6144 + (sc % 2) * 3072
            arg = A_h[:, base:base + 512]
            ki = A_h[:, base + 512:base + 1024].bitcast(I32)
            kf = A_h[:, base + 1024:base + 1536]
            so = A_h[:, base + 1536:base + 2048]
            ab = A_h[:, base + 2048:base + 2560]
            co = A_h[:, base + 2560:base + 3072]
            kk = lambda n, sc=sc: ("tg", n, sc % 2)
            P.emit("dve", TS(arg, iota, T["thr"][:, sc:sc + 1], None, ALU.mult), reads=["iota", "thr"], writes=[kk("arg")])
            P.emit("dve", TS(ki, arg, 1.0 / TWO_PI, None, ALU.mult), reads=[kk("arg")], writes=[kk("ki")])
            P.emit("dve", CP(kf, ki), reads=[kk("ki")], writes=[kk("kf")])
            P.emit("dve", STT(arg, kf, -TWO_PI, arg, ALU.mult, ALU.add), reads=[kk("kf"), kk("arg")], writes=[kk("arg")])
            P.emit("dve", TS(arg, arg, PI_LO, -PI_LO, ALU.min, ALU.max), reads=[kk("arg")], writes=[kk("arg")])
            P.emit("act", ACTF(so, arg, AF.Sin), reads=[kk("arg")], writes=[kk("so")])
            P.emit("act", ACTF(ab, arg, AF.Abs), reads=[kk("arg")], writes=[kk("ab")])
            P.emit("dve", TS(ab, ab, -1.0, math.pi / 2, ALU.mult, ALU.add), reads=[kk("ab")], writes=[kk("ab")])
            P.emit("act", ACTF(co, ab, AF.Sin), reads=[kk("ab")], writes=[kk("co")])
            P.emit("sp", DMA(trig_scr[:, 0, sc, :], co), reads=[kk("co")], writes=[("trig_scr", 0, sc)], dma=True, arena=True)
            P.emit("sp", DMA(trig_scr[:, 1, sc, :], so), reads=[kk("so")], writes=[("trig_scr", 1, sc)], dma=True, arena=True)

        def mtk(chunks):
            return [("mT", c, tt) for c in chunks for tt in range(NT)]

        def norm_mod(b, gs_col, sh_chunk0, dst):
            for t in range(NT):
                sq = A_m[:, (t % 2) * 4096:(t % 2) * 4096 + 4096].rearrange("p (k c) -> p k c", k=8)
                rstd = Amf[:, 4096 + (t % 2) * 512: 4096 + (t % 2) * 512 + 512]
                sqk = mtk((2 * (t % 2), 2 * (t % 2) + 1))
                P.emit("act", ACTF(sq, hT[:, :, tl(t)], AF.Square), reads=[("hT", k, t) for k in range(8)],
                       writes=[("nsq", t % 2)] + sqk)
                b_ = nb()
                P.emit("pe", MM([(ps[b_][:, :], onesb[:, :], sq[:, k, :], k == 0, k == 7) for k in range(8)]),
                       reads=[("nsq", t % 2), "ones"] + sqk, writes=[bk(b_)])
                P.emit("act", ACTF(rstd, ps[b_][:, :], AF.Ln, bias=eps_col, scale=1.0 / D),
                       reads=[bk(b_), "consts"], writes=[("nrstd", t % 2)] + mtk((4,)))
                P.emit("act", ACTF(rstd, rstd, AF.Exp, scale=-0.5), reads=[("nrstd", t % 2)], writes=[("nrstd", t % 2)] + mtk((4,)))
                for k in range(8):
                    tmp = Amf[:, 5120 + (k % 2) * 512: 5120 + (k % 2) * 512 + 512]
                    P.emit("dve", TT(tmp, hT[:, k, tl(t)], rstd, ALU.mult), reads=[("hT", k, t), ("nrstd", t % 2)] + mtk((4,)),
                           writes=[("ntmp", k % 2)] + mtk((5,)))
                    P.emit("act", ACTF(dst[:, k, tl(t)], tmp, AF.Identity, bias=mod_col(sh_chunk0 + k, b), scale=gs_col(k, b)),
                           reads=[("ntmp", k % 2), "mod", "gs"] + mtk((5,)), writes=[("uT", k, t)])

        Etabs = [Ahb[:, i * 1536:(i + 1) * 1536].rearrange("p (i q) -> p i q", i=12) for i in range(2)]
        qT = Ahb[:, 3072:5120]
        kT = Ahb[:, 5120:7168]
        Vt = [Ahb[:, 7168 + i * 2048: 7168 + (i + 1) * 2048].rearrange("p (b c) -> p b c", b=16) for i in range(3)]
        Pb = [Ahb[:, 13312 + i * 512: 13312 + (i + 1) * 512] for i in range(8)]
        acc_o = A_h[:, 8704:10752]
        acc_d = A_h[:, 10752:12800]

        def load_E(hp):
            P.emit("pool", DMA(Etabs[hp % 2], E_d[:, hp * 12:(hp + 1) * 12, :]), writes=[("Etab", hp % 2)], dma=True, arena=True)

        def tokset(pat, blk):
            if pat == 0:
                return 128 * blk, 1
            if pat == 1:
                return 512 * (blk % 4) + blk // 4, 4
            return blk, 16

        def cs_(ap2d, pat, blk):
            base, stp = tokset(pat, blk)
            return ap2d[:, base: base + 127 * stp + 1: stp]

        def acc_view(acc, pat, grp):
            if pat == 0:
                return acc[:, 512 * grp: 512 * grp + 512].rearrange("p (j a) -> p j a", j=4)
            if pat == 1:
                return xap(acc[:, 0:1], 512 * grp, [[1, 4], [4, 128]])
            return xap(acc[:, 0:1], 4 * grp, [[1, 4], [16, 128]])

        pbi = [0]

        def attention_hp(b, hp, key, wv, tick):
            if hp < 3:
                load_E(hp + 1)
            for which, dst, nm in ((0, qT, "qT"), (1, kT, "kT")):
                for t in range(NT):
                    b_ = nb()
                    P.emit("pe", MM([(ps[b_][:, :], wv[:, k, which * 128:(which + 1) * 128], uT[:, k, tl(t)], k == 0, k == 7)
                                     for k in range(8)]), reads=key + [("uT", k, t) for k in range(8)], writes=[bk(b_)])
                    evac(dst[:, tl(t)], ps[b_][:, :], [bk(b_)], [(nm, t)])
            for pat in range(3):
                for b4 in range(4):
                    b_ = nb()
                    lst = []
                    for jj in range(4):
                        blk = b4 * 4 + jj
                        for k in range(8):
                            lst.append((ps[b_][:, jj * 128:(jj + 1) * 128], cs_(uT[:, k, :], pat, blk), wv[:, k, 256:384],
                                        k == 0, k == 7))
                    P.emit("pe", MM(lst), reads=key + [("uT", k, t) for k in range(8) for t in range(NT)], writes=[bk(b_)])
                    evac(Vt[pat][:, b4 * 4:(b4 + 1) * 4, :], ps[b_][:, :].rearrange("p (j c) -> p j c", j=4), [bk(b_)],
                         [("V", pat, b4)])
            units = [(pat, grp, hd) for pat in range(3) for grp in range(4) for hd in range(2)]
            pend = None
            obank = {}

            def emit_qk(u):
                pat, grp, hd = u
                h = hp * 2 + hd
                rows = slice(hd * 64, hd * 64 + 64)
                info = {"cur": None, "prev": None, "mask": []}
                for typ in ("cur", "prev"):
                    lst = []
                    js = []
                    for j in range(4):
                        if pat == 0:
                            qb = 4 * grp + j
                            kb = qb if typ == "cur" else qb - 1
                            ok = kb >= 0
                        elif pat == 1:
                            qb = j * 4 + grp
                            kb = qb if typ == "cur" else qb - 1
                            ok = (typ == "cur") or grp >= 1
                        else:
                            qb = 4 * grp + j
                            kb = qb
                            ok = typ == "cur"
                        if ok:
                            js.append((j, qb, kb))
                    if not js:
                        continue
                    b_ = nb()
                    for (j, qb, kb) in js:
                        lst.append((ps[b_][:, j * 128:(j + 1) * 128], cs_(kT[rows, :], pat, kb), cs_(qT[rows, :], pat, qb), True, True))
                    P.emit("pe", MM(lst), reads=[("qT", t) for t in range(NT)] + [("kT", t) for t in range(NT)], writes=[bk(b_)])
                    j0 = js[0][0]
                    pi = pbi[0]
                    pbi[0] = (pi + 1) % 8
                    pv = Pb[pi][:, j0 * 128:512]
                    P.emit("act", ACTF(pv, ps[b_][:, j0 * 128:512], AF.Exp, bias=nshift_col, scale=0.125),
                           reads=[bk(b_), "consts"], writes=[("Pb", pi)])
                    ei = (hd * 3 + pat) * 2 + (0 if typ == "cur" else 1)
                    nj = 4 - j0
                    ebc = xap(Etabs[hp % 2][:, ei, 0:1], 0, [[0, nj], [1, 128]])
                    pv3 = pv.rearrange("p (j a) -> p j a", j=nj)
                    info["mask"].append((pv3, ebc, pi))
                    info[typ] = (pi, js)
                return info

            def emit_mask(info):
                for (pv3, ebc, pi) in info["mask"]:
                    P.emit("pool", TT(pv3, pv3, ebc, ALU.mult), reads=[("Pb", pi), ("Etab", hp % 2)], writes=[("Pb", pi)])

            def emit_pv(u, info):
                pat, grp, hd = u
                rows = slice(hd * 64, hd * 64 + 64)
                if hd == 0:
                    obank[(pat, grp)] = (6, 7)
                bo, bd = obank[(pat, grp)]
                lst = []
                reads = [("V", pat, b4) for b4 in range(4)] + ["ones"]
                pcur, jcur = info["cur"]
                reads.append(("Pb", pcur))
                prevmap = {}
                if info["prev"] is not None:
                    pprev, jprev = info["prev"]
                    reads.append(("Pb", pprev))
                    prevmap = {j: kb for (j, qb, kb) in jprev}
                for (j, qb, kb) in jcur:
                    oc = slice(j * 128, (j + 1) * 128)
                    hasp = j in prevmap
                    for dst, isden in ((ps[bo], False), (ps[bd], True)):
                        if hasp:
                            lw = onesb[:, 0:64] if isden else Vt[pat][:, prevmap[j], rows]
                            lst.append((dst[rows, oc], lw, Pb[pprev][:, oc], True, False))
                        lw = onesb[:, 0:64] if isden else Vt[pat][:, kb, rows]
                        lst.append((dst[rows, oc], lw, Pb[pcur][:, oc], not hasp, True))
                P.emit("pe", MM(lst), reads=reads, writes=[bk(bo), bk(bd)])
                if hd == 1:
                    gk = [grp] if pat < 2 else [0, 1, 2, 3]
                    for acc, bb, nm in ((acc_o, bo, "acc_o"), (acc_d, bd, "acc_d")):
                        av = acc_view(acc, pat, grp)
                        src = ps[bb][:, :].rearrange("p (j a) -> p j a", j=4)
                        if pat == 0:
                            evac(av, src, [bk(bb)], [(nm, g) for g in gk])
                        else:
                            P.emit("dve", TT(av, av, src, ALU.add), reads=[bk(bb)] + [(nm, g) for g in gk],
                                   writes=[(nm, g) for g in gk])

            infos = []
            for ui, u in enumerate(units):
                infos.append(emit_qk(u))
                tick()
                if ui >= 1:
                    emit_mask(infos[ui - 1])
                if ui >= 2:
                    emit_pv(units[ui - 2], infos[ui - 2])
                tick()
            nU = len(units)
            emit_mask(infos[nU - 1])
            emit_pv(units[nU - 2], infos[nU - 2])
            emit_pv(units[nU - 1], infos[nU - 1])
            P.emit("act", ACTF(acc_d, acc_d, AF.Ln), reads=[("acc_d", g) for g in range(4)], writes=[("acc_d", g) for g in range(4)])
            P.emit("act", ACTF(acc_d, acc_d, AF.Exp, scale=-1.0), reads=[("acc_d", g) for g in range(4)], writes=[("acc_d", g) for g in range(4)])
            P.emit("dve", TT(oatt[:, hp, :], acc_o, acc_d, ALU.mult), reads=[("acc_o", g) for g in range(4)] + [("acc_d", g) for g in range(4)],
                   writes=[("oatt", hp)])

        cosT = slots[3][:, :].bitcast(F32).rearrange("p (s c) -> p s c", s=4)
        sinT = p3buf[:, :, 0:512]
        wk = lambda i: A_h[:, 12800 + i * 512: 12800 + (i + 1) * 512]
        xb = lambda i: A_m[:, 12288 + i * 512: 12288 + (i + 1) * 512]
        Btab = A_m[:, 0:2048].rearrange("p (s r c) -> p s r c", s=8, r=2)
        Cre = A_m[:, 2048:3072].rearrange("p (s c) -> p s c", s=8)
        Cim = A_m[:, 3072:4096].rearrange("p (s c) -> p s c", s=8)
        Dx = A_m[:, 4096:4352].rearrange("p (s c) -> p s c", s=2)
        yg = A_m[:, 4352:4352 + 4096].rearrange("p (h t) -> p h t", h=2)
        ysb = Amf[:, 4608:5120]
        yt = Amf[:, 5120:5632]
        ysig = Amf[:, 5632:6144]

        def ssm_tables():
            P.emit("sp", DMA(Btab, btab_scr[:, :, :, :]), reads=["btab_scr"], writes=["Btab"], dma=True, arena=True)
            P.emit("pool", DMA(Cre, cre_x[:, :, :]), writes=["Cre"], dma=True, arena=True)
            P.emit("pool", DMA(Cim, cim_x[:, :, :]), writes=["Cim"], dma=True, arena=True)
            P.emit("pool", DMA(Dx, d_x[:, :, :]), writes=["Dx"], dma=True, arena=True)

        def us_proj(b, key, wv):
            for c in range(2):
                for t in range(NT):
                    b_ = nb()
                    P.emit("pe", MM([(ps[b_][:, :], wv[:, k, c * 128:(c + 1) * 128], uT[:, k, tl(t)], k == 0, k == 7) for k in range(8)]),
                           reads=key + [("uT", k, t) for k in range(8)], writes=[bk(b_)])
                    evac(usT[:, c, tl(t)], ps[b_][:, :], [bk(b_)], [("usT", c, t)])

        def ssm_gen(b, kglu, wglu):
            for h in range(2):
                P.emit("sp", DMA(cosT, trig_scr[:, 0, 4 * h:4 * h + 4, :]), reads=[("trig_scr", 0, sc) for sc in range(8)],
                       writes=[("slot", 3, 0), ("slot", 3, 1)], dma=True, arena=True)
                P.emit("sp", DMA(sinT, trig_scr[:, 1, 4 * h:4 * h + 4, :]), reads=[("trig_scr", 1, sc) for sc in range(8)],
                       writes=["sinT"], dma=True, arena=True)
                ck = [("slot", 3, 0), ("slot", 3, 1)]
                for tc in range(NT):
                    xbufs = []
                    for s4 in range(4):
                        sc = h * 4 + s4
                        br_, bi_ = nb(), nb()
                        P.emit("pe", MM([(ps[br_][:, :], Btab[:, sc, 0, :], usT[:, h, tl(tc)], True, True)]),
                               reads=["Btab", ("usT", h, tc)], writes=[bk(br_)])
                        P.emit("pe", MM([(ps[bi_][:, :], Btab[:, sc, 1, :], usT[:, h, tl(tc)], True, True)]),
                               reads=["Btab", ("usT", h, tc)], writes=[bk(bi_)])
                        c_, s_ = cosT[:, s4, :], sinT[:, s4, :]
                        W0, W1, W2, W3, W4, W5 = [wk(i) for i in range(6)]
                        K = lambda n: ("wk", n)
                        PR, PI = ps[br_][:, :], ps[bi_][:, :]
                        d = lambda fn, r, w: P.emit("dve", fn, reads=r, writes=w)
                        d(TT(W0, PR, c_, ALU.mult), [bk(br_)] + ck, [K(0)])
                        d(TT(W1, PI, s_, ALU.mult), [bk(bi_), "sinT"], [K(1)])
                        d(TT(W0, W0, W1, ALU.add), [K(0), K(1)], [K(0)])
                        d(TT(W1, PI, c_, ALU.mult), [bk(bi_)] + ck, [K(1)])
                        d(TT(W2, PR, s_, ALU.mult), [bk(br_), "sinT"], [K(2)])
                        d(TT(W1, W1, W2, ALU.subtract), [K(1), K(2)], [K(1)])
                        yield 1
                        magb = xap(mag_col(sc), 0, [[0, 512]])
                        ir = 0.0 if tc == 0 else carr_col(sc)
                        ii = 0.0 if tc == 0 else cari_col(sc)
                        d(SCAN(W3, magb, W0, ir), [K(0), "mag", ("car", sc)], [K(3)])
                        d(SCAN(W4, magb, W1, ii), [K(1), "mag", ("cari", sc)], [K(4)])
                        yield 1
                        xr_b, xi_b = xb(2 * s4), xb(2 * s4 + 1)
                        kxr, kxi = ("xb", 2 * s4), ("xb", 2 * s4 + 1)
                        d(TT(W0, c_, W3, ALU.mult), [K(3)] + ck, [K(0)])
                        d(TT(W2, s_, W4, ALU.mult), [K(4), "sinT"], [K(2)])
                        d(TT(xr_b, W0, W2, ALU.subtract), [K(0), K(2)], [kxr])
                        d(TT(carr_col(sc), W0[:, 511:512], W2[:, 511:512], ALU.subtract), [K(0), K(2)], [("car", sc)])
                        yield 1
                        g = lambda fn, r, w: P.emit("pool", fn, reads=r, writes=w)
                        WP = A_h[:, 15872:16384]
                        g(TT(W5, s_, W3, ALU.mult), [K(3), "sinT"], [K(5)])
                        g(TT(WP, c_, W4, ALU.mult), [K(4)] + ck, [K("p")])
                        g(TT(xi_b, W5, WP, ALU.add), [K(5), K("p")], [kxi])
                        g(TT(cari_col(sc), W5[:, 511:512], WP[:, 511:512], ALU.add), [K(5), K("p")], [("cari", sc)])
                        xbufs.append((sc, xr_b, xi_b, kxr, kxi))
                        yield 1
                    by = nb()
                    lst = []
                    reads = ["Cre", "Cim", "Dx", ("usT", h, tc)]
                    for i, (sc, xr_b, xi_b, kxr, kxi) in enumerate(xbufs):
                        lst.append((ps[by][:, :], Cre[:, sc, :], xr_b, i == 0, False))
                        lst.append((ps[by][:, :], Cim[:, sc, :], xi_b, False, False))
                        reads += [kxr, kxi]
                    lst.append((ps[by][:, :], Dx[:, h, :], usT[:, h, tl(tc)], False, True))
                    P.emit("pe", MM(lst), reads=reads, writes=[bk(by)])
                    P.emit("act", ACP(ysb, ps[by][:, :]), reads=[bk(by)], writes=["ysb"])
                    P.emit("dve", TT(yt, ysb, ysb, ALU.mult), reads=["ysb"], writes=["yt"])
                    P.emit("dve", TS(yt, yt, 0.044715, 1.0, ALU.mult, ALU.add), reads=["yt"], writes=["yt"])
                    P.emit("dve", TT(yt, yt, ysb, ALU.mult), reads=["yt", "ysb"], writes=["yt"])
                    P.emit("act", ACTF(ysig, yt, AF.Sigmoid, scale=1.5957691216057308), reads=["yt"], writes=["ysig"])
                    P.emit("dve", TT(yg[:, h, tl(tc)], ysb, ysig, ALU.mult), reads=["ysb", "ysig"], writes=[("yg", h, tc)])
            for c in range(2):
                for t in range(NT):
                    b_ = nb()
                    P.emit("pe", MM([(ps[b_][:, :], wglu[:, hh, c * 128:(c + 1) * 128], yg[:, hh, tl(t)], hh == 0, hh == 1) for hh in range(2)]),
                           reads=kglu + [("yg", 0, t), ("yg", 1, t)], writes=[bk(b_)])
                    P.emit("act", ACTF(ysig, ps[b_][:, :], AF.Sigmoid, bias=smv(V_BGLU + c)), reads=[bk(b_), "vec"], writes=["ysig"])
                    P.emit("dve", TT(ssmo[:, c, tl(t)], yg[:, c, tl(t)], ysig, ALU.mult), reads=["ysig", ("yg", c, t)],
                           writes=[("ssmo", c, t)])

        def load_x(b):
            for t in range(NT):
                for k in range(8):
                    P.emit("sp", DMA(hT[:, k, tl(t)], xT[b, k * 128:(k + 1) * 128, tl(t)]), writes=[("hT", k, t)], dma=True, arena=True)

        def mixer_q(b, q, kpa, wpa, kps, wps, kg, wga, wgs):
            for jj in range(2):
                j = q * 2 + jj
                jc = slice(j * 128, (j + 1) * 128)
                jjc = slice(jj * 128, (jj + 1) * 128)
                for t in range(NT):
                    b1, b2, b3, b4 = nb(), nb(), nb(), nb()
                    P.emit("pe", MM([(ps[b3][:, :], wga[:, k, jjc], uT[:, k, tl(t)], k == 0, k == 7) for k in range(8)]),
                           reads=kg + [("uT", k, t) for k in range(8)], writes=[bk(b3)])
                    P.emit("pe", MM([(ps[b4][:, :], wgs[:, k, jjc], uT[:, k, tl(t)], k == 0, k == 7) for k in range(8)]),
                           reads=kg + [("uT", k, t) for k in range(8)], writes=[bk(b4)])
                    P.emit("pe", MM([(ps[b1][:, :], wpa[:, k, jc], oatt[:, k, tl(t)], k == 0, k == 3) for k in range(4)]),
                           reads=kpa + [("oatt", k) for k in range(4)], writes=[bk(b1)])
                    P.emit("pe", MM([(ps[b2][:, :], wps[:, k, jc], ssmo[:, k, tl(t)], k == 0, k == 1) for k in range(2)]),
                           reads=kps + [("ssmo", 0, t), ("ssmo", 1, t)], writes=[bk(b2)])
                    sa, ss_, m1, m2 = (p3buf[:, i, 0:512] for i in range(4))
                    P.emit("act", ACTF(sa, ps[b3][:, :], AF.Sigmoid, bias=smv(V_BGATE + j)), reads=[bk(b3), "vec"], writes=["sa"])
                    P.emit("act", ACTF(ss_, ps[b4][:, :], AF.Sigmoid, bias=smv(V_BGATE + 8 + j)), reads=[bk(b4), "vec"], writes=["ss"])
                    P.emit("dve", TT(m1, sa, ps[b1][:, :], ALU.mult), reads=["sa", bk(b1)], writes=["m1"])
                    P.emit("dve", TT(m2, ss_, ps[b2][:, :], ALU.mult), reads=["ss", bk(b2)], writes=["m2"])
                    P.emit("dve", TT(mT[:, j, tl(t)], m1, m2, ALU.add), reads=["m1", "m2"], writes=[("mT", j, t)])

        def wout_half(b, half, kwo, wo):
            for jj in range(4):
                j = half * 4 + jj
                for t in range(NT):
                    b_ = nb()
                    P.emit("pe", MM([(ps[b_][:, :], wo[:, k, jj * 128:(jj + 1) * 128], mT[:, k, tl(t)], k == 0, k == 7) for k in range(8)]),
                           reads=kwo + [("mT", k, t) for k in range(8)], writes=[bk(b_)])
                    P.emit("dve", STT(hT[:, j, tl(t)], ps[b_][:, :], mod_col(16 + j, b), hT[:, j, tl(t)], ALU.mult, ALU.add),
                           reads=[bk(b_), "mod", ("hT", j, t)], writes=[("hT", j, t)])

        gT = mT

        def ffn_up(b, hf, fb, kwa, wa, kwv, wvv):
            for ff in range(4):
                f = hf * 8 + fb * 4 + ff
                fl = fb * 4 + ff
                fc = slice(ff * 128, (ff + 1) * 128)
                for t in range(NT):
                    ba, bv = nb(), nb()
                    P.emit("pe", MM([(ps[ba][:, :], wa[:, k, fc], uT[:, k, tl(t)], k == 0, k == 7) for k in range(8)]),
                           reads=kwa + [("uT", k, t) for k in range(8)], writes=[bk(ba)])
                    P.emit("pe", MM([(ps[bv][:, :], wvv[:, k, fc], uT[:, k, tl(t)], k == 0, k == 7) for k in range(8)]),
                           reads=kwv + [("uT", k, t) for k in range(8)], writes=[bk(bv)])
                    ab_ = p3buf[:, t % 2, :]
                    pvb = p3buf[:, (t + 1) % 2, :]
                    if t == 0:
                        P.emit("dve", MEMSET(ab_[:, 0:2], 0.0), writes=[("asbh", t % 2)])
                    else:
                        P.emit("act", ACP(ab_[:, 0:2], pvb[:, 512:514]), reads=[("asb", (t + 1) % 2)], writes=[("asbh", t % 2)])
                    P.emit("act", ACP(ab_[:, 2:514], ps[ba][:, :]), reads=[bk(ba)], writes=[("asb", t % 2)])
                    cv = p3buf[:, 2, 0:512]
                    sl = p3buf[:, 3, 0:512]
                    wc = lambda jtap, f=f: smv(V_WCONV + f * 3 + jtap)
                    rk = [("asb", t % 2), ("asbh", t % 2), "vec"]
                    P.emit("dve", TS(cv, ab_[:, 2:514], wc(0), smv(V_BCONV + f), ALU.mult, ALU.add), reads=rk, writes=["cv"])
                    P.emit("dve", STT(cv, ab_[:, 1:513], wc(1), cv, ALU.mult, ALU.add), reads=rk + ["cv"], writes=["cv"])
                    P.emit("dve", STT(cv, ab_[:, 0:512], wc(2), cv, ALU.mult, ALU.add), reads=rk + ["cv"], writes=["cv"])
                    P.emit("act", ACTF(sl, cv, AF.Silu), reads=["cv"], writes=["sl"])
                    P.emit("dve", TT(gT[:, fl, tl(t)], sl, ps[bv][:, :], ALU.mult), reads=["sl", bk(bv)], writes=[("mT", fl, t)])

        def ffn_down(b, hf, cb, kwd, wd):
            for jj in range(4):
                j = cb * 4 + jj
                for t in range(NT):
                    b_ = nb()
                    P.emit("pe", MM([(ps[b_][:, :], wd[:, k, jj * 128:(jj + 1) * 128], gT[:, k, tl(t)], k == 0, k == 7) for k in range(8)]),
                           reads=kwd + [("mT", k, t) for k in range(8)], writes=[bk(b_)])
                    P.emit("dve", STT(hT[:, j, tl(t)], ps[b_][:, :], mod_col(40 + j, b), hT[:, j, tl(t)], ALU.mult, ALU.add),
                           reads=[bk(b_), "mod", ("hT", j, t)], writes=[("hT", j, t)])

        def final_norm(b):
            for t in range(NT):
                sq = A_m[:, (t % 2) * 4096:(t % 2) * 4096 + 4096].rearrange("p (k c) -> p k c", k=8)
                rstd = Amf[:, 4096 + (t % 2) * 512: 4096 + (t % 2) * 512 + 512]
                P.emit("act", ACTF(sq, hT[:, :, tl(t)], AF.Square), reads=[("hT", k, t) for k in range(8)], writes=[("nsq", t % 2)])
                b_ = nb()
                P.emit("pe", MM([(ps[b_][:, :], onesb[:, :], sq[:, k, :], k == 0, k == 7) for k in range(8)]),
                       reads=[("nsq", t % 2), "ones"], writes=[bk(b_)])
                P.emit("act", ACTF(rstd, ps[b_][:, :], AF.Ln, bias=eps_col, scale=1.0 / D), reads=[bk(b_), "consts"], writes=[("nrstd", t % 2)])
                P.emit("act", ACTF(rstd, rstd, AF.Exp, scale=-0.5), reads=[("nrstd", t % 2)], writes=[("nrstd", t % 2)])
                for k in range(8):
                    P.emit("dve", STT(hT[:, k, tl(t)], hT[:, k, tl(t)], smv(V_GFIN + k), rstd, ALU.mult, ALU.mult),
                           reads=[("hT", k, t), ("nrstd", t % 2), "vec"], writes=[("hT", k, t)])
                for k in range(8):
                    P.emit("sp", DMA(outT[b, k * 128:(k + 1) * 128, tl(t)], hT[:, k, tl(t)]),
                           reads=[("hT", k, t)], writes=[("outT", b, k, t)], dma=True, arena=True)

        stages = []

        def gq_src(q):
            return [("ga", w_in[:, 1792 + q * 256: 1792 + (q + 1) * 256], 8, 256, 2 if q % 2 == 0 else 3, 0),
                    ("gs", w_in[:, 2816 + q * 256: 2816 + (q + 1) * 256], 8, 256, 2 if q % 2 == 0 else 3, 2048)]

        gen_state = {}

        for b in range(NB):
            def c_us(W, b=b):
                load_x(b)
                norm_mod(b, gs1_col, 0, uT)
                P.barrier()
                ssm_tables()
                P.emit("act", lambda e: e.activation(out=Cim, in_=Cim, func=AF.Copy, scale=-1.0), reads=["Cim"], writes=["Cim"])
                load_E(0)
                us_proj(b, *W["us"])
                gen_state["gen"] = ssm_gen(b, *W["glu"])
            stages.append(([("us", w_in[:, 1536:1792], 8, 256, 2, 0), ("glu", w_glu[:, :], 2, 256, 1, 3072)], c_us))

            def tick():
                g = gen_state.get("gen")
                if g is not None:
                    try:
                        next(g)
                    except StopIteration:
                        gen_state["gen"] = None

            for hp in range(4):
                def c_att(W, b=b, hp=hp, tick=tick):
                    attention_hp(b, hp, *W["w"], tick)
                stages.append(([("w", w_in[:, hp * 384:(hp + 1) * 384], 8, 384, hp % 2, 0)], c_att))

            def c_tail(W, tick=tick):
                while gen_state.get("gen") is not None:
                    tick()
            stages.append(([], c_tail))
            for q in range(4):
                def c_mq(W, b=b, q=q, st_=state):
                    if q == 0:
                        st_["pa"] = W["pa"]
                        st_["ps"] = W["ps"]
                        P.barrier()
                        load_x(b)
                    kg = W["ga"][0] + W["gs"][0]
                    mixer_q(b, q, st_["pa"][0], st_["pa"][1], st_["ps"][0], st_["ps"][1], kg, W["ga"][1], W["gs"][1])
                lds = gq_src(q)
                if q == 0:
                    lds = [("pa", w_pa[:, :], 4, 1024, 0, 0), ("ps", w_ps[:, :], 2, 1024, 1, 0)] + lds
                stages.append((lds, c_mq))
            for half in range(2):
                def c_wo(W, b=b, half=half):
                    wout_half(b, half, *W["wo"])
                stages.append(([("wo", w_out[:, half * 512:(half + 1) * 512], 8, 512, 2 if half == 0 else 3, 0)], c_wo))
            fslots = [(0, 1), (2, 3), (0,), (1,), (2, 3), (0, 1), (2,), (3,)]
            fi = 0
            for hf in range(2):
                for fb in range(2):
                    c0 = hf * 1024 + fb * 512

                    def c_up(W, b=b, hf=hf, fb=fb):
                        if hf == 0 and fb == 0:
                            P.barrier()
                            norm_mod(b, gs2_col, 24, uT)
                        ffn_up(b, hf, fb, W["wa"][0], W["wa"][1], W["wv"][0], W["wv"][1])
                    stages.append(([("wa", w_up[:, c0:c0 + 512], 8, 512, fslots[fi][0], 0),
                                    ("wv", w_up[:, 2048 + c0:2048 + c0 + 512], 8, 512, fslots[fi][1], 0)], c_up))
                    fi += 1
                for cb in range(2):
                    def c_dn(W, b=b, hf=hf, cb=cb):
                        ffn_down(b, hf, cb, *W["wd"])
                        if hf == 1 and cb == 1:
                            P.barrier()
                            final_norm(b)
                    stages.append(([("wd", w_down[hf * 1024:(hf + 1) * 1024, cb * 512:(cb + 1) * 512], 8, 512, fslots[fi][0], 0)], c_dn))
                    fi += 1

        def do_loads(lds):
            return {nm: load_w(src, nk, ncols, slot=sl_, off=off) for (nm, src, nk, ncols, sl_, off) in lds}

        Wn = do_loads(stages[0][0])
        P.barrier()
        for i, (lds, comp) in enumerate(stages):
            Wc = Wn
            if i + 1 < len(stages):
                Wn = do_loads(stages[i + 1][0])
            comp(Wc)
        P.final_wait("sp")
        P.build()
    return nc


def _host_layouts(inp, NB):
    f32 = np.float32
    x = np.asarray(inp["x"], f32)
    c = np.asarray(inp["c"], f32)
    col8 = lambda v: np.ascontiguousarray(np.asarray(v, f32).reshape(-1, 128).T)
    w_in = np.asarray(inp["w_in"][0], f32)
    perm = []
    for hp in range(4):
        for sec in range(3):
            perm += list(range(sec * 512 + hp * 128, sec * 512 + (hp + 1) * 128))
    perm += list(range(1536, 3840))
    w_in_p = np.ascontiguousarray(w_in[:, perm])
    a_re, a_im, ldt = inp["a_re"][0], inp["a_im"][0], inp["log_dt"][0]
    st_major = lambda a: np.ascontiguousarray(np.asarray(a, f32).reshape(8, 128).T)
    ldt_l = np.ascontiguousarray(np.repeat(np.asarray(ldt, f32), 64).reshape(8, 128).T)
    wconv = np.asarray(inp["w_conv"][0], f32)
    wconv_l = np.ascontiguousarray(wconv.reshape(3, 16, 128).transpose(2, 1, 0).reshape(128, 48))
    vec = np.concatenate([
        col8(inp["g_mix"][0]), col8(inp["g_ffn"][0]), col8(inp["g_final"]), col8(inp["b_gate"][0]),
        col8(inp["b_glu"][0]), wconv_l, col8(inp["b_conv"][0]), st_major(a_re), st_major(a_im), ldt_l,
        col8(inp["b_ada"][0])], axis=1).astype(f32)
    assert vec.shape == (128, NV), vec.shape

    def expand_b(bmat):
        out = np.zeros((128, 8, 128), f32)
        for g in range(16):
            sc, gl = g // 2, g % 2
            c0 = 32 * (sc % 4) + 16 * gl
            out[gl * 64:(gl + 1) * 64, sc, c0:c0 + 16] = bmat[g]
        return out

    def expand_c(cmat):
        out = np.zeros((128, 8, 128), f32)
        for g in range(16):
            sc, gl = g // 2, g % 2
            c0 = 32 * (sc % 4) + 16 * gl
            out[gl * 64:(gl + 1) * 64, sc, c0:c0 + 16] = cmat[g].T
        return out

    d = np.asarray(inp["d_skip"][0], f32)
    d_x = np.zeros((128, 2, 128), f32)
    for h in range(2):
        d_x[np.arange(128), h, np.arange(128)] = d[h * 128:(h + 1) * 128]
    ak = np.arange(128, dtype=np.float64)[:, None]
    aq = np.arange(128, dtype=np.float64)[None, :]
    E = np.zeros((128, 48, 128), f32)
    for h in range(8):
        slope = 2.0 ** (-8.0 * (h + 1) / 8)
        for p, dil in enumerate((1, 4, 16)):
            cur = np.where(ak <= aq, np.exp(-slope * dil * (aq - ak) - 0.0), 0.0)
            prv = np.where(ak >= aq, np.exp(-slope * dil * (128 + aq - ak)), 0.0)
            E[:, (h * 3 + p) * 2 + 0, :] = cur
            E[:, (h * 3 + p) * 2 + 1, :] = prv
    common = dict(
        w_ada=np.ascontiguousarray(inp["w_ada"][0], f32), w_in_p=w_in_p, vec=vec,
        bre_x=expand_b(np.asarray(inp["b_re"][0], f32)), bim_x=expand_b(np.asarray(inp["b_im"][0], f32)),
        cre_x=expand_c(np.asarray(inp["c_re"][0], f32)), cim_x=expand_c(np.asarray(inp["c_im"][0], f32)),
        d_x=d_x, ident=np.eye(128, dtype=f32), iota=np.tile(np.arange(1, 513, dtype=f32)[None, :], (128, 1)), E=E,
        w_glu=np.ascontiguousarray(inp["w_glu"][0], f32), w_proj_att=np.ascontiguousarray(inp["w_proj_att"][0], f32),
        w_proj_ssm=np.ascontiguousarray(inp["w_proj_ssm"][0], f32), w_out=np.ascontiguousarray(inp["w_out"][0], f32),
        w_up=np.ascontiguousarray(inp["w_up"][0], f32), w_down=np.ascontiguousarray(inp["w_down"][0], f32))
    maps = []
    for core in range(NCORES):
        bs = slice(core * NB, (core + 1) * NB)
        m = dict(common)
        m["xT"] = np.ascontiguousarray(x[bs].transpose(0, 2, 1))
        m["cT"] = np.ascontiguousarray(c[bs].reshape(NB, 8, 128).transpose(2, 1, 0))
        maps.append(m)
    return maps


def kernel(**inputs):
    B = inputs["x"].shape[0]
    NB = B // NCORES
    maps = _host_layouts(inputs, NB)
    nc = build_nc(NB)
    res = run_bass_kernel_spmd(nc, maps, core_ids=list(range(NCORES)))
    outs = [np.asarray(r["outT"]).transpose(0, 2, 1) for r in res.results]
    return np.ascontiguousarray(np.concatenate(outs, axis=0).astype(np.float32))
```

```python
import math
from contextlib import ExitStack

import numpy as np
import concourse.bass as bass
import concourse.mybir as mybir
from concourse.bass_utils import run_bass_kernel_spmd

F32 = mybir.dt.float32
BF16 = mybir.dt.bfloat16
I32 = mybir.dt.int32
AF = mybir.ActivationFunctionType
ALU = mybir.AluOpType

NCORES = 8
D = 1024
S = 2048
NT = 4
EPS = 1e-6
SHIFT = 8.0
TWO_PI = 2.0 * math.pi
PI_LO = 3.1415925
ENGS = ("pe", "act", "dve", "pool", "sp")

V_GMIX, V_GFFN, V_GFIN, V_BGATE, V_BGLU, V_WCONV, V_BCONV, V_ARE, V_AIM, V_LDT, V_BADA = (
    0, 8, 16, 24, 40, 42, 90, 106, 114, 122, 130)
NV = 178


class Prog:
    def __init__(self, nc, stack, n_dma_sems=8):
        self.nc = nc
        self.ops = {e: [] for e in ENGS}
        self.cnt = {e: 0 for e in ENGS}
        self.semobj = {}
        for e in ENGS:
            self.semobj[("c", e)] = stack.enter_context(nc.semaphore("c_" + e))
        self.dval = {q: [0] * n_dma_sems for q in ("sp", "pool")}
        self.dnext = {q: 0 for q in ("sp", "pool")}
        for q in ("sp", "pool"):
            for i in range(n_dma_sems):
                self.semobj[("d", q, i)] = stack.enter_context(nc.semaphore("d_%s%d" % (q, i)))
        self.waited = {e: {} for e in ENGS}
        self.regions = {}
        self.fence = []
        self.arena_toks = {}

    def _need(self, eng, waits, tok):
        if tok is None:
            return
        sid, val = tok
        if self.waited[eng].get(sid, 0) >= val:
            return
        if waits.get(sid, 0) < val:
            waits[sid] = val

    def emit(self, eng, fn, reads=(), writes=(), dma=False, arena=False):
        waits = {}
        own = ("c", eng)
        if dma and arena:
            for t in self.fence:
                self._need(eng, waits, t)
        for key in reads:
            r = self.regions.get(key)
            if r is not None:
                self._need(eng, waits, r["w"])
        for key in writes:
            r = self.regions.get(key)
            if r is not None:
                w = r["w"]
                if w is not None and not (w[0] == own and not dma):
                    self._need(eng, waits, w)
                for sid, val in r["r"].items():
                    if sid == own and not dma:
                        continue
                    self._need(eng, waits, (sid, val))
        if dma:
            i = self.dnext[eng]
            self.dnext[eng] = (i + 1) % len(self.dval[eng])
            sid = ("d", eng, i)
            if self.dval[eng][i] > 0:
                self._need(eng, waits, (sid, self.dval[eng][i]))
            self.dval[eng][i] += 16
            tok = (sid, self.dval[eng][i])
            amt = 16
            if arena:
                self.arena_toks[sid] = tok[1]
        else:
            self.cnt[eng] += 1
            tok = (own, self.cnt[eng])
            amt = 1
        for sid, val in waits.items():
            self.waited[eng][sid] = val
        for key in reads:
            r = self.regions.setdefault(key, {"w": None, "r": {}})
            if r["r"].get(tok[0], 0) < tok[1]:
                r["r"][tok[0]] = tok[1]
        for key in writes:
            self.regions[key] = {"w": tok, "r": {}}
        self.ops[eng].append((list(waits.items()), fn, tok, amt))
        return tok

    def _all_toks(self):
        toks = []
        for e in ENGS:
            if self.cnt[e] > 0:
                toks.append((("c", e), self.cnt[e]))
        for q in ("sp", "pool"):
            for i, v in enumerate(self.dval[q]):
                if v > 0:
                    toks.append((("d", q, i), v))
        return toks

    def barrier(self):
        toks = self._all_toks()
        for e in ENGS:
            waits = {}
            for t in toks:
                if t[0] == ("c", e):
                    continue
                self._need(e, waits, t)
            for sid, val in waits.items():
                self.waited[e][sid] = val
            if waits:
                self.ops[e].append((list(waits.items()), None, None, 0))

    def final_wait(self, eng="sp"):
        waits = {}
        for t in self._all_toks():
            if t[0] == ("c", eng):
                continue
            self._need(eng, waits, t)
        self.ops[eng].append((list(waits.items()), None, None, 0))

    def build(self):
        nc = self.nc
        with nc.Block() as block:
            def mk(e):
                def body(h):
                    for waits, fn, tok, amt in self.ops[e]:
                        for sid, val in waits:
                            h.wait_ge(self.semobj[sid], val)
                        if fn is None:
                            continue
                        ins = fn(h)
                        ins.then_inc(self.semobj[tok[0]], amt)
                return body
            block.tensor(mk("pe"))
            block.scalar(mk("act"))
            block.vector(mk("dve"))
            block.gpsimd(mk("pool"))
            block.sync(mk("sp"))


def MM(lst):
    def fn(e):
        ins = None
        for (out, lhsT, rhs, st, sp) in lst:
            ins = e.matmul(out, lhsT=lhsT, rhs=rhs, start=st, stop=sp)
        return ins
    return fn


def ACTF(out, in_, func, bias=None, scale=None):
    def fn(e):
        kw = {}
        if bias is not None:
            kw["bias"] = bias
        if scale is not None:
            kw["scale"] = scale
        return e.activation(out=out, in_=in_, func=func, **kw)
    return fn


def TT(out, in0, in1, op):
    return lambda e: e.tensor_tensor(out=out, in0=in0, in1=in1, op=op)


def TS(out, in0, s1, s2, op0, op1=None):
    if op1 is None:
        return lambda e: e.tensor_scalar(out=out, in0=in0, scalar1=s1, scalar2=None, op0=op0)
    return lambda e: e.tensor_scalar(out=out, in0=in0, scalar1=s1, scalar2=s2, op0=op0, op1=op1)


def STT(out, in0, scalar, in1, op0, op1):
    return lambda e: e.scalar_tensor_tensor(out=out, in0=in0, scalar=scalar, in1=in1, op0=op0, op1=op1)


def CP(out, in_):
    return lambda e: e.tensor_copy(out=out, in_=in_)


def ACP(out, in_):
    return lambda e: e.activation(out=out, in_=in_, func=AF.Copy)


def DMA(out, in_):
    return lambda e: e.dma_start(out=out, in_=in_)


def SCAN(out, d0, d1, init):
    return lambda e: e.tensor_tensor_scan(out=out, data0=d0, data1=d1, initial=init, op0=ALU.mult, op1=ALU.add)


def RECIP(out, in_):
    return lambda e: e.reciprocal(out=out, in_=in_)


def MEMSET(ap, v):
    return lambda e: e.memset(ap, v)


def TRANSP(out, in_, ident):
    return lambda e: e.transpose(out=out, in_=in_, identity=ident)


def xap(base, start, dims):
    return bass.AP(base.tensor, base.offset + start, [list(base.ap[0])] + [list(d) for d in dims])


def build_nc(NB):
    nc = bass.Bass("TRN2", target_bir_lowering=False)

    def din(name, shape, dtype=F32):
        return nc.dram_tensor(name, list(shape), dtype, kind="ExternalInput").ap()

    xT = din("xT", [NB, D, S])
    cT = din("cT", [128, 8, NB])
    w_ada = din("w_ada", [D, 6 * D])
    w_in = din("w_in_p", [D, 3840])
    vec = din("vec", [128, NV])
    bre_x = din("bre_x", [128, 8, 128])
    bim_x = din("bim_x", [128, 8, 128])
    cre_x = din("cre_x", [128, 8, 128])
    cim_x = din("cim_x", [128, 8, 128])
    d_x = din("d_x", [128, 2, 128])
    ident_d = din("ident", [128, 128])
    iota_d = din("iota", [128, 512])
    E_d = din("E", [128, 48, 128])
    w_glu = din("w_glu", [256, 256])
    w_pa = din("w_proj_att", [512, D])
    w_ps = din("w_proj_ssm", [256, D])
    w_out = din("w_out", [D, D])
    w_up = din("w_up", [D, 4096])
    w_down = din("w_down", [2048, D])
    outT = nc.dram_tensor("outT", [NB, D, S], F32, kind="ExternalOutput").ap()
    trig_scr = nc.dram_tensor("trig_scr", [128, 2, 8, 512], F32).ap()
    btab_scr = nc.dram_tensor("btab_scr", [128, 8, 2, 128], BF16).ap()

    with ExitStack() as st:
        P = Prog(nc, st)
        sb = lambda name, shape, dt: st.enter_context(nc.sbuf_tensor(name, list(shape), dt))
        A_h = sb("A_h", [128, 16384], F32)
        A_u = sb("A_u", [128, 16384], BF16)
        A_m = sb("A_m", [128, 16384], BF16)
        oatt = sb("oatt", [128, 4, S], BF16)
        ssmo = sb("ssmo", [128, 2, S], BF16)
        usT = sb("usT", [128, 2, S], BF16)
        NSLOT = 4
        slots = [sb("slot%d" % i, [128, 4096], BF16) for i in range(NSLOT)]
        p3buf = sb("p3buf", [128, 4, 514], F32)
        sm = sb("sm", [128, 768], F32)
        csb = sb("csb", [128, 8, NB], BF16)
        onesb = sb("onesb", [128, 128], BF16)
        ps = [st.enter_context(nc.psum_tensor("ps%d" % i, [128, 512], F32)) for i in range(8)]

        hT = A_h[:, :].rearrange("p (k t) -> p k t", k=8)
        uT = A_u[:, :].rearrange("p (k t) -> p k t", k=8)
        mT = A_m[:, :].rearrange("p (k t) -> p k t", k=8)
        Ahb = A_h[:, :].bitcast(BF16)
        Amf = A_m[:, :].bitcast(F32)

        SM_VEC = 0
        SM_MOD = 192
        SM_GS1 = SM_MOD + 48 * NB
        SM_GS2 = SM_GS1 + 8 * NB
        SM_SSM = SM_GS2 + 8 * NB
        assert SM_SSM + 26 <= 768
        eps_col = sm[:, SM_SSM + 24: SM_SSM + 25]
        nshift_col = sm[:, SM_SSM + 25: SM_SSM + 26]
        smv = lambda off, n=1: sm[:, off:off + n]
        mod_col = lambda chunk, b: sm[:, SM_MOD + chunk * NB + b: SM_MOD + chunk * NB + b + 1]
        gs1_col = lambda k, b: sm[:, SM_GS1 + k * NB + b: SM_GS1 + k * NB + b + 1]
        gs2_col = lambda k, b: sm[:, SM_GS2 + k * NB + b: SM_GS2 + k * NB + b + 1]
        mag_col = lambda sc: sm[:, SM_SSM + sc: SM_SSM + sc + 1]
        carr_col = lambda sc: sm[:, SM_SSM + 8 + sc: SM_SSM + 9 + sc]
        cari_col = lambda sc: sm[:, SM_SSM + 16 + sc: SM_SSM + 17 + sc]

        state = {"bank": 0, "slot": 0, "ev": 0}

        def nb():
            i = state["bank"]
            state["bank"] = (i + 1) % 6
            return i

        def bk(i):
            return ("ps", i)

        def ev():
            state["ev"] ^= 1
            return "act" if state["ev"] else "dve"

        def evac(out, in_, reads, writes):
            P.emit("act", ACP(out, in_), reads=reads, writes=writes)

        def load_w(src2d, nk, ncols, slot=None, off=0):
            if slot is None:
                i = state["slot"]
                state["slot"] = (i + 1) % NSLOT
            else:
                i = slot
            n = nk * ncols
            keys = [("slot", i, h) for h in range(2) if off < (h + 1) * 2048 and off + n > h * 2048]
            view = slots[i][:, off:off + n].rearrange("p (k c) -> p k c", k=nk)
            P.emit("pool", DMA(view, src2d.rearrange("(k p) c -> p k c", p=128)), writes=keys, dma=True)
            return keys, view

        tl = lambda t: slice(t * 512, (t + 1) * 512)

        P.emit("sp", DMA(sm[:, 0:NV], vec[:, :]), writes=["vec"], dma=True, arena=True)
        P.emit("dve", MEMSET(onesb[:, :], 1.0), writes=["ones"])
        P.emit("dve", MEMSET(eps_col, EPS), writes=["consts"])
        P.emit("dve", MEMSET(nshift_col, -SHIFT), writes=["consts"])
        ctmp = A_h[:, 0:8 * NB].rearrange("p (k b) -> p k b", k=8)
        P.emit("sp", DMA(ctmp, cT[:, :, :]), writes=["ctmp"], dma=True, arena=True)
        P.emit("act", ACTF(csb[:, :, :], ctmp, AF.Silu), reads=["ctmp"], writes=["csb"])
        for jb in range(12):
            key, wv = load_w(w_ada[:, jb * 512:(jb + 1) * 512], 8, 512)
            for jj in range(4):
                j = jb * 4 + jj
                b_ = nb()
                P.emit("pe", MM([(ps[b_][:, 0:NB], wv[:, k, jj * 128:(jj + 1) * 128], csb[:, k, :], k == 0, k == 7)
                                 for k in range(8)]), reads=key + ["csb"], writes=[bk(b_)])
                P.emit("dve", TS(sm[:, SM_MOD + j * NB: SM_MOD + (j + 1) * NB], ps[b_][:, 0:NB],
                                 smv(V_BADA + j), None, ALU.add), reads=[bk(b_), "vec"], writes=["mod"])
        for k in range(8):
            P.emit("dve", TS(sm[:, SM_GS1 + k * NB: SM_GS1 + (k + 1) * NB],
                             sm[:, SM_MOD + (8 + k) * NB: SM_MOD + (9 + k) * NB], 1.0, smv(V_GMIX + k), ALU.add, ALU.mult),
                   reads=["mod", "vec"], writes=["gs"])
            P.emit("dve", TS(sm[:, SM_GS2 + k * NB: SM_GS2 + (k + 1) * NB],
                             sm[:, SM_MOD + (32 + k) * NB: SM_MOD + (33 + k) * NB], 1.0, smv(V_GFFN + k), ALU.add, ALU.mult),
                   reads=["mod", "vec"], writes=["gs"])

        pt = lambda i: A_h[:, 512 + 8 * i: 512 + 8 * (i + 1)]
        Xre = A_h[:, 1024:2048].rearrange("p (s c) -> p s c", s=8)
        Xim = A_h[:, 2048:3072].rearrange("p (s c) -> p s c", s=8)
        bre = A_h[:, 3072:4096].rearrange("p (s c) -> p s c", s=8)
        bim = A_h[:, 4096:5120].rearrange("p (s c) -> p s c", s=8)
        iota = A_h[:, 5120:5632]
        ident = A_h[:, 5632:5760]
        P.emit("sp", DMA(bre, bre_x[:, :, :]), writes=["bre"], dma=True, arena=True)
        P.emit("sp", DMA(bim, bim_x[:, :, :]), writes=["bim"], dma=True, arena=True)
        P.emit("sp", DMA(iota, iota_d[:, :]), writes=["iota"], dma=True, arena=True)
        P.emit("sp", DMA(ident, ident_d[:, :]), writes=["ident"], dma=True, arena=True)
        lr, li, ldt = smv(V_ARE, 8), smv(V_AIM, 8), smv(V_LDT, 8)
        T = {}

        def pe_(name, eng, fn, reads):
            P.emit(eng, fn, reads=reads + ["vec"], writes=[name])

        names = ["dt", "lrdt", "th", "ki", "kf", "thr", "sin", "ab", "cos", "abr", "abi", "nr", "t1", "t2", "den",
                 "rden", "u1", "u2", "fre", "fim", "nfim", "u3", "u4"]
        for i, n in enumerate(names):
            T[n] = pt(i)
        T["ki"] = pt(names.index("ki")).bitcast(I32)
        pe_("dt", "act", ACTF(T["dt"], ldt, AF.Exp), [])
        pe_("lrdt", "dve", TT(T["lrdt"], lr, T["dt"], ALU.mult), ["dt"])
        pe_("mag", "act", ACTF(smv(SM_SSM, 8), T["lrdt"], AF.Exp), ["lrdt"])
        pe_("th", "dve", TT(T["th"], li, T["dt"], ALU.mult), ["dt"])
        pe_("ki", "dve", TS(T["ki"], T["th"], 1.0 / TWO_PI, None, ALU.mult), ["th"])
        pe_("kf", "dve", CP(T["kf"], T["ki"]), ["ki"])
        pe_("thr", "dve", STT(T["thr"], T["kf"], -TWO_PI, T["th"], ALU.mult, ALU.add), ["kf", "th"])
        pe_("thr", "dve", TS(T["thr"], T["thr"], PI_LO, -PI_LO, ALU.min, ALU.max), ["thr"])
        pe_("sin", "act", ACTF(T["sin"], T["thr"], AF.Sin), ["thr"])
        pe_("ab", "act", ACTF(T["ab"], T["thr"], AF.Abs), ["thr"])
        pe_("ab2", "dve", TS(T["u4"], T["ab"], -1.0, math.pi / 2, ALU.mult, ALU.add), ["ab"])
        pe_("cos", "act", ACTF(T["cos"], T["u4"], AF.Sin), ["ab2"])
        pe_("abr", "dve", TT(T["abr"], smv(SM_SSM, 8), T["cos"], ALU.mult), ["mag", "cos"])
        pe_("abi", "dve", TT(T["abi"], smv(SM_SSM, 8), T["sin"], ALU.mult), ["mag", "sin"])
        pe_("nr", "dve", TS(T["nr"], T["abr"], -1.0, None, ALU.add), ["abr"])
        pe_("t1", "dve", TT(T["t1"], lr, lr, ALU.mult), [])
        pe_("t2", "dve", TT(T["t2"], li, li, ALU.mult), [])
        pe_("den", "dve", TT(T["den"], T["t1"], T["t2"], ALU.add), ["t1", "t2"])
        pe_("rden", "dve", RECIP(T["rden"], T["den"]), ["den"])
        pe_("u1", "dve", TT(T["u1"], T["nr"], lr, ALU.mult), ["nr"])
        pe_("u2", "dve", TT(T["u2"], T["abi"], li, ALU.mult), ["abi"])
        pe_("u3", "dve", TT(T["u3"], T["u1"], T["u2"], ALU.add), ["u1", "u2"])
        pe_("fre", "dve", TT(T["fre"], T["u3"], T["rden"], ALU.mult), ["u3", "rden"])
        pe_("u1b", "dve", TT(T["u1"], T["abi"], lr, ALU.mult), ["abi", "u3"])
        pe_("u2b", "dve", TT(T["u2"], T["nr"], li, ALU.mult), ["nr", "u3"])
        pe_("u3b", "dve", TT(T["u3"], T["u1"], T["u2"], ALU.subtract), ["u1b", "u2b", "fre"])
        pe_("fim", "dve", TT(T["fim"], T["u3"], T["rden"], ALU.mult), ["u3b", "rden"])
        pe_("nfim", "dve", TS(T["nfim"], T["fim"], -1.0, None, ALU.mult), ["fim"])
        btab = Ahb[:, 2 * 13312: 2 * 13312 + 2048].rearrange("p (s r c) -> p s r c", s=8, r=2)
        for sc in range(8):
            c1 = lambda t, sc=sc: t[:, sc:sc + 1]
            P.emit("dve", TS(Xre[:, sc, :], bre[:, sc, :], c1(T["fre"]), None, ALU.mult), reads=["bre", "fre"], writes=[("Xre", sc)])
            P.emit("dve", STT(Xre[:, sc, :], bim[:, sc, :], c1(T["nfim"]), Xre[:, sc, :], ALU.mult, ALU.add),
                   reads=["bim", "nfim", ("Xre", sc)], writes=[("Xre", sc)])
            P.emit("dve", TS(Xim[:, sc, :], bre[:, sc, :], c1(T["fim"]), None, ALU.mult), reads=["bre", "fim"], writes=[("Xim", sc)])
            P.emit("dve", STT(Xim[:, sc, :], bim[:, sc, :], c1(T["fre"]), Xim[:, sc, :], ALU.mult, ALU.add),
                   reads=["bim", "fre", ("Xim", sc)], writes=[("Xim", sc)])
            for ri, X in enumerate((Xre, Xim)):
                b_ = nb()
                P.emit("pe", TRANSP(ps[b_][:, 0:128], X[:, sc, :], ident), reads=[("Xre" if ri == 0 else "Xim", sc), "ident"],
                       writes=[bk(b_)])
                evac(btab[:, sc, ri, :], ps[b_][:, 0:128], [bk(b_)], [("btab", sc, ri)])
        P.emit("sp", DMA(btab_scr[:, :, :, :], btab), reads=[("btab", sc, ri) for sc in range(8) for ri in range(2)],
               writes=["btab_scr"], dma=True, arena=True)
        for sc in range(8):
            base = 6144 + (sc % 2) * 3072
            arg = A_h[:, base:base + 512]
            ki = A_h[:, base + 512:base + 1024].bitcast(I32)
            kf = A_h[:, base + 1024:base + 1536]
            so = A_h[:, base + 1536:base + 2048]
            ab = A_h[:, base + 2048:base + 2560]
            co = A_h[:, base + 2560:base + 3072]
            kk = lambda n, sc=sc: ("tg", n, sc % 2)
            P.emit("dve", TS(arg, iota, T["thr"][:, sc:sc + 1], None, ALU.mult), reads=["iota", "thr"], writes=[kk("arg")])
            P.emit("dve", TS(ki, arg, 1.0 / TWO_PI, None, ALU.mult), reads=[kk("arg")], writes=[kk("ki")])
            P.emit("dve", CP(kf, ki), reads=[kk("ki")], writes=[kk("kf")])
            P.emit("dve", STT(arg, kf, -TWO_PI, arg, ALU.mult, ALU.add), reads=[kk("kf"), kk("arg")], writes=[kk("arg")])
            P.emit("dve", TS(arg, arg, PI_LO, -PI_LO, ALU.min, ALU.max), reads=[kk("arg")], writes=[kk("arg")])
            P.emit("act", ACTF(so, arg, AF.Sin), reads=[kk("arg")], writes=[kk("so")])
            P.emit("act", ACTF(ab, arg, AF.Abs), reads=[kk("arg")], writes=[kk("ab")])
            P.emit("dve", TS(ab, ab, -1.0, math.pi / 2, ALU.mult, ALU.add), reads=[kk("ab")], writes=[kk("ab")])
            P.emit("act", ACTF(co, ab, AF.Sin), reads=[kk("ab")], writes=[kk("co")])
            P.emit("sp", DMA(trig_scr[:, 0, sc, :], co), reads=[kk("co")], writes=[("trig_scr", 0, sc)], dma=True, arena=True)
            P.emit("sp", DMA(trig_scr[:, 1, sc, :], so), reads=[kk("so")], writes=[("trig_scr", 1, sc)], dma=True, arena=True)

        def mtk(chunks):
            return [("mT", c, tt) for c in chunks for tt in range(NT)]

        def norm_mod(b, gs_col, sh_chunk0, dst):
            for t in range(NT):
                sq = A_m[:, (t % 2) * 4096:(t % 2) * 4096 + 4096].rearrange("p (k c) -> p k c", k=8)
                rstd = Amf[:, 4096 + (t % 2) * 512: 4096 + (t % 2) * 512 + 512]
                sqk = mtk((2 * (t % 2), 2 * (t % 2) + 1))
                P.emit("act", ACTF(sq, hT[:, :, tl(t)], AF.Square), reads=[("hT", k, t) for k in range(8)],
                       writes=[("nsq", t % 2)] + sqk)
                b_ = nb()
                P.emit("pe", MM([(ps[b_][:, :], onesb[:, :], sq[:, k, :], k == 0, k == 7) for k in range(8)]),
                       reads=[("nsq", t % 2), "ones"] + sqk, writes=[bk(b_)])
                P.emit("act", ACTF(rstd, ps[b_][:, :], AF.Ln, bias=eps_col, scale=1.0 / D),
                       reads=[bk(b_), "consts"], writes=[("nrstd", t % 2)] + mtk((4,)))
                P.emit("act", ACTF(rstd, rstd, AF.Exp, scale=-0.5), reads=[("nrstd", t % 2)], writes=[("nrstd", t % 2)] + mtk((4,)))
                for k in range(8):
                    tmp = Amf[:, 5120 + (k % 2) * 512: 5120 + (k % 2) * 512 + 512]
                    P.emit("dve", TT(tmp, hT[:, k, tl(t)], rstd, ALU.mult), reads=[("hT", k, t), ("nrstd", t % 2)] + mtk((4,)),
                           writes=[("ntmp", k % 2)] + mtk((5,)))
                    P.emit("act", ACTF(dst[:, k, tl(t)], tmp, AF.Identity, bias=mod_col(sh_chunk0 + k, b), scale=gs_col(k, b)),
                           reads=[("ntmp", k % 2), "mod", "gs"] + mtk((5,)), writes=[("uT", k, t)])

        Etabs = [Ahb[:, i * 1536:(i + 1) * 1536].rearrange("p (i q) -> p i q", i=12) for i in range(2)]
        qT = Ahb[:, 3072:5120]
        kT = Ahb[:, 5120:7168]
        Vt = [Ahb[:, 7168 + i * 2048: 7168 + (i + 1) * 2048].rearrange("p (b c) -> p b c", b=16) for i in range(3)]
        Pb = [Ahb[:, 13312 + i * 512: 13312 + (i + 1) * 512] for i in range(8)]
        acc_o = A_h[:, 8704:10752]
        acc_d = A_h[:, 10752:12800]

        def load_E(hp):
            P.emit("pool", DMA(Etabs[hp % 2], E_d[:, hp * 12:(hp + 1) * 12, :]), writes=[("Etab", hp % 2)], dma=True, arena=True)

        def tokset(pat, blk):
            if pat == 0:
                return 128 * blk, 1
            if pat == 1:
                return 512 * (blk % 4) + blk // 4, 4
            return blk, 16

        def cs_(ap2d, pat, blk):
            base, stp = tokset(pat, blk)
            return ap2d[:, base: base + 127 * stp + 1: stp]

        def acc_view(acc, pat, grp):
            if pat == 0:
                return acc[:, 512 * grp: 512 * grp + 512].rearrange("p (j a) -> p j a", j=4)
            if pat == 1:
                return xap(acc[:, 0:1], 512 * grp, [[1, 4], [4, 128]])
            return xap(acc[:, 0:1], 4 * grp, [[1, 4], [16, 128]])

        pbi = [0]

        def attention_hp(b, hp, key, wv, tick):
            if hp < 3:
                load_E(hp + 1)
            for which, dst, nm in ((0, qT, "qT"), (1, kT, "kT")):
                for t in range(NT):
                    b_ = nb()
                    P.emit("pe", MM([(ps[b_][:, :], wv[:, k, which * 128:(which + 1) * 128], uT[:, k, tl(t)], k == 0, k == 7)
                                     for k in range(8)]), reads=key + [("uT", k, t) for k in range(8)], writes=[bk(b_)])
                    evac(dst[:, tl(t)], ps[b_][:, :], [bk(b_)], [(nm, t)])
            for pat in range(3):
                for b4 in range(4):
                    b_ = nb()
                    lst = []
                    for jj in range(4):
                        blk = b4 * 4 + jj
                        for k in range(8):
                            lst.append((ps[b_][:, jj * 128:(jj + 1) * 128], cs_(uT[:, k, :], pat, blk), wv[:, k, 256:384],
                                        k == 0, k == 7))
                    P.emit("pe", MM(lst), reads=key + [("uT", k, t) for k in range(8) for t in range(NT)], writes=[bk(b_)])
                    evac(Vt[pat][:, b4 * 4:(b4 + 1) * 4, :], ps[b_][:, :].rearrange("p (j c) -> p j c", j=4), [bk(b_)],
                         [("V", pat, b4)])
            units = [(pat, grp, hd) for pat in range(3) for grp in range(4) for hd in range(2)]
            pend = None
            obank = {}

            def emit_qk(u):
                pat, grp, hd = u
                h = hp * 2 + hd
                rows = slice(hd * 64, hd * 64 + 64)
                info = {"cur": None, "prev": None, "mask": []}
                for typ in ("cur", "prev"):
                    lst = []
                    js = []
                    for j in range(4):
                        if pat == 0:
                            qb = 4 * grp + j
                            kb = qb if typ == "cur" else qb - 1
                            ok = kb >= 0
                        elif pat == 1:
                            qb = j * 4 + grp
                            kb = qb if typ == "cur" else qb - 1
                            ok = (typ == "cur") or grp >= 1
                        else:
                            qb = 4 * grp + j
                            kb = qb
                            ok = typ == "cur"
                        if ok:
                            js.append((j, qb, kb))
                    if not js:
                        continue
                    b_ = nb()
                    for (j, qb, kb) in js:
                        lst.append((ps[b_][:, j * 128:(j + 1) * 128], cs_(kT[rows, :], pat, kb), cs_(qT[rows, :], pat, qb), True, True))
                    P.emit("pe", MM(lst), reads=[("qT", t) for t in range(NT)] + [("kT", t) for t in range(NT)], writes=[bk(b_)])
                    j0 = js[0][0]
                    pi = pbi[0]
                    pbi[0] = (pi + 1) % 8
                    pv = Pb[pi][:, j0 * 128:512]
                    P.emit("act", ACTF(pv, ps[b_][:, j0 * 128:512], AF.Exp, bias=nshift_col, scale=0.125),
                           reads=[bk(b_), "consts"], writes=[("Pb", pi)])
                    ei = (hd * 3 + pat) * 2 + (0 if typ == "cur" else 1)
                    nj = 4 - j0
                    ebc = xap(Etabs[hp % 2][:, ei, 0:1], 0, [[0, nj], [1, 128]])
                    pv3 = pv.rearrange("p (j a) -> p j a", j=nj)
                    info["mask"].append((pv3, ebc, pi))
                    info[typ] = (pi, js)
                return info

            def emit_mask(info):
                for (pv3, ebc, pi) in info["mask"]:
                    P.emit("dve", TT(pv3, pv3, ebc, ALU.mult), reads=[("Pb", pi), ("Etab", hp % 2)], writes=[("Pb", pi)])

            def emit_pv(u, info):
                pat, grp, hd = u
                rows = slice(hd * 64, hd * 64 + 64)
                if hd == 0:
                    obank[(pat, grp)] = (6, 7)
                bo, bd = obank[(pat, grp)]
                lst = []
                reads = [("V", pat, b4) for b4 in range(4)] + ["ones"]
                pcur, jcur = info["cur"]
                reads.append(("Pb", pcur))
                prevmap = {}
                if info["prev"] is not None:
                    pprev, jprev = info["prev"]
                    reads.append(("Pb", pprev))
                    prevmap = {j: kb for (j, qb, kb) in jprev}
                for (j, qb, kb) in jcur:
                    oc = slice(j * 128, (j + 1) * 128)
                    hasp = j in prevmap
                    for dst, isden in ((ps[bo], False), (ps[bd], True)):
                        if hasp:
                            lw = onesb[:, 0:64] if isden else Vt[pat][:, prevmap[j], rows]
                            lst.append((dst[rows, oc], lw, Pb[pprev][:, oc], True, False))
                        lw = onesb[:, 0:64] if isden else Vt[pat][:, kb, rows]
                        lst.append((dst[rows, oc], lw, Pb[pcur][:, oc], not hasp, True))
                P.emit("pe", MM(lst), reads=reads, writes=[bk(bo), bk(bd)])
                if hd == 1:
                    gk = [grp] if pat < 2 else [0, 1, 2, 3]
                    for acc, bb, nm in ((acc_o, bo, "acc_o"), (acc_d, bd, "acc_d")):
                        av = acc_view(acc, pat, grp)
                        src = ps[bb][:, :].rearrange("p (j a) -> p j a", j=4)
                        if pat == 0:
                            evac(av, src, [bk(bb)], [(nm, g) for g in gk])
                        else:
                            P.emit("dve", TT(av, av, src, ALU.add), reads=[bk(bb)] + [(nm, g) for g in gk],
                                   writes=[(nm, g) for g in gk])

            infos = []
            for ui, u in enumerate(units):
                infos.append(emit_qk(u))
                tick()
                if ui >= 1:
                    emit_mask(infos[ui - 1])
                if ui >= 2:
                    emit_pv(units[ui - 2], infos[ui - 2])
                tick()
            nU = len(units)
            emit_mask(infos[nU - 1])
            emit_pv(units[nU - 2], infos[nU - 2])
            emit_pv(units[nU - 1], infos[nU - 1])
            P.emit("act", ACTF(acc_d, acc_d, AF.Ln), reads=[("acc_d", g) for g in range(4)], writes=[("acc_d", g) for g in range(4)])
            P.emit("act", ACTF(acc_d, acc_d, AF.Exp, scale=-1.0), reads=[("acc_d", g) for g in range(4)], writes=[("acc_d", g) for g in range(4)])
            P.emit("dve", TT(oatt[:, hp, :], acc_o, acc_d, ALU.mult), reads=[("acc_o", g) for g in range(4)] + [("acc_d", g) for g in range(4)],
                   writes=[("oatt", hp)])

        cosT = slots[3][:, :].bitcast(F32).rearrange("p (s c) -> p s c", s=4)
        sinT = p3buf[:, :, 0:512]
        wk = lambda i: A_h[:, 12800 + i * 512: 12800 + (i + 1) * 512]
        xb = lambda i: A_m[:, 12288 + i * 512: 12288 + (i + 1) * 512]
        Btab = A_m[:, 0:2048].rearrange("p (s r c) -> p s r c", s=8, r=2)
        Cre = A_m[:, 2048:3072].rearrange("p (s c) -> p s c", s=8)
        Cim = A_m[:, 3072:4096].rearrange("p (s c) -> p s c", s=8)
        Dx = A_m[:, 4096:4352].rearrange("p (s c) -> p s c", s=2)
        yg = A_m[:, 4352:4352 + 4096].rearrange("p (h t) -> p h t", h=2)
        ysb = Amf[:, 4608:5120]
        yt = Amf[:, 5120:5632]
        ysig = Amf[:, 5632:6144]

        def ssm_tables():
            P.emit("sp", DMA(Btab, btab_scr[:, :, :, :]), reads=["btab_scr"], writes=["Btab"], dma=True, arena=True)
            P.emit("pool", DMA(Cre, cre_x[:, :, :]), writes=["Cre"], dma=True, arena=True)
            P.emit("pool", DMA(Cim, cim_x[:, :, :]), writes=["Cim"], dma=True, arena=True)
            P.emit("pool", DMA(Dx, d_x[:, :, :]), writes=["Dx"], dma=True, arena=True)

        def us_proj(b, key, wv):
            for c in range(2):
                for t in range(NT):
                    b_ = nb()
                    P.emit("pe", MM([(ps[b_][:, :], wv[:, k, c * 128:(c + 1) * 128], uT[:, k, tl(t)], k == 0, k == 7) for k in range(8)]),
                           reads=key + [("uT", k, t) for k in range(8)], writes=[bk(b_)])
                    evac(usT[:, c, tl(t)], ps[b_][:, :], [bk(b_)], [("usT", c, t)])

        def ssm_gen(b, kglu, wglu):
            for h in range(2):
                P.emit("sp", DMA(cosT, trig_scr[:, 0, 4 * h:4 * h + 4, :]), reads=[("trig_scr", 0, sc) for sc in range(8)],
                       writes=[("slot", 3, 0), ("slot", 3, 1)], dma=True, arena=True)
                P.emit("sp", DMA(sinT, trig_scr[:, 1, 4 * h:4 * h + 4, :]), reads=[("trig_scr", 1, sc) for sc in range(8)],
                       writes=["sinT"], dma=True, arena=True)
                ck = [("slot", 3, 0), ("slot", 3, 1)]
                for tc in range(NT):
                    xbufs = []
                    for s4 in range(4):
                        sc = h * 4 + s4
                        br_, bi_ = nb(), nb()
                        P.emit("pe", MM([(ps[br_][:, :], Btab[:, sc, 0, :], usT[:, h, tl(tc)], True, True)]),
                               reads=["Btab", ("usT", h, tc)], writes=[bk(br_)])
                        P.emit("pe", MM([(ps[bi_][:, :], Btab[:, sc, 1, :], usT[:, h, tl(tc)], True, True)]),
                               reads=["Btab", ("usT", h, tc)], writes=[bk(bi_)])
                        c_, s_ = cosT[:, s4, :], sinT[:, s4, :]
                        W0, W1, W2, W3, W4, W5 = [wk(i) for i in range(6)]
                        K = lambda n: ("wk", n)
                        PR, PI = ps[br_][:, :], ps[bi_][:, :]
                        d = lambda fn, r, w: P.emit("dve", fn, reads=r, writes=w)
                        d(TT(W0, PR, c_, ALU.mult), [bk(br_)] + ck, [K(0)])
                        d(TT(W1, PI, s_, ALU.mult), [bk(bi_), "sinT"], [K(1)])
                        d(TT(W0, W0, W1, ALU.add), [K(0), K(1)], [K(0)])
                        d(TT(W1, PI, c_, ALU.mult), [bk(bi_)] + ck, [K(1)])
                        d(TT(W2, PR, s_, ALU.mult), [bk(br_), "sinT"], [K(2)])
                        d(TT(W1, W1, W2, ALU.subtract), [K(1), K(2)], [K(1)])
                        yield 1
                        magb = xap(mag_col(sc), 0, [[0, 512]])
                        ir = 0.0 if tc == 0 else carr_col(sc)
                        ii = 0.0 if tc == 0 else cari_col(sc)
                        d(SCAN(W3, magb, W0, ir), [K(0), "mag", ("car", sc)], [K(3)])
                        d(SCAN(W4, magb, W1, ii), [K(1), "mag", ("car", sc)], [K(4)])
                        yield 1
                        xr_b, xi_b = xb(2 * s4), xb(2 * s4 + 1)
                        kxr, kxi = ("xb", 2 * s4), ("xb", 2 * s4 + 1)
                        d(TT(W0, c_, W3, ALU.mult), [K(3)] + ck, [K(0)])
                        d(TT(W2, s_, W4, ALU.mult), [K(4), "sinT"], [K(2)])
                        d(TT(xr_b, W0, W2, ALU.subtract), [K(0), K(2)], [kxr])
                        d(TT(carr_col(sc), W0[:, 511:512], W2[:, 511:512], ALU.subtract), [K(0), K(2)], [("car", sc)])
                        yield 1
                        d(TT(W5, s_, W3, ALU.mult), [K(3), "sinT"], [K(5)])
                        d(TT(W2, c_, W4, ALU.mult), [K(4)] + ck, [K(2)])
                        d(STT(xi_b, W5, -1.0, W2, ALU.mult, ALU.subtract), [K(5), K(2)], [kxi])
                        d(TT(cari_col(sc), W5[:, 511:512], W2[:, 511:512], ALU.add), [K(5), K(2)], [("car", sc)])
                        xbufs.append((sc, xr_b, xi_b, kxr, kxi))
                        yield 1
                    by = nb()
                    lst = []
                    reads = ["Cre", "Cim", "Dx", ("usT", h, tc)]
                    for i, (sc, xr_b, xi_b, kxr, kxi) in enumerate(xbufs):
                        lst.append((ps[by][:, :], Cre[:, sc, :], xr_b, i == 0, False))
                        lst.append((ps[by][:, :], Cim[:, sc, :], xi_b, False, False))
                        reads += [kxr, kxi]
                    lst.append((ps[by][:, :], Dx[:, h, :], usT[:, h, tl(tc)], False, True))
                    P.emit("pe", MM(lst), reads=reads, writes=[bk(by)])
                    P.emit("act", ACP(ysb, ps[by][:, :]), reads=[bk(by)], writes=["ysb"])
                    P.emit("dve", TT(yt, ysb, ysb, ALU.mult), reads=["ysb"], writes=["yt"])
                    P.emit("dve", TS(yt, yt, 0.044715, 1.0, ALU.mult, ALU.add), reads=["yt"], writes=["yt"])
                    P.emit("dve", TT(yt, yt, ysb, ALU.mult), reads=["yt", "ysb"], writes=["yt"])
                    P.emit("act", ACTF(ysig, yt, AF.Sigmoid, scale=1.5957691216057308), reads=["yt"], writes=["ysig"])
                    P.emit("dve", TT(yg[:, h, tl(tc)], ysb, ysig, ALU.mult), reads=["ysb", "ysig"], writes=[("yg", h, tc)])
            for c in range(2):
                for t in range(NT):
                    b_ = nb()
                    P.emit("pe", MM([(ps[b_][:, :], wglu[:, hh, c * 128:(c + 1) * 128], yg[:, hh, tl(t)], hh == 0, hh == 1) for hh in range(2)]),
                           reads=kglu + [("yg", 0, t), ("yg", 1, t)], writes=[bk(b_)])
                    P.emit("act", ACTF(ysig, ps[b_][:, :], AF.Sigmoid, bias=smv(V_BGLU + c)), reads=[bk(b_), "vec"], writes=["ysig"])
                    P.emit("dve", TT(ssmo[:, c, tl(t)], yg[:, c, tl(t)], ysig, ALU.mult), reads=["ysig", ("yg", c, t)],
                           writes=[("ssmo", c, t)])

        def load_x(b):
            for t in range(NT):
                for k in range(8):
                    P.emit("sp", DMA(hT[:, k, tl(t)], xT[b, k * 128:(k + 1) * 128, tl(t)]), writes=[("hT", k, t)], dma=True, arena=True)

        def mixer_q(b, q, kpa, wpa, kps, wps, kg, wga, wgs):
            for jj in range(2):
                j = q * 2 + jj
                jc = slice(j * 128, (j + 1) * 128)
                jjc = slice(jj * 128, (jj + 1) * 128)
                for t in range(NT):
                    b1, b2, b3, b4 = nb(), nb(), nb(), nb()
                    P.emit("pe", MM([(ps[b3][:, :], wga[:, k, jjc], uT[:, k, tl(t)], k == 0, k == 7) for k in range(8)]),
                           reads=kg + [("uT", k, t) for k in range(8)], writes=[bk(b3)])
                    P.emit("pe", MM([(ps[b4][:, :], wgs[:, k, jjc], uT[:, k, tl(t)], k == 0, k == 7) for k in range(8)]),
                           reads=kg + [("uT", k, t) for k in range(8)], writes=[bk(b4)])
                    P.emit("pe", MM([(ps[b1][:, :], wpa[:, k, jc], oatt[:, k, tl(t)], k == 0, k == 3) for k in range(4)]),
                           reads=kpa + [("oatt", k) for k in range(4)], writes=[bk(b1)])
                    P.emit("pe", MM([(ps[b2][:, :], wps[:, k, jc], ssmo[:, k, tl(t)], k == 0, k == 1) for k in range(2)]),
                           reads=kps + [("ssmo", 0, t), ("ssmo", 1, t)], writes=[bk(b2)])
                    sa, ss_, m1, m2 = (p3buf[:, i, 0:512] for i in range(4))
                    P.emit("act", ACTF(sa, ps[b3][:, :], AF.Sigmoid, bias=smv(V_BGATE + j)), reads=[bk(b3), "vec"], writes=["sa"])
                    P.emit("act", ACTF(ss_, ps[b4][:, :], AF.Sigmoid, bias=smv(V_BGATE + 8 + j)), reads=[bk(b4), "vec"], writes=["ss"])
                    P.emit("dve", TT(m1, sa, ps[b1][:, :], ALU.mult), reads=["sa", bk(b1)], writes=["m1"])
                    P.emit("dve", TT(m2, ss_, ps[b2][:, :], ALU.mult), reads=["ss", bk(b2)], writes=["m2"])
                    P.emit("dve", TT(mT[:, j, tl(t)], m1, m2, ALU.add), reads=["m1", "m2"], writes=[("mT", j, t)])

        def wout_half(b, half, kwo, wo):
            for jj in range(4):
                j = half * 4 + jj
                for t in range(NT):
                    b_ = nb()
                    P.emit("pe", MM([(ps[b_][:, :], wo[:, k, jj * 128:(jj + 1) * 128], mT[:, k, tl(t)], k == 0, k == 7) for k in range(8)]),
                           reads=kwo + [("mT", k, t) for k in range(8)], writes=[bk(b_)])
                    P.emit("dve", STT(hT[:, j, tl(t)], ps[b_][:, :], mod_col(16 + j, b), hT[:, j, tl(t)], ALU.mult, ALU.add),
                           reads=[bk(b_), "mod", ("hT", j, t)], writes=[("hT", j, t)])

        gT = mT

        def ffn_up(b, hf, fb, kwa, wa, kwv, wvv):
            for ff in range(4):
                f = hf * 8 + fb * 4 + ff
                fl = fb * 4 + ff
                fc = slice(ff * 128, (ff + 1) * 128)
                for t in range(NT):
                    ba, bv = nb(), nb()
                    P.emit("pe", MM([(ps[ba][:, :], wa[:, k, fc], uT[:, k, tl(t)], k == 0, k == 7) for k in range(8)]),
                           reads=kwa + [("uT", k, t) for k in range(8)], writes=[bk(ba)])
                    P.emit("pe", MM([(ps[bv][:, :], wvv[:, k, fc], uT[:, k, tl(t)], k == 0, k == 7) for k in range(8)]),
                           reads=kwv + [("uT", k, t) for k in range(8)], writes=[bk(bv)])
                    ab_ = p3buf[:, t % 2, :]
                    pvb = p3buf[:, (t + 1) % 2, :]
                    if t == 0:
                        P.emit("dve", MEMSET(ab_[:, 0:2], 0.0), writes=[("asbh", t % 2)])
                    else:
                        P.emit("act", ACP(ab_[:, 0:2], pvb[:, 512:514]), reads=[("asb", (t + 1) % 2)], writes=[("asbh", t % 2)])
                    P.emit("act", ACP(ab_[:, 2:514], ps[ba][:, :]), reads=[bk(ba)], writes=[("asb", t % 2)])
                    cv = p3buf[:, 2, 0:512]
                    sl = p3buf[:, 3, 0:512]
                    wc = lambda jtap, f=f: smv(V_WCONV + f * 3 + jtap)
                    rk = [("asb", t % 2), ("asbh", t % 2), "vec"]
                    P.emit("dve", TS(cv, ab_[:, 2:514], wc(0), smv(V_BCONV + f), ALU.mult, ALU.add), reads=rk, writes=["cv"])
                    P.emit("dve", STT(cv, ab_[:, 1:513], wc(1), cv, ALU.mult, ALU.add), reads=rk + ["cv"], writes=["cv"])
                    P.emit("dve", STT(cv, ab_[:, 0:512], wc(2), cv, ALU.mult, ALU.add), reads=rk + ["cv"], writes=["cv"])
                    P.emit("act", ACTF(sl, cv, AF.Silu), reads=["cv"], writes=["sl"])
                    P.emit("dve", TT(gT[:, fl, tl(t)], sl, ps[bv][:, :], ALU.mult), reads=["sl", bk(bv)], writes=[("mT", fl, t)])

        def ffn_down(b, hf, cb, kwd, wd):
            for jj in range(4):
                j = cb * 4 + jj
                for t in range(NT):
                    b_ = nb()
                    P.emit("pe", MM([(ps[b_][:, :], wd[:, k, jj * 128:(jj + 1) * 128], gT[:, k, tl(t)], k == 0, k == 7) for k in range(8)]),
                           reads=kwd + [("mT", k, t) for k in range(8)], writes=[bk(b_)])
                    P.emit("dve", STT(hT[:, j, tl(t)], ps[b_][:, :], mod_col(40 + j, b), hT[:, j, tl(t)], ALU.mult, ALU.add),
                           reads=[bk(b_), "mod", ("hT", j, t)], writes=[("hT", j, t)])

        def final_norm(b):
            for t in range(NT):
                sq = A_m[:, (t % 2) * 4096:(t % 2) * 4096 + 4096].rearrange("p (k c) -> p k c", k=8)
                rstd = Amf[:, 4096 + (t % 2) * 512: 4096 + (t % 2) * 512 + 512]
                P.emit("act", ACTF(sq, hT[:, :, tl(t)], AF.Square), reads=[("hT", k, t) for k in range(8)], writes=[("nsq", t % 2)])
                b_ = nb()
                P.emit("pe", MM([(ps[b_][:, :], onesb[:, :], sq[:, k, :], k == 0, k == 7) for k in range(8)]),
                       reads=[("nsq", t % 2), "ones"], writes=[bk(b_)])
                P.emit("act", ACTF(rstd, ps[b_][:, :], AF.Ln, bias=eps_col, scale=1.0 / D), reads=[bk(b_), "consts"], writes=[("nrstd", t % 2)])
                P.emit("act", ACTF(rstd, rstd, AF.Exp, scale=-0.5), reads=[("nrstd", t % 2)], writes=[("nrstd", t % 2)])
                for k in range(8):
                    si = (t * 8 + k) % 4
                    stg = Amf[:, 6144 + si * 512: 6144 + (si + 1) * 512]
                    P.emit("dve", STT(stg, hT[:, k, tl(t)], smv(V_GFIN + k), rstd, ALU.mult, ALU.mult),
                           reads=[("hT", k, t), ("nrstd", t % 2), "vec"], writes=[("ostg", si)])
                    P.emit("sp", DMA(outT[b, k * 128:(k + 1) * 128, tl(t)], stg),
                           reads=[("ostg", si)], writes=[("outT", b, k, t)], dma=True, arena=True)

        stages = []

        def gq_src(q):
            return [("ga", w_in[:, 1792 + q * 256: 1792 + (q + 1) * 256], 8, 256, 2 if q % 2 == 0 else 3, 0),
                    ("gs", w_in[:, 2816 + q * 256: 2816 + (q + 1) * 256], 8, 256, 2 if q % 2 == 0 else 3, 2048)]

        gen_state = {}

        for b in range(NB):
            def c_us(W, b=b):
                load_x(b)
                norm_mod(b, gs1_col, 0, uT)
                P.barrier()
                ssm_tables()
                load_E(0)
                us_proj(b, *W["us"])
                gen_state["gen"] = ssm_gen(b, *W["glu"])
            stages.append(([("us", w_in[:, 1536:1792], 8, 256, 2, 0), ("glu", w_glu[:, :], 2, 256, 1, 3072)], c_us))

            def tick():
                g = gen_state.get("gen")
                if g is not None:
                    try:
                        next(g)
                    except StopIteration:
                        gen_state["gen"] = None

            for hp in range(4):
                def c_att(W, b=b, hp=hp, tick=tick):
                    attention_hp(b, hp, *W["w"], tick)
                stages.append(([("w", w_in[:, hp * 384:(hp + 1) * 384], 8, 384, hp % 2, 0)], c_att))

            def c_tail(W, tick=tick):
                while gen_state.get("gen") is not None:
                    tick()
            stages.append(([], c_tail))
            for q in range(4):
                def c_mq(W, b=b, q=q, st_=state):
                    if q == 0:
                        st_["pa"] = W["pa"]
                        st_["ps"] = W["ps"]
                        P.barrier()
                        load_x(b)
                    kg = W["ga"][0] + W["gs"][0]
                    mixer_q(b, q, st_["pa"][0], st_["pa"][1], st_["ps"][0], st_["ps"][1], kg, W["ga"][1], W["gs"][1])
                lds = gq_src(q)
                if q == 0:
                    lds = [("pa", w_pa[:, :], 4, 1024, 0, 0), ("ps", w_ps[:, :], 2, 1024, 1, 0)] + lds
                stages.append((lds, c_mq))
            for half in range(2):
                def c_wo(W, b=b, half=half):
                    wout_half(b, half, *W["wo"])
                stages.append(([("wo", w_out[:, half * 512:(half + 1) * 512], 8, 512, 2 if half == 0 else 3, 0)], c_wo))
            fslots = [(0, 1), (2, 3), (0,), (1,), (2, 3), (0, 1), (2,), (3,)]
            fi = 0
            for hf in range(2):
                for fb in range(2):
                    c0 = hf * 1024 + fb * 512

                    def c_up(W, b=b, hf=hf, fb=fb):
                        if hf == 0 and fb == 0:
                            P.barrier()
                            norm_mod(b, gs2_col, 24, uT)
                        ffn_up(b, hf, fb, W["wa"][0], W["wa"][1], W["wv"][0], W["wv"][1])
                    stages.append(([("wa", w_up[:, c0:c0 + 512], 8, 512, fslots[fi][0], 0),
                                    ("wv", w_up[:, 2048 + c0:2048 + c0 + 512], 8, 512, fslots[fi][1], 0)], c_up))
                    fi += 1
                for cb in range(2):
                    def c_dn(W, b=b, hf=hf, cb=cb):
                        ffn_down(b, hf, cb, *W["wd"])
                        if hf == 1 and cb == 1:
                            P.barrier()
                            final_norm(b)
                    stages.append(([("wd", w_down[hf * 1024:(hf + 1) * 1024, cb * 512:(cb + 1) * 512], 8, 512, fslots[fi][0], 0)], c_dn))
                    fi += 1

        def do_loads(lds):
            return {nm: load_w(src, nk, ncols, slot=sl_, off=off) for (nm, src, nk, ncols, sl_, off) in lds}

        Wn = do_loads(stages[0][0])
        P.barrier()
        for i, (lds, comp) in enumerate(stages):
            Wc = Wn
            if i + 1 < len(stages):
                Wn = do_loads(stages[i + 1][0])
            comp(Wc)
        P.final_wait("sp")
        P.build()
    return nc


def _host_layouts(inp, NB):
    f32 = np.float32
    x = np.asarray(inp["x"], f32)
    c = np.asarray(inp["c"], f32)
    col8 = lambda v: np.ascontiguousarray(np.asarray(v, f32).reshape(-1, 128).T)
    w_in = np.asarray(inp["w_in"][0], f32)
    perm = []
    for hp in range(4):
        for sec in range(3):
            perm += list(range(sec * 512 + hp * 128, sec * 512 + (hp + 1) * 128))
    perm += list(range(1536, 3840))
    w_in_p = np.ascontiguousarray(w_in[:, perm])
    a_re, a_im, ldt = inp["a_re"][0], inp["a_im"][0], inp["log_dt"][0]
    st_major = lambda a: np.ascontiguousarray(np.asarray(a, f32).reshape(8, 128).T)
    ldt_l = np.ascontiguousarray(np.repeat(np.asarray(ldt, f32), 64).reshape(8, 128).T)
    wconv = np.asarray(inp["w_conv"][0], f32)
    wconv_l = np.ascontiguousarray(wconv.reshape(3, 16, 128).transpose(2, 1, 0).reshape(128, 48))
    vec = np.concatenate([
        col8(inp["g_mix"][0]), col8(inp["g_ffn"][0]), col8(inp["g_final"]), col8(inp["b_gate"][0]),
        col8(inp["b_glu"][0]), wconv_l, col8(inp["b_conv"][0]), st_major(a_re), st_major(a_im), ldt_l,
        col8(inp["b_ada"][0])], axis=1).astype(f32)
    assert vec.shape == (128, NV), vec.shape

    def expand_b(bmat):
        out = np.zeros((128, 8, 128), f32)
        for g in range(16):
            sc, gl = g // 2, g % 2
            c0 = 32 * (sc % 4) + 16 * gl
            out[gl * 64:(gl + 1) * 64, sc, c0:c0 + 16] = bmat[g]
        return out

    def expand_c(cmat):
        out = np.zeros((128, 8, 128), f32)
        for g in range(16):
            sc, gl = g // 2, g % 2
            c0 = 32 * (sc % 4) + 16 * gl
            out[gl * 64:(gl + 1) * 64, sc, c0:c0 + 16] = cmat[g].T
        return out

    d = np.asarray(inp["d_skip"][0], f32)
    d_x = np.zeros((128, 2, 128), f32)
    for h in range(2):
        d_x[np.arange(128), h, np.arange(128)] = d[h * 128:(h + 1) * 128]
    ak = np.arange(128, dtype=np.float64)[:, None]
    aq = np.arange(128, dtype=np.float64)[None, :]
    E = np.zeros((128, 48, 128), f32)
    for h in range(8):
        slope = 2.0 ** (-8.0 * (h + 1) / 8)
        for p, dil in enumerate((1, 4, 16)):
            cur = np.where(ak <= aq, np.exp(-slope * dil * (aq - ak) - 0.0), 0.0)
            prv = np.where(ak >= aq, np.exp(-slope * dil * (128 + aq - ak)), 0.0)
            E[:, (h * 3 + p) * 2 + 0, :] = cur
            E[:, (h * 3 + p) * 2 + 1, :] = prv
    common = dict(
        w_ada=np.ascontiguousarray(inp["w_ada"][0], f32), w_in_p=w_in_p, vec=vec,
        bre_x=expand_b(np.asarray(inp["b_re"][0], f32)), bim_x=expand_b(np.asarray(inp["b_im"][0], f32)),
        cre_x=expand_c(np.asarray(inp["c_re"][0], f32)), cim_x=expand_c(np.asarray(inp["c_im"][0], f32)),
        d_x=d_x, ident=np.eye(128, dtype=f32), iota=np.tile(np.arange(1, 513, dtype=f32)[None, :], (128, 1)), E=E,
        w_glu=np.ascontiguousarray(inp["w_glu"][0], f32), w_proj_att=np.ascontiguousarray(inp["w_proj_att"][0], f32),
        w_proj_ssm=np.ascontiguousarray(inp["w_proj_ssm"][0], f32), w_out=np.ascontiguousarray(inp["w_out"][0], f32),
        w_up=np.ascontiguousarray(inp["w_up"][0], f32), w_down=np.ascontiguousarray(inp["w_down"][0], f32))
    maps = []
    for core in range(NCORES):
        bs = slice(core * NB, (core + 1) * NB)
        m = dict(common)
        m["xT"] = np.ascontiguousarray(x[bs].transpose(0, 2, 1))
        m["cT"] = np.ascontiguousarray(c[bs].reshape(NB, 8, 128).transpose(2, 1, 0))
        maps.append(m)
    return maps


def kernel(**inputs):
    B = inputs["x"].shape[0]
    NB = B // NCORES
    maps = _host_layouts(inputs, NB)
    nc = build_nc(NB)
    res = run_bass_kernel_spmd(nc, maps, core_ids=list(range(NCORES)))
    outs = [np.asarray(r["outT"]).transpose(0, 2, 1) for r in res.results]
    return np.ascontiguousarray(np.concatenate(outs, axis=0).astype(np.float32))
```

```python
import math
from contextlib import ExitStack

import numpy as np
import concourse.bass as bass
import concourse.mybir as mybir
from concourse.bass_utils import run_bass_kernel_spmd

F32 = mybir.dt.float32
BF16 = mybir.dt.bfloat16
I32 = mybir.dt.int32
AF = mybir.ActivationFunctionType
ALU = mybir.AluOpType

NCORES = 8
D = 1024
S = 2048
NT = 4
EPS = 1e-6
SHIFT = 8.0
TWO_PI = 2.0 * math.pi
PI_LO = 3.1415925
ENGS = ("pe", "act", "dve", "pool", "sp")

V_GMIX, V_GFFN, V_GFIN, V_BGATE, V_BGLU, V_WCONV, V_BCONV, V_ARE, V_AIM, V_LDT, V_BADA = (
    0, 8, 16, 24, 40, 42, 90, 106, 114, 122, 130)
NV = 178


class Prog:
    def __init__(self, nc, stack, n_dma_sems=8):
        self.nc = nc
        self.ops = {e: [] for e in ENGS}
        self.cnt = {e: 0 for e in ENGS}
        self.semobj = {}
        for e in ENGS:
            self.semobj[("c", e)] = stack.enter_context(nc.semaphore("c_" + e))
        self.dval = {q: [0] * n_dma_sems for q in ("sp", "pool")}
        self.dnext = {q: 0 for q in ("sp", "pool")}
        for q in ("sp", "pool"):
            for i in range(n_dma_sems):
                self.semobj[("d", q, i)] = stack.enter_context(nc.semaphore("d_%s%d" % (q, i)))
        self.waited = {e: {} for e in ENGS}
        self.regions = {}
        self.fence = []
        self.arena_toks = {}

    def _need(self, eng, waits, tok):
        if tok is None:
            return
        sid, val = tok
        if self.waited[eng].get(sid, 0) >= val:
            return
        if waits.get(sid, 0) < val:
            waits[sid] = val

    def emit(self, eng, fn, reads=(), writes=(), dma=False, arena=False):
        waits = {}
        own = ("c", eng)
        if dma and arena:
            for t in self.fence:
                self._need(eng, waits, t)
        for key in reads:
            r = self.regions.get(key)
            if r is not None:
                self._need(eng, waits, r["w"])
        for key in writes:
            r = self.regions.get(key)
            if r is not None:
                w = r["w"]
                if w is not None and not (w[0] == own and not dma):
                    self._need(eng, waits, w)
                for sid, val in r["r"].items():
                    if sid == own and not dma:
                        continue
                    self._need(eng, waits, (sid, val))
        if dma:
            i = self.dnext[eng]
            self.dnext[eng] = (i + 1) % len(self.dval[eng])
            sid = ("d", eng, i)
            if self.dval[eng][i] > 0:
                self._need(eng, waits, (sid, self.dval[eng][i]))
            self.dval[eng][i] += 16
            tok = (sid, self.dval[eng][i])
            amt = 16
            if arena:
                self.arena_toks[sid] = tok[1]
        else:
            self.cnt[eng] += 1
            tok = (own, self.cnt[eng])
            amt = 1
        for sid, val in waits.items():
            self.waited[eng][sid] = val
        for key in reads:
            r = self.regions.setdefault(key, {"w": None, "r": {}})
            if r["r"].get(tok[0], 0) < tok[1]:
                r["r"][tok[0]] = tok[1]
        for key in writes:
            self.regions[key] = {"w": tok, "r": {}}
        self.ops[eng].append((list(waits.items()), fn, tok, amt))
        return tok

    def _all_toks(self):
        toks = []
        for e in ENGS:
            if self.cnt[e] > 0:
                toks.append((("c", e), self.cnt[e]))
        for q in ("sp", "pool"):
            for i, v in enumerate(self.dval[q]):
                if v > 0:
                    toks.append((("d", q, i), v))
        return toks

    def barrier(self):
        toks = self._all_toks()
        for e in ENGS:
            waits = {}
            for t in toks:
                if t[0] == ("c", e):
                    continue
                self._need(e, waits, t)
            for sid, val in waits.items():
                self.waited[e][sid] = val
            if waits:
                self.ops[e].append((list(waits.items()), None, None, 0))

    def final_wait(self, eng="sp"):
        waits = {}
        for t in self._all_toks():
            if t[0] == ("c", eng):
                continue
            self._need(eng, waits, t)
        self.ops[eng].append((list(waits.items()), None, None, 0))

    def build(self):
        nc = self.nc
        with nc.Block() as block:
            def mk(e):
                def body(h):
                    for waits, fn, tok, amt in self.ops[e]:
                        for sid, val in waits:
                            h.wait_ge(self.semobj[sid], val)
                        if fn is None:
                            continue
                        ins = fn(h)
                        ins.then_inc(self.semobj[tok[0]], amt)
                return body
            block.tensor(mk("pe"))
            block.scalar(mk("act"))
            block.vector(mk("dve"))
            block.gpsimd(mk("pool"))
            block.sync(mk("sp"))


def MM(lst):
    def fn(e):
        ins = None
        for (out, lhsT, rhs, st, sp) in lst:
            ins = e.matmul(out, lhsT=lhsT, rhs=rhs, start=st, stop=sp)
        return ins
    return fn


def ACTF(out, in_, func, bias=None, scale=None):
    def fn(e):
        kw = {}
        if bias is not None:
            kw["bias"] = bias
        if scale is not None:
            kw["scale"] = scale
        return e.activation(out=out, in_=in_, func=func, **kw)
    return fn


def TT(out, in0, in1, op):
    return lambda e: e.tensor_tensor(out=out, in0=in0, in1=in1, op=op)


def TS(out, in0, s1, s2, op0, op1=None):
    if op1 is None:
        return lambda e: e.tensor_scalar(out=out, in0=in0, scalar1=s1, scalar2=None, op0=op0)
    return lambda e: e.tensor_scalar(out=out, in0=in0, scalar1=s1, scalar2=s2, op0=op0, op1=op1)


def STT(out, in0, scalar, in1, op0, op1):
    return lambda e: e.scalar_tensor_tensor(out=out, in0=in0, scalar=scalar, in1=in1, op0=op0, op1=op1)


def CP(out, in_):
    return lambda e: e.tensor_copy(out=out, in_=in_)


def ACP(out, in_):
    return lambda e: e.activation(out=out, in_=in_, func=AF.Copy)


def DMA(out, in_):
    return lambda e: e.dma_start(out=out, in_=in_)


def SCAN(out, d0, d1, init):
    return lambda e: e.tensor_tensor_scan(out=out, data0=d0, data1=d1, initial=init, op0=ALU.mult, op1=ALU.add)


def RECIP(out, in_):
    return lambda e: e.reciprocal(out=out, in_=in_)


def MEMSET(ap, v):
    return lambda e: e.memset(ap, v)


def TRANSP(out, in_, ident):
    return lambda e: e.transpose(out=out, in_=in_, identity=ident)


def xap(base, start, dims):
    return bass.AP(base.tensor, base.offset + start, [list(base.ap[0])] + [list(d) for d in dims])


def build_nc(NB):
    nc = bass.Bass("TRN2", target_bir_lowering=False)

    def din(name, shape, dtype=F32):
        return nc.dram_tensor(name, list(shape), dtype, kind="ExternalInput").ap()

    xT = din("xT", [NB, D, S])
    cT = din("cT", [128, 8, NB])
    w_ada = din("w_ada", [D, 6 * D])
    w_in = din("w_in_p", [D, 3840])
    vec = din("vec", [128, NV])
    bre_x = din("bre_x", [128, 8, 128])
    bim_x = din("bim_x", [128, 8, 128])
    cre_x = din("cre_x", [128, 8, 128])
    cim_x = din("cim_x", [128, 8, 128])
    d_x = din("d_x", [128, 2, 128])
    ident_d = din("ident", [128, 128])
    iota_d = din("iota", [128, 512])
    E_d = din("E", [128, 48, 128])
    w_glu = din("w_glu", [256, 256])
    w_pa = din("w_proj_att", [512, D])
    w_ps = din("w_proj_ssm", [256, D])
    w_out = din("w_out", [D, D])
    w_up = din("w_up", [D, 4096])
    w_down = din("w_down", [2048, D])
    outT = nc.dram_tensor("outT", [NB, D, S], F32, kind="ExternalOutput").ap()
    trig_scr = nc.dram_tensor("trig_scr", [128, 2, 8, 512], F32).ap()
    btab_scr = nc.dram_tensor("btab_scr", [128, 8, 2, 128], BF16).ap()

    with ExitStack() as st:
        P = Prog(nc, st)
        sb = lambda name, shape, dt: st.enter_context(nc.sbuf_tensor(name, list(shape), dt))
        A_h = sb("A_h", [128, 16384], F32)
        A_u = sb("A_u", [128, 16384], BF16)
        A_m = sb("A_m", [128, 16384], BF16)
        oatt = sb("oatt", [128, 4, S], BF16)
        ssmo = sb("ssmo", [128, 2, S], BF16)
        usT = sb("usT", [128, 2, S], BF16)
        NSLOT = 4
        slots = [sb("slot%d" % i, [128, 4096], BF16) for i in range(NSLOT)]
        p3buf = sb("p3buf", [128, 4, 514], F32)
        sm = sb("sm", [128, 768], F32)
        csb = sb("csb", [128, 8, NB], BF16)
        onesb = sb("onesb", [128, 128], BF16)
        ps = [st.enter_context(nc.psum_tensor("ps%d" % i, [128, 512], F32)) for i in range(8)]

        hT = A_h[:, :].rearrange("p (k t) -> p k t", k=8)
        uT = A_u[:, :].rearrange("p (k t) -> p k t", k=8)
        mT = A_m[:, :].rearrange("p (k t) -> p k t", k=8)
        Ahb = A_h[:, :].bitcast(BF16)
        Amf = A_m[:, :].bitcast(F32)

        SM_VEC = 0
        SM_MOD = 192
        SM_GS1 = SM_MOD + 48 * NB
        SM_GS2 = SM_GS1 + 8 * NB
        SM_SSM = SM_GS2 + 8 * NB
        assert SM_SSM + 26 <= 768
        eps_col = sm[:, SM_SSM + 24: SM_SSM + 25]
        nshift_col = sm[:, SM_SSM + 25: SM_SSM + 26]
        smv = lambda off, n=1: sm[:, off:off + n]
        mod_col = lambda chunk, b: sm[:, SM_MOD + chunk * NB + b: SM_MOD + chunk * NB + b + 1]
        gs1_col = lambda k, b: sm[:, SM_GS1 + k * NB + b: SM_GS1 + k * NB + b + 1]
        gs2_col = lambda k, b: sm[:, SM_GS2 + k * NB + b: SM_GS2 + k * NB + b + 1]
        mag_col = lambda sc: sm[:, SM_SSM + sc: SM_SSM + sc + 1]
        carr_col = lambda sc: sm[:, SM_SSM + 8 + sc: SM_SSM + 9 + sc]
        cari_col = lambda sc: sm[:, SM_SSM + 16 + sc: SM_SSM + 17 + sc]

        state = {"bank": 0, "slot": 0, "ev": 0}

        def nb():
            i = state["bank"]
            state["bank"] = (i + 1) % 6
            return i

        def bk(i):
            return ("ps", i)

        def ev():
            state["ev"] ^= 1
            return "act" if state["ev"] else "dve"

        def evac(out, in_, reads, writes):
            P.emit("act", ACP(out, in_), reads=reads, writes=writes)

        def load_w(src2d, nk, ncols, slot=None, off=0):
            if slot is None:
                i = state["slot"]
                state["slot"] = (i + 1) % NSLOT
            else:
                i = slot
            n = nk * ncols
            keys = [("slot", i, h) for h in range(2) if off < (h + 1) * 2048 and off + n > h * 2048]
            view = slots[i][:, off:off + n].rearrange("p (k c) -> p k c", k=nk)
            P.emit("pool", DMA(view, src2d.rearrange("(k p) c -> p k c", p=128)), writes=keys, dma=True)
            return keys, view

        tl = lambda t: slice(t * 512, (t + 1) * 512)

        P.emit("sp", DMA(sm[:, 0:NV], vec[:, :]), writes=["vec"], dma=True, arena=True)
        P.emit("dve", MEMSET(onesb[:, :], 1.0), writes=["ones"])
        P.emit("dve", MEMSET(eps_col, EPS), writes=["consts"])
        P.emit("dve", MEMSET(nshift_col, -SHIFT), writes=["consts"])
        ctmp = A_h[:, 0:8 * NB].rearrange("p (k b) -> p k b", k=8)
        P.emit("sp", DMA(ctmp, cT[:, :, :]), writes=["ctmp"], dma=True, arena=True)
        P.emit("act", ACTF(csb[:, :, :], ctmp, AF.Silu), reads=["ctmp"], writes=["csb"])
        for jb in range(12):
            key, wv = load_w(w_ada[:, jb * 512:(jb + 1) * 512], 8, 512)
            for jj in range(4):
                j = jb * 4 + jj
                b_ = nb()
                P.emit("pe", MM([(ps[b_][:, 0:NB], wv[:, k, jj * 128:(jj + 1) * 128], csb[:, k, :], k == 0, k == 7)
                                 for k in range(8)]), reads=key + ["csb"], writes=[bk(b_)])
                P.emit("dve", TS(sm[:, SM_MOD + j * NB: SM_MOD + (j + 1) * NB], ps[b_][:, 0:NB],
                                 smv(V_BADA + j), None, ALU.add), reads=[bk(b_), "vec"], writes=["mod"])
        for k in range(8):
            P.emit("dve", TS(sm[:, SM_GS1 + k * NB: SM_GS1 + (k + 1) * NB],
                             sm[:, SM_MOD + (8 + k) * NB: SM_MOD + (9 + k) * NB], 1.0, smv(V_GMIX + k), ALU.add, ALU.mult),
                   reads=["mod", "vec"], writes=["gs"])
            P.emit("dve", TS(sm[:, SM_GS2 + k * NB: SM_GS2 + (k + 1) * NB],
                             sm[:, SM_MOD + (32 + k) * NB: SM_MOD + (33 + k) * NB], 1.0, smv(V_GFFN + k), ALU.add, ALU.mult),
                   reads=["mod", "vec"], writes=["gs"])

        pt = lambda i: A_h[:, 512 + 8 * i: 512 + 8 * (i + 1)]
        Xre = A_h[:, 1024:2048].rearrange("p (s c) -> p s c", s=8)
        Xim = A_h[:, 2048:3072].rearrange("p (s c) -> p s c", s=8)
        bre = A_h[:, 3072:4096].rearrange("p (s c) -> p s c", s=8)
        bim = A_h[:, 4096:5120].rearrange("p (s c) -> p s c", s=8)
        iota = A_h[:, 5120:5632]
        ident = A_h[:, 5632:5760]
        P.emit("sp", DMA(bre, bre_x[:, :, :]), writes=["bre"], dma=True, arena=True)
        P.emit("sp", DMA(bim, bim_x[:, :, :]), writes=["bim"], dma=True, arena=True)
        P.emit("sp", DMA(iota, iota_d[:, :]), writes=["iota"], dma=True, arena=True)
        P.emit("sp", DMA(ident, ident_d[:, :]), writes=["ident"], dma=True, arena=True)
        lr, li, ldt = smv(V_ARE, 8), smv(V_AIM, 8), smv(V_LDT, 8)
        T = {}

        def pe_(name, eng, fn, reads):
            P.emit(eng, fn, reads=reads + ["vec"], writes=[name])

        names = ["dt", "lrdt", "th", "ki", "kf", "thr", "sin", "ab", "cos", "abr", "abi", "nr", "t1", "t2", "den",
                 "rden", "u1", "u2", "fre", "fim", "nfim", "u3", "u4"]
        for i, n in enumerate(names):
            T[n] = pt(i)
        T["ki"] = pt(names.index("ki")).bitcast(I32)
        pe_("dt", "act", ACTF(T["dt"], ldt, AF.Exp), [])
        pe_("lrdt", "dve", TT(T["lrdt"], lr, T["dt"], ALU.mult), ["dt"])
        pe_("mag", "act", ACTF(smv(SM_SSM, 8), T["lrdt"], AF.Exp), ["lrdt"])
        pe_("th", "dve", TT(T["th"], li, T["dt"], ALU.mult), ["dt"])
        pe_("ki", "dve", TS(T["ki"], T["th"], 1.0 / TWO_PI, None, ALU.mult), ["th"])
        pe_("kf", "dve", CP(T["kf"], T["ki"]), ["ki"])
        pe_("thr", "dve", STT(T["thr"], T["kf"], -TWO_PI, T["th"], ALU.mult, ALU.add), ["kf", "th"])
        pe_("thr", "dve", TS(T["thr"], T["thr"], PI_LO, -PI_LO, ALU.min, ALU.max), ["thr"])
        pe_("sin", "act", ACTF(T["sin"], T["thr"], AF.Sin), ["thr"])
        pe_("ab", "act", ACTF(T["ab"], T["thr"], AF.Abs), ["thr"])
        pe_("ab2", "dve", TS(T["u4"], T["ab"], -1.0, math.pi / 2, ALU.mult, ALU.add), ["ab"])
        pe_("cos", "act", ACTF(T["cos"], T["u4"], AF.Sin), ["ab2"])
        pe_("abr", "dve", TT(T["abr"], smv(SM_SSM, 8), T["cos"], ALU.mult), ["mag", "cos"])
        pe_("abi", "dve", TT(T["abi"], smv(SM_SSM, 8), T["sin"], ALU.mult), ["mag", "sin"])
        pe_("nr", "dve", TS(T["nr"], T["abr"], -1.0, None, ALU.add), ["abr"])
        pe_("t1", "dve", TT(T["t1"], lr, lr, ALU.mult), [])
        pe_("t2", "dve", TT(T["t2"], li, li, ALU.mult), [])
        pe_("den", "dve", TT(T["den"], T["t1"], T["t2"], ALU.add), ["t1", "t2"])
        pe_("rden", "dve", RECIP(T["rden"], T["den"]), ["den"])
        pe_("u1", "dve", TT(T["u1"], T["nr"], lr, ALU.mult), ["nr"])
        pe_("u2", "dve", TT(T["u2"], T["abi"], li, ALU.mult), ["abi"])
        pe_("u3", "dve", TT(T["u3"], T["u1"], T["u2"], ALU.add), ["u1", "u2"])
        pe_("fre", "dve", TT(T["fre"], T["u3"], T["rden"], ALU.mult), ["u3", "rden"])
        pe_("u1b", "dve", TT(T["u1"], T["abi"], lr, ALU.mult), ["abi", "u3"])
        pe_("u2b", "dve", TT(T["u2"], T["nr"], li, ALU.mult), ["nr", "u3"])
        pe_("u3b", "dve", TT(T["u3"], T["u1"], T["u2"], ALU.subtract), ["u1b", "u2b", "fre"])
        pe_("fim", "dve", TT(T["fim"], T["u3"], T["rden"], ALU.mult), ["u3b", "rden"])
        pe_("nfim", "dve", TS(T["nfim"], T["fim"], -1.0, None, ALU.mult), ["fim"])
        btab = Ahb[:, 2 * 13312: 2 * 13312 + 2048].rearrange("p (s r c) -> p s r c", s=8, r=2)
        for sc in range(8):
            c1 = lambda t, sc=sc: t[:, sc:sc + 1]
            P.emit("dve", TS(Xre[:, sc, :], bre[:, sc, :], c1(T["fre"]), None, ALU.mult), reads=["bre", "fre"], writes=[("Xre", sc)])
            P.emit("dve", STT(Xre[:, sc, :], bim[:, sc, :], c1(T["nfim"]), Xre[:, sc, :], ALU.mult, ALU.add),
                   reads=["bim", "nfim", ("Xre", sc)], writes=[("Xre", sc)])
            P.emit("dve", TS(Xim[:, sc, :], bre[:, sc, :], c1(T["fim"]), None, ALU.mult), reads=["bre", "fim"], writes=[("Xim", sc)])
            P.emit("dve", STT(Xim[:, sc, :], bim[:, sc, :], c1(T["fre"]), Xim[:, sc, :], ALU.mult, ALU.add),
                   reads=["bim", "fre", ("Xim", sc)], writes=[("Xim", sc)])
            for ri, X in enumerate((Xre, Xim)):
                b_ = nb()
                P.emit("pe", TRANSP(ps[b_][:, 0:128], X[:, sc, :], ident), reads=[("Xre" if ri == 0 else "Xim", sc), "ident"],
                       writes=[bk(b_)])
                evac(btab[:, sc, ri, :], ps[b_][:, 0:128], [bk(b_)], [("btab", sc, ri)])
        P.emit("sp", DMA(btab_scr[:, :, :, :], btab), reads=[("btab", sc, ri) for sc in range(8) for ri in range(2)],
               writes=["btab_scr"], dma=True, arena=True)
        for sc in range(8):
            base = 6144 + (sc % 2) * 3072
            arg = A_h[:, base:base + 512]
            ki = A_h[:, base + 512:base + 1024].bitcast(I32)
            kf = A_h[:, base + 1024:base + 1536]
            so = A_h[:, base + 1536:base + 2048]
            ab = A_h[:, base + 2048:base + 2560]
            co = A_h[:, base + 2560:base + 3072]
            kk = lambda n, sc=sc: ("tg", n, sc % 2)
            P.emit("dve", TS(arg, iota, T["thr"][:, sc:sc + 1], None, ALU.mult), reads=["iota", "thr"], writes=[kk("arg")])
            P.emit("dve", TS(ki, arg, 1.0 / TWO_PI, None, ALU.mult), reads=[kk("arg")], writes=[kk("ki")])
            P.emit("dve", CP(kf, ki), reads=[kk("ki")], writes=[kk("kf")])
            P.emit("dve", STT(arg, kf, -TWO_PI, arg, ALU.mult, ALU.add), reads=[kk("kf"), kk("arg")], writes=[kk("arg")])
            P.emit("dve", TS(arg, arg, PI_LO, -PI_LO, ALU.min, ALU.max), reads=[kk("arg")], writes=[kk("arg")])
            P.emit("act", ACTF(so, arg, AF.Sin), reads=[kk("arg")], writes=[kk("so")])
            P.emit("act", ACTF(ab, arg, AF.Abs), reads=[kk("arg")], writes=[kk("ab")])
            P.emit("dve", TS(ab, ab, -1.0, math.pi / 2, ALU.mult, ALU.add), reads=[kk("ab")], writes=[kk("ab")])
            P.emit("act", ACTF(co, ab, AF.Sin), reads=[kk("ab")], writes=[kk("co")])
            P.emit("sp", DMA(trig_scr[:, 0, sc, :], co), reads=[kk("co")], writes=[("trig_scr", 0, sc)], dma=True, arena=True)
            P.emit("sp", DMA(trig_scr[:, 1, sc, :], so), reads=[kk("so")], writes=[("trig_scr", 1, sc)], dma=True, arena=True)

        def mtk(chunks):
            return [("mT", c, tt) for c in chunks for tt in range(NT)]

        def norm_mod(b, gs_col, sh_chunk0, dst):
            for t in range(NT):
                sq = A_m[:, (t % 2) * 4096:(t % 2) * 4096 + 4096].rearrange("p (k c) -> p k c", k=8)
                rstd = Amf[:, 4096 + (t % 2) * 512: 4096 + (t % 2) * 512 + 512]
                sqk = mtk((2 * (t % 2), 2 * (t % 2) + 1))
                P.emit("act", ACTF(sq, hT[:, :, tl(t)], AF.Square), reads=[("hT", k, t) for k in range(8)],
                       writes=[("nsq", t % 2)] + sqk)
                b_ = nb()
                P.emit("pe", MM([(ps[b_][:, :], onesb[:, :], sq[:, k, :], k == 0, k == 7) for k in range(8)]),
                       reads=[("nsq", t % 2), "ones"] + sqk, writes=[bk(b_)])
                P.emit("act", ACTF(rstd, ps[b_][:, :], AF.Ln, bias=eps_col, scale=1.0 / D),
                       reads=[bk(b_), "consts"], writes=[("nrstd", t % 2)] + mtk((4,)))
                P.emit("act", ACTF(rstd, rstd, AF.Exp, scale=-0.5), reads=[("nrstd", t % 2)], writes=[("nrstd", t % 2)] + mtk((4,)))
                for k in range(8):
                    tmp = Amf[:, 5120 + (k % 2) * 512: 5120 + (k % 2) * 512 + 512]
                    P.emit("dve", TT(tmp, hT[:, k, tl(t)], rstd, ALU.mult), reads=[("hT", k, t), ("nrstd", t % 2)] + mtk((4,)),
                           writes=[("ntmp", k % 2)] + mtk((5,)))
                    P.emit("act", ACTF(dst[:, k, tl(t)], tmp, AF.Identity, bias=mod_col(sh_chunk0 + k, b), scale=gs_col(k, b)),
                           reads=[("ntmp", k % 2), "mod", "gs"] + mtk((5,)), writes=[("uT", k, t)])

        Etabs = [Ahb[:, i * 1536:(i + 1) * 1536].rearrange("p (i q) -> p i q", i=12) for i in range(2)]
        qT = Ahb[:, 3072:5120]
        kT = Ahb[:, 5120:7168]
        Vt = [Ahb[:, 7168 + i * 2048: 7168 + (i + 1) * 2048].rearrange("p (b c) -> p b c", b=16) for i in range(3)]
        Pb = [Ahb[:, 13312 + i * 512: 13312 + (i + 1) * 512] for i in range(8)]
        acc_o = A_h[:, 8704:10752]
        acc_d = A_h[:, 10752:12800]

        def load_E(hp):
            P.emit("pool", DMA(Etabs[hp % 2], E_d[:, hp * 12:(hp + 1) * 12, :]), writes=[("Etab", hp % 2)], dma=True, arena=True)

        def tokset(pat, blk):
            if pat == 0:
                return 128 * blk, 1
            if pat == 1:
                return 512 * (blk % 4) + blk // 4, 4
            return blk, 16

        def cs_(ap2d, pat, blk):
            base, stp = tokset(pat, blk)
            return ap2d[:, base: base + 127 * stp + 1: stp]

        def acc_view(acc, pat, grp):
            if pat == 0:
                return acc[:, 512 * grp: 512 * grp + 512].rearrange("p (j a) -> p j a", j=4)
            if pat == 1:
                return xap(acc[:, 0:1], 512 * grp, [[1, 4], [4, 128]])
            return xap(acc[:, 0:1], 4 * grp, [[1, 4], [16, 128]])

        pbi = [0]

        def attention_hp(b, hp, key, wv, tick):
            if hp < 3:
                load_E(hp + 1)
            for which, dst, nm in ((0, qT, "qT"), (1, kT, "kT")):
                for t in range(NT):
                    b_ = nb()
                    P.emit("pe", MM([(ps[b_][:, :], wv[:, k, which * 128:(which + 1) * 128], uT[:, k, tl(t)], k == 0, k == 7)
                                     for k in range(8)]), reads=key + [("uT", k, t) for k in range(8)], writes=[bk(b_)])
                    evac(dst[:, tl(t)], ps[b_][:, :], [bk(b_)], [(nm, t)])
            for pat in range(3):
                for b4 in range(4):
                    b_ = nb()
                    lst = []
                    for jj in range(4):
                        blk = b4 * 4 + jj
                        for k in range(8):
                            lst.append((ps[b_][:, jj * 128:(jj + 1) * 128], cs_(uT[:, k, :], pat, blk), wv[:, k, 256:384],
                                        k == 0, k == 7))
                    P.emit("pe", MM(lst), reads=key + [("uT", k, t) for k in range(8) for t in range(NT)], writes=[bk(b_)])
                    evac(Vt[pat][:, b4 * 4:(b4 + 1) * 4, :], ps[b_][:, :].rearrange("p (j c) -> p j c", j=4), [bk(b_)],
                         [("V", pat, b4)])
            units = [(pat, grp, hd) for pat in range(3) for grp in range(4) for hd in range(2)]
            pend = None
            obank = {}

            def emit_qk(u):
                pat, grp, hd = u
                h = hp * 2 + hd
                rows = slice(hd * 64, hd * 64 + 64)
                info = {"cur": None, "prev": None, "mask": []}
                for typ in ("cur", "prev"):
                    lst = []
                    js = []
                    for j in range(4):
                        if pat == 0:
                            qb = 4 * grp + j
                            kb = qb if typ == "cur" else qb - 1
                            ok = kb >= 0
                        elif pat == 1:
                            qb = j * 4 + grp
                            kb = qb if typ == "cur" else qb - 1
                            ok = (typ == "cur") or grp >= 1
                        else:
                            qb = 4 * grp + j
                            kb = qb
                            ok = typ == "cur"
                        if ok:
                            js.append((j, qb, kb))
                    if not js:
                        continue
                    b_ = nb()
                    for (j, qb, kb) in js:
                        lst.append((ps[b_][:, j * 128:(j + 1) * 128], cs_(kT[rows, :], pat, kb), cs_(qT[rows, :], pat, qb), True, True))
                    P.emit("pe", MM(lst), reads=[("qT", t) for t in range(NT)] + [("kT", t) for t in range(NT)], writes=[bk(b_)])
                    j0 = js[0][0]
                    pi = pbi[0]
                    pbi[0] = (pi + 1) % 8
                    pv = Pb[pi][:, j0 * 128:512]
                    P.emit("act", ACTF(pv, ps[b_][:, j0 * 128:512], AF.Exp, bias=nshift_col, scale=0.125),
                           reads=[bk(b_), "consts"], writes=[("Pb", pi)])
                    ei = (hd * 3 + pat) * 2 + (0 if typ == "cur" else 1)
                    nj = 4 - j0
                    ebc = xap(Etabs[hp % 2][:, ei, 0:1], 0, [[0, nj], [1, 128]])
                    pv3 = pv.rearrange("p (j a) -> p j a", j=nj)
                    info["mask"].append((pv3, ebc, pi))
                    info[typ] = (pi, js)
                return info

            def emit_mask(info):
                for (pv3, ebc, pi) in info["mask"]:
                    P.emit("dve", TT(pv3, pv3, ebc, ALU.mult), reads=[("Pb", pi), ("Etab", hp % 2)], writes=[("Pb", pi)])

            def emit_pv(u, info):
                pat, grp, hd = u
                rows = slice(hd * 64, hd * 64 + 64)
                if hd == 0:
                    obank[(pat, grp)] = (6, 7)
                bo, bd = obank[(pat, grp)]
                lst = []
                reads = [("V", pat, b4) for b4 in range(4)] + ["ones"]
                pcur, jcur = info["cur"]
                reads.append(("Pb", pcur))
                prevmap = {}
                if info["prev"] is not None:
                    pprev, jprev = info["prev"]
                    reads.append(("Pb", pprev))
                    prevmap = {j: kb for (j, qb, kb) in jprev}
                for (j, qb, kb) in jcur:
                    oc = slice(j * 128, (j + 1) * 128)
                    hasp = j in prevmap
                    for dst, isden in ((ps[bo], False), (ps[bd], True)):
                        if hasp:
                            lw = onesb[:, 0:64] if isden else Vt[pat][:, prevmap[j], rows]
                            lst.append((dst[rows, oc], lw, Pb[pprev][:, oc], True, False))
                        lw = onesb[:, 0:64] if isden else Vt[pat][:, kb, rows]
                        lst.append((dst[rows, oc], lw, Pb[pcur][:, oc], not hasp, True))
                P.emit("pe", MM(lst), reads=reads, writes=[bk(bo), bk(bd)])
                if hd == 1:
                    gk = [grp] if pat < 2 else [0, 1, 2, 3]
                    for acc, bb, nm in ((acc_o, bo, "acc_o"), (acc_d, bd, "acc_d")):
                        av = acc_view(acc, pat, grp)
                        src = ps[bb][:, :].rearrange("p (j a) -> p j a", j=4)
                        if pat == 0:
                            evac(av, src, [bk(bb)], [(nm, g) for g in gk])
                        else:
                            P.emit("dve", TT(av, av, src, ALU.add), reads=[bk(bb)] + [(nm, g) for g in gk],
                                   writes=[(nm, g) for g in gk])

            infos = []
            for ui, u in enumerate(units):
                infos.append(emit_qk(u))
                tick()
                if ui >= 1:
                    emit_mask(infos[ui - 1])
                if ui >= 2:
                    emit_pv(units[ui - 2], infos[ui - 2])
                tick()
            nU = len(units)
            emit_mask(infos[nU - 1])
            emit_pv(units[nU - 2], infos[nU - 2])
            emit_pv(units[nU - 1], infos[nU - 1])
            P.emit("act", ACTF(acc_d, acc_d, AF.Ln), reads=[("acc_d", g) for g in range(4)], writes=[("acc_d", g) for g in range(4)])
            P.emit("act", ACTF(acc_d, acc_d, AF.Exp, scale=-1.0), reads=[("acc_d", g) for g in range(4)], writes=[("acc_d", g) for g in range(4)])
            P.emit("dve", TT(oatt[:, hp, :], acc_o, acc_d, ALU.mult), reads=[("acc_o", g) for g in range(4)] + [("acc_d", g) for g in range(4)],
                   writes=[("oatt", hp)])

        cosT = slots[3][:, :].bitcast(F32).rearrange("p (s c) -> p s c", s=4)
        sinT = p3buf[:, :, 0:512]
        wk = lambda i: A_h[:, 12800 + i * 512: 12800 + (i + 1) * 512]
        xb = lambda i: A_m[:, 12288 + i * 512: 12288 + (i + 1) * 512]
        Btab = A_m[:, 0:2048].rearrange("p (s r c) -> p s r c", s=8, r=2)
        Cre = A_m[:, 2048:3072].rearrange("p (s c) -> p s c", s=8)
        Cim = A_m[:, 3072:4096].rearrange("p (s c) -> p s c", s=8)
        Dx = A_m[:, 4096:4352].rearrange("p (s c) -> p s c", s=2)
        yg = A_m[:, 4352:4352 + 4096].rearrange("p (h t) -> p h t", h=2)
        ysb = Amf[:, 4608:5120]
        yt = Amf[:, 5120:5632]
        ysig = Amf[:, 5632:6144]

        def ssm_tables():
            P.emit("sp", DMA(Btab, btab_scr[:, :, :, :]), reads=["btab_scr"], writes=["Btab"], dma=True, arena=True)
            P.emit("pool", DMA(Cre, cre_x[:, :, :]), writes=["Cre"], dma=True, arena=True)
            P.emit("pool", DMA(Cim, cim_x[:, :, :]), writes=["Cim"], dma=True, arena=True)
            P.emit("pool", DMA(Dx, d_x[:, :, :]), writes=["Dx"], dma=True, arena=True)

        def us_proj(b, key, wv):
            for c in range(2):
                for t in range(NT):
                    b_ = nb()
                    P.emit("pe", MM([(ps[b_][:, :], wv[:, k, c * 128:(c + 1) * 128], uT[:, k, tl(t)], k == 0, k == 7) for k in range(8)]),
                           reads=key + [("uT", k, t) for k in range(8)], writes=[bk(b_)])
                    evac(usT[:, c, tl(t)], ps[b_][:, :], [bk(b_)], [("usT", c, t)])

        def ssm_gen(b, kglu, wglu):
            for h in range(2):
                P.emit("sp", DMA(cosT, trig_scr[:, 0, 4 * h:4 * h + 4, :]), reads=[("trig_scr", 0, sc) for sc in range(8)],
                       writes=[("slot", 3, 0), ("slot", 3, 1)], dma=True, arena=True)
                P.emit("sp", DMA(sinT, trig_scr[:, 1, 4 * h:4 * h + 4, :]), reads=[("trig_scr", 1, sc) for sc in range(8)],
                       writes=["sinT"], dma=True, arena=True)
                ck = [("slot", 3, 0), ("slot", 3, 1)]
                for tc in range(NT):
                    xbufs = []
                    for s4 in range(4):
                        sc = h * 4 + s4
                        br_, bi_ = nb(), nb()
                        P.emit("pe", MM([(ps[br_][:, :], Btab[:, sc, 0, :], usT[:, h, tl(tc)], True, True)]),
                               reads=["Btab", ("usT", h, tc)], writes=[bk(br_)])
                        P.emit("pe", MM([(ps[bi_][:, :], Btab[:, sc, 1, :], usT[:, h, tl(tc)], True, True)]),
                               reads=["Btab", ("usT", h, tc)], writes=[bk(bi_)])
                        c_, s_ = cosT[:, s4, :], sinT[:, s4, :]
                        W0, W1, W2, W3, W4, W5 = [wk(i) for i in range(6)]
                        K = lambda n: ("wk", n)
                        PR, PI = ps[br_][:, :], ps[bi_][:, :]
                        d = lambda fn, r, w: P.emit("dve", fn, reads=r, writes=w)
                        d(TT(W0, PR, c_, ALU.mult), [bk(br_)] + ck, [K(0)])
                        d(TT(W1, PI, s_, ALU.mult), [bk(bi_), "sinT"], [K(1)])
                        d(TT(W0, W0, W1, ALU.add), [K(0), K(1)], [K(0)])
                        d(TT(W1, PI, c_, ALU.mult), [bk(bi_)] + ck, [K(1)])
                        d(TT(W2, PR, s_, ALU.mult), [bk(br_), "sinT"], [K(2)])
                        d(TT(W1, W1, W2, ALU.subtract), [K(1), K(2)], [K(1)])
                        yield 1
                        magb = xap(mag_col(sc), 0, [[0, 512]])
                        ir = 0.0 if tc == 0 else carr_col(sc)
                        ii = 0.0 if tc == 0 else cari_col(sc)
                        d(SCAN(W3, magb, W0, ir), [K(0), "mag", ("car", sc)], [K(3)])
                        d(SCAN(W4, magb, W1, ii), [K(1), "mag", ("car", sc)], [K(4)])
                        yield 1
                        xr_b, xi_b = xb(2 * s4), xb(2 * s4 + 1)
                        kxr, kxi = ("xb", 2 * s4), ("xb", 2 * s4 + 1)
                        d(TT(W0, c_, W3, ALU.mult), [K(3)] + ck, [K(0)])
                        d(TT(W2, s_, W4, ALU.mult), [K(4), "sinT"], [K(2)])
                        d(TT(xr_b, W0, W2, ALU.subtract), [K(0), K(2)], [kxr])
                        d(TT(carr_col(sc), W0[:, 511:512], W2[:, 511:512], ALU.subtract), [K(0), K(2)], [("car", sc)])
                        yield 1
                        d(TT(W5, s_, W3, ALU.mult), [K(3), "sinT"], [K(5)])
                        d(TT(W2, c_, W4, ALU.mult), [K(4)] + ck, [K(2)])
                        d(STT(xi_b, W5, -1.0, W2, ALU.mult, ALU.subtract), [K(5), K(2)], [kxi])
                        d(TT(cari_col(sc), W5[:, 511:512], W2[:, 511:512], ALU.add), [K(5), K(2)], [("car", sc)])
                        xbufs.append((sc, xr_b, xi_b, kxr, kxi))
                        yield 1
                    by = nb()
                    lst = []
                    reads = ["Cre", "Cim", "Dx", ("usT", h, tc)]
                    for i, (sc, xr_b, xi_b, kxr, kxi) in enumerate(xbufs):
                        lst.append((ps[by][:, :], Cre[:, sc, :], xr_b, i == 0, False))
                        lst.append((ps[by][:, :], Cim[:, sc, :], xi_b, False, False))
                        reads += [kxr, kxi]
                    lst.append((ps[by][:, :], Dx[:, h, :], usT[:, h, tl(tc)], False, True))
                    P.emit("pe", MM(lst), reads=reads, writes=[bk(by)])
                    P.emit("act", ACP(ysb, ps[by][:, :]), reads=[bk(by)], writes=["ysb"])
                    P.emit("dve", TT(yt, ysb, ysb, ALU.mult), reads=["ysb"], writes=["yt"])
                    P.emit("dve", TS(yt, yt, 0.044715, 1.0, ALU.mult, ALU.add), reads=["yt"], writes=["yt"])
                    P.emit("dve", TT(yt, yt, ysb, ALU.mult), reads=["yt", "ysb"], writes=["yt"])
                    P.emit("act", ACTF(ysig, yt, AF.Sigmoid, scale=1.5957691216057308), reads=["yt"], writes=["ysig"])
                    P.emit("dve", TT(yg[:, h, tl(tc)], ysb, ysig, ALU.mult), reads=["ysb", "ysig"], writes=[("yg", h, tc)])
            for c in range(2):
                for t in range(NT):
                    b_ = nb()
                    P.emit("pe", MM([(ps[b_][:, :], wglu[:, hh, c * 128:(c + 1) * 128], yg[:, hh, tl(t)], hh == 0, hh == 1) for hh in range(2)]),
                           reads=kglu + [("yg", 0, t), ("yg", 1, t)], writes=[bk(b_)])
                    P.emit("act", ACTF(ysig, ps[b_][:, :], AF.Sigmoid, bias=smv(V_BGLU + c)), reads=[bk(b_), "vec"], writes=["ysig"])
                    P.emit("dve", TT(ssmo[:, c, tl(t)], yg[:, c, tl(t)], ysig, ALU.mult), reads=["ysig", ("yg", c, t)],
                           writes=[("ssmo", c, t)])

        def load_x(b):
            for t in range(NT):
                for k in range(8):
                    P.emit("sp", DMA(hT[:, k, tl(t)], xT[b, k * 128:(k + 1) * 128, tl(t)]), writes=[("hT", k, t)], dma=True, arena=True)

        def mixer_q(b, q, kpa, wpa, kps, wps, kg, wga, wgs):
            for jj in range(2):
                j = q * 2 + jj
                jc = slice(j * 128, (j + 1) * 128)
                jjc = slice(jj * 128, (jj + 1) * 128)
                for t in range(NT):
                    b1, b2, b3, b4 = nb(), nb(), nb(), nb()
                    P.emit("pe", MM([(ps[b3][:, :], wga[:, k, jjc], uT[:, k, tl(t)], k == 0, k == 7) for k in range(8)]),
                           reads=kg + [("uT", k, t) for k in range(8)], writes=[bk(b3)])
                    P.emit("pe", MM([(ps[b4][:, :], wgs[:, k, jjc], uT[:, k, tl(t)], k == 0, k == 7) for k in range(8)]),
                           reads=kg + [("uT", k, t) for k in range(8)], writes=[bk(b4)])
                    P.emit("pe", MM([(ps[b1][:, :], wpa[:, k, jc], oatt[:, k, tl(t)], k == 0, k == 3) for k in range(4)]),
                           reads=kpa + [("oatt", k) for k in range(4)], writes=[bk(b1)])
                    P.emit("pe", MM([(ps[b2][:, :], wps[:, k, jc], ssmo[:, k, tl(t)], k == 0, k == 1) for k in range(2)]),
                           reads=kps + [("ssmo", 0, t), ("ssmo", 1, t)], writes=[bk(b2)])
                    sa, ss_, m1, m2 = (p3buf[:, i, 0:512] for i in range(4))
                    P.emit("act", ACTF(sa, ps[b3][:, :], AF.Sigmoid, bias=smv(V_BGATE + j)), reads=[bk(b3), "vec"], writes=["sa"])
                    P.emit("act", ACTF(ss_, ps[b4][:, :], AF.Sigmoid, bias=smv(V_BGATE + 8 + j)), reads=[bk(b4), "vec"], writes=["ss"])
                    P.emit("dve", TT(m1, sa, ps[b1][:, :], ALU.mult), reads=["sa", bk(b1)], writes=["m1"])
                    P.emit("dve", TT(m2, ss_, ps[b2][:, :], ALU.mult), reads=["ss", bk(b2)], writes=["m2"])
                    P.emit("dve", TT(mT[:, j, tl(t)], m1, m2, ALU.add), reads=["m1", "m2"], writes=[("mT", j, t)])

        def wout_half(b, half, kwo, wo):
            for jj in range(4):
                j = half * 4 + jj
                for t in range(NT):
                    b_ = nb()
                    P.emit("pe", MM([(ps[b_][:, :], wo[:, k, jj * 128:(jj + 1) * 128], mT[:, k, tl(t)], k == 0, k == 7) for k in range(8)]),
                           reads=kwo + [("mT", k, t) for k in range(8)], writes=[bk(b_)])
                    P.emit("dve", STT(hT[:, j, tl(t)], ps[b_][:, :], mod_col(16 + j, b), hT[:, j, tl(t)], ALU.mult, ALU.add),
                           reads=[bk(b_), "mod", ("hT", j, t)], writes=[("hT", j, t)])

        gT = mT

        def ffn_up(b, hf, fb, kwa, wa, kwv, wvv):
            for ff in range(4):
                f = hf * 8 + fb * 4 + ff
                fl = fb * 4 + ff
                fc = slice(ff * 128, (ff + 1) * 128)
                for t in range(NT):
                    ba, bv = nb(), nb()
                    P.emit("pe", MM([(ps[ba][:, :], wa[:, k, fc], uT[:, k, tl(t)], k == 0, k == 7) for k in range(8)]),
                           reads=kwa + [("uT", k, t) for k in range(8)], writes=[bk(ba)])
                    P.emit("pe", MM([(ps[bv][:, :], wvv[:, k, fc], uT[:, k, tl(t)], k == 0, k == 7) for k in range(8)]),
                           reads=kwv + [("uT", k, t) for k in range(8)], writes=[bk(bv)])
                    ab_ = p3buf[:, t % 2, :]
                    pvb = p3buf[:, (t + 1) % 2, :]
                    if t == 0:
                        P.emit("dve", MEMSET(ab_[:, 0:2], 0.0), writes=[("asbh", t % 2)])
                    else:
                        P.emit("act", ACP(ab_[:, 0:2], pvb[:, 512:514]), reads=[("asb", (t + 1) % 2)], writes=[("asbh", t % 2)])
                    P.emit("act", ACP(ab_[:, 2:514], ps[ba][:, :]), reads=[bk(ba)], writes=[("asb", t % 2)])
                    cv = p3buf[:, 2, 0:512]
                    sl = p3buf[:, 3, 0:512]
                    wc = lambda jtap, f=f: smv(V_WCONV + f * 3 + jtap)
                    rk = [("asb", t % 2), ("asbh", t % 2), "vec"]
                    P.emit("dve", TS(cv, ab_[:, 2:514], wc(0), smv(V_BCONV + f), ALU.mult, ALU.add), reads=rk, writes=["cv"])
                    P.emit("dve", STT(cv, ab_[:, 1:513], wc(1), cv, ALU.mult, ALU.add), reads=rk + ["cv"], writes=["cv"])
                    P.emit("dve", STT(cv, ab_[:, 0:512], wc(2), cv, ALU.mult, ALU.add), reads=rk + ["cv"], writes=["cv"])
                    P.emit("act", ACTF(sl, cv, AF.Silu), reads=["cv"], writes=["sl"])
                    P.emit("dve", TT(gT[:, fl, tl(t)], sl, ps[bv][:, :], ALU.mult), reads=["sl", bk(bv)], writes=[("mT", fl, t)])

        def ffn_down(b, hf, cb, kwd, wd):
            for jj in range(4):
                j = cb * 4 + jj
                for t in range(NT):
                    b_ = nb()
                    P.emit("pe", MM([(ps[b_][:, :], wd[:, k, jj * 128:(jj + 1) * 128], gT[:, k, tl(t)], k == 0, k == 7) for k in range(8)]),
                           reads=kwd + [("mT", k, t) for k in range(8)], writes=[bk(b_)])
                    P.emit("dve", STT(hT[:, j, tl(t)], ps[b_][:, :], mod_col(40 + j, b), hT[:, j, tl(t)], ALU.mult, ALU.add),
                           reads=[bk(b_), "mod", ("hT", j, t)], writes=[("hT", j, t)])

        def final_norm(b):
            for t in range(NT):
                sq = A_m[:, (t % 2) * 4096:(t % 2) * 4096 + 4096].rearrange("p (k c) -> p k c", k=8)
                rstd = Amf[:, 4096 + (t % 2) * 512: 4096 + (t % 2) * 512 + 512]
                P.emit("act", ACTF(sq, hT[:, :, tl(t)], AF.Square), reads=[("hT", k, t) for k in range(8)], writes=[("nsq", t % 2)])
                b_ = nb()
                P.emit("pe", MM([(ps[b_][:, :], onesb[:, :], sq[:, k, :], k == 0, k == 7) for k in range(8)]),
                       reads=[("nsq", t % 2), "ones"], writes=[bk(b_)])
                P.emit("act", ACTF(rstd, ps[b_][:, :], AF.Ln, bias=eps_col, scale=1.0 / D), reads=[bk(b_), "consts"], writes=[("nrstd", t % 2)])
                P.emit("act", ACTF(rstd, rstd, AF.Exp, scale=-0.5), reads=[("nrstd", t % 2)], writes=[("nrstd", t % 2)])
                for k in range(8):
                    si = (t * 8 + k) % 4
                    stg = Amf[:, 6144 + si * 512: 6144 + (si + 1) * 512]
                    P.emit("dve", STT(stg, hT[:, k, tl(t)], smv(V_GFIN + k), rstd, ALU.mult, ALU.mult),
                           reads=[("hT", k, t), ("nrstd", t % 2), "vec"], writes=[("ostg", si)])
                    P.emit("sp", DMA(outT[b, k * 128:(k + 1) * 128, tl(t)], stg),
                           reads=[("ostg", si)], writes=[("outT", b, k, t)], dma=True, arena=True)

        stages = []

        def gq_src(q):
            return [("ga", w_in[:, 1792 + q * 256: 1792 + (q + 1) * 256], 8, 256, 2 if q % 2 == 0 else 3, 0),
                    ("gs", w_in[:, 2816 + q * 256: 2816 + (q + 1) * 256], 8, 256, 2 if q % 2 == 0 else 3, 2048)]

        gen_state = {}

        for b in range(NB):
            def c_us(W, b=b):
                load_x(b)
                norm_mod(b, gs1_col, 0, uT)
                P.barrier()
                ssm_tables()
                load_E(0)
                us_proj(b, *W["us"])
                gen_state["gen"] = ssm_gen(b, *W["glu"])
            stages.append(([("us", w_in[:, 1536:1792], 8, 256, 2, 0), ("glu", w_glu[:, :], 2, 256, 1, 3072)], c_us))

            def tick(force=False):
                gen_state["n"] = gen_state.get("n", 0) + 1
                if not force and gen_state["n"] % 3 == 0:
                    return
                g = gen_state.get("gen")
                if g is not None:
                    try:
                        next(g)
                    except StopIteration:
                        gen_state["gen"] = None

            for hp in range(4):
                def c_att(W, b=b, hp=hp, tick=tick):
                    attention_hp(b, hp, *W["w"], tick)
                stages.append(([("w", w_in[:, hp * 384:(hp + 1) * 384], 8, 384, hp % 2, 0)], c_att))

            def c_tail(W, tick=tick):
                while gen_state.get("gen") is not None:
                    tick(force=True)
            stages.append(([], c_tail))
            for q in range(4):
                def c_mq(W, b=b, q=q, st_=state):
                    if q == 0:
                        st_["pa"] = W["pa"]
                        st_["ps"] = W["ps"]
                        P.barrier()
                        load_x(b)
                    kg = W["ga"][0] + W["gs"][0]
                    mixer_q(b, q, st_["pa"][0], st_["pa"][1], st_["ps"][0], st_["ps"][1], kg, W["ga"][1], W["gs"][1])
                lds = gq_src(q)
                if q == 0:
                    lds = [("pa", w_pa[:, :], 4, 1024, 0, 0), ("ps", w_ps[:, :], 2, 1024, 1, 0)] + lds
                stages.append((lds, c_mq))
            for half in range(2):
                def c_wo(W, b=b, half=half):
                    wout_half(b, half, *W["wo"])
                stages.append(([("wo", w_out[:, half * 512:(half + 1) * 512], 8, 512, 2 if half == 0 else 3, 0)], c_wo))
            fslots = [(0, 1), (2, 3), (0,), (1,), (2, 3), (0, 1), (2,), (3,)]
            fi = 0
            for hf in range(2):
                for fb in range(2):
                    c0 = hf * 1024 + fb * 512

                    def c_up(W, b=b, hf=hf, fb=fb):
                        if hf == 0 and fb == 0:
                            P.barrier()
                            norm_mod(b, gs2_col, 24, uT)
                        ffn_up(b, hf, fb, W["wa"][0], W["wa"][1], W["wv"][0], W["wv"][1])
                    stages.append(([("wa", w_up[:, c0:c0 + 512], 8, 512, fslots[fi][0], 0),
                                    ("wv", w_up[:, 2048 + c0:2048 + c0 + 512], 8, 512, fslots[fi][1], 0)], c_up))
                    fi += 1
                for cb in range(2):
                    def c_dn(W, b=b, hf=hf, cb=cb):
                        ffn_down(b, hf, cb, *W["wd"])
                        if hf == 1 and cb == 1:
                            P.barrier()
                            final_norm(b)
                    stages.append(([("wd", w_down[hf * 1024:(hf + 1) * 1024, cb * 512:(cb + 1) * 512], 8, 512, fslots[fi][0], 0)], c_dn))
                    fi += 1

        def do_loads(lds):
            return {nm: load_w(src, nk, ncols, slot=sl_, off=off) for (nm, src, nk, ncols, sl_, off) in lds}

        Wn = do_loads(stages[0][0])
        P.barrier()
        for i, (lds, comp) in enumerate(stages):
            Wc = Wn
            if i + 1 < len(stages):
                Wn = do_loads(stages[i + 1][0])
            comp(Wc)
        P.final_wait("sp")
        P.build()
    return nc


def _host_layouts(inp, NB):
    f32 = np.float32
    x = np.asarray(inp["x"], f32)
    c = np.asarray(inp["c"], f32)
    col8 = lambda v: np.ascontiguousarray(np.asarray(v, f32).reshape(-1, 128).T)
    w_in = np.asarray(inp["w_in"][0], f32)
    perm = []
    for hp in range(4):
        for sec in range(3):
            perm += list(range(sec * 512 + hp * 128, sec * 512 + (hp + 1) * 128))
    perm += list(range(1536, 3840))
    w_in_p = np.ascontiguousarray(w_in[:, perm])
    a_re, a_im, ldt = inp["a_re"][0], inp["a_im"][0], inp["log_dt"][0]
    st_major = lambda a: np.ascontiguousarray(np.asarray(a, f32).reshape(8, 128).T)
    ldt_l = np.ascontiguousarray(np.repeat(np.asarray(ldt, f32), 64).reshape(8, 128).T)
    wconv = np.asarray(inp["w_conv"][0], f32)
    wconv_l = np.ascontiguousarray(wconv.reshape(3, 16, 128).transpose(2, 1, 0).reshape(128, 48))
    vec = np.concatenate([
        col8(inp["g_mix"][0]), col8(inp["g_ffn"][0]), col8(inp["g_final"]), col8(inp["b_gate"][0]),
        col8(inp["b_glu"][0]), wconv_l, col8(inp["b_conv"][0]), st_major(a_re), st_major(a_im), ldt_l,
        col8(inp["b_ada"][0])], axis=1).astype(f32)
    assert vec.shape == (128, NV), vec.shape

    def expand_b(bmat):
        out = np.zeros((128, 8, 128), f32)
        for g in range(16):
            sc, gl = g // 2, g % 2
            c0 = 32 * (sc % 4) + 16 * gl
            out[gl * 64:(gl + 1) * 64, sc, c0:c0 + 16] = bmat[g]
        return out

    def expand_c(cmat):
        out = np.zeros((128, 8, 128), f32)
        for g in range(16):
            sc, gl = g // 2, g % 2
            c0 = 32 * (sc % 4) + 16 * gl
            out[gl * 64:(gl + 1) * 64, sc, c0:c0 + 16] = cmat[g].T
        return out

    d = np.asarray(inp["d_skip"][0], f32)
    d_x = np.zeros((128, 2, 128), f32)
    for h in range(2):
        d_x[np.arange(128), h, np.arange(128)] = d[h * 128:(h + 1) * 128]
    ak = np.arange(128, dtype=np.float64)[:, None]
    aq = np.arange(128, dtype=np.float64)[None, :]
    E = np.zeros((128, 48, 128), f32)
    for h in range(8):
        slope = 2.0 ** (-8.0 * (h + 1) / 8)
        for p, dil in enumerate((1, 4, 16)):
            cur = np.where(ak <= aq, np.exp(-slope * dil * (aq - ak) - 0.0), 0.0)
            prv = np.where(ak >= aq, np.exp(-slope * dil * (128 + aq - ak)), 0.0)
            E[:, (h * 3 + p) * 2 + 0, :] = cur
            E[:, (h * 3 + p) * 2 + 1, :] = prv
    common = dict(
        w_ada=np.ascontiguousarray(inp["w_ada"][0], f32), w_in_p=w_in_p, vec=vec,
        bre_x=expand_b(np.asarray(inp["b_re"][0], f32)), bim_x=expand_b(np.asarray(inp["b_im"][0], f32)),
        cre_x=expand_c(np.asarray(inp["c_re"][0], f32)), cim_x=expand_c(np.asarray(inp["c_im"][0], f32)),
        d_x=d_x, ident=np.eye(128, dtype=f32), iota=np.tile(np.arange(1, 513, dtype=f32)[None, :], (128, 1)), E=E,
        w_glu=np.ascontiguousarray(inp["w_glu"][0], f32), w_proj_att=np.ascontiguousarray(inp["w_proj_att"][0], f32),
        w_proj_ssm=np.ascontiguousarray(inp["w_proj_ssm"][0], f32), w_out=np.ascontiguousarray(inp["w_out"][0], f32),
        w_up=np.ascontiguousarray(inp["w_up"][0], f32), w_down=np.ascontiguousarray(inp["w_down"][0], f32))
    maps = []
    for core in range(NCORES):
        bs = slice(core * NB, (core + 1) * NB)
        m = dict(common)
        m["xT"] = np.ascontiguousarray(x[bs].transpose(0, 2, 1))
        m["cT"] = np.ascontiguousarray(c[bs].reshape(NB, 8, 128).transpose(2, 1, 0))
        maps.append(m)
    return maps


def kernel(**inputs):
    B = inputs["x"].shape[0]
    NB = B // NCORES
    maps = _host_layouts(inputs, NB)
    nc = build_nc(NB)
    res = run_bass_kernel_spmd(nc, maps, core_ids=list(range(NCORES)))
    outs = [np.asarray(r["outT"]).transpose(0, 2, 1) for r in res.results]
    return np.ascontiguousarray(np.concatenate(outs, axis=0).astype(np.float32))
```

```python
import math
from contextlib import ExitStack

import numpy as np
import concourse.bass as bass
import concourse.mybir as mybir
from concourse.bass_utils import run_bass_kernel_spmd

F32 = mybir.dt.float32
BF16 = mybir.dt.bfloat16
I32 = mybir.dt.int32
AF = mybir.ActivationFunctionType
ALU = mybir.AluOpType

NCORES = 8
D = 1024
S = 2048
NT = 4
EPS = 1e-6
SHIFT = 8.0
TWO_PI = 2.0 * math.pi
PI_LO = 3.1415925
ENGS = ("pe", "act", "dve", "pool", "sp")

V_GMIX, V_GFFN, V_GFIN, V_BGATE, V_BGLU, V_WCONV, V_BCONV, V_ARE, V_AIM, V_LDT, V_BADA = (
    0, 8, 16, 24, 40, 42, 90, 106, 114, 122, 130)
NV = 178


class Prog:
    def __init__(self, nc, stack, n_dma_sems=8):
        self.nc = nc
        self.ops = {e: [] for e in ENGS}
        self.cnt = {e: 0 for e in ENGS}
        self.semobj = {}
        for e in ENGS:
            self.semobj[("c", e)] = stack.enter_context(nc.semaphore("c_" + e))
        self.dval = {q: [0] * n_dma_sems for q in ("sp", "pool")}
        self.dnext = {q: 0 for q in ("sp", "pool")}
        for q in ("sp", "pool"):
            for i in range(n_dma_sems):
                self.semobj[("d", q, i)] = stack.enter_context(nc.semaphore("d_%s%d" % (q, i)))
        self.waited = {e: {} for e in ENGS}
        self.regions = {}
        self.fence = []
        self.arena_toks = {}

    def _need(self, eng, waits, tok):
        if tok is None:
            return
        sid, val = tok
        if self.waited[eng].get(sid, 0) >= val:
            return
        if waits.get(sid, 0) < val:
            waits[sid] = val

    def emit(self, eng, fn, reads=(), writes=(), dma=False, arena=False):
        waits = {}
        own = ("c", eng)
        if dma and arena:
            for t in self.fence:
                self._need(eng, waits, t)
        for key in reads:
            r = self.regions.get(key)
            if r is not None:
                self._need(eng, waits, r["w"])
        for key in writes:
            r = self.regions.get(key)
            if r is not None:
                w = r["w"]
                if w is not None and not (w[0] == own and not dma):
                    self._need(eng, waits, w)
                for sid, val in r["r"].items():
                    if sid == own and not dma:
                        continue
                    self._need(eng, waits, (sid, val))
        if dma:
            i = self.dnext[eng]
            self.dnext[eng] = (i + 1) % len(self.dval[eng])
            sid = ("d", eng, i)
            if self.dval[eng][i] > 0:
                self._need(eng, waits, (sid, self.dval[eng][i]))
            self.dval[eng][i] += 16
            tok = (sid, self.dval[eng][i])
            amt = 16
            if arena:
                self.arena_toks[sid] = tok[1]
        else:
            self.cnt[eng] += 1
            tok = (own, self.cnt[eng])
            amt = 1
        for sid, val in waits.items():
            self.waited[eng][sid] = val
        for key in reads:
            r = self.regions.setdefault(key, {"w": None, "r": {}})
            if r["r"].get(tok[0], 0) < tok[1]:
                r["r"][tok[0]] = tok[1]
        for key in writes:
            self.regions[key] = {"w": tok, "r": {}}
        self.ops[eng].append((list(waits.items()), fn, tok, amt))
        return tok

    def _all_toks(self):
        toks = []
        for e in ENGS:
            if self.cnt[e] > 0:
                toks.append((("c", e), self.cnt[e]))
        for q in ("sp", "pool"):
            for i, v in enumerate(self.dval[q]):
                if v > 0:
                    toks.append((("d", q, i), v))
        return toks

    def barrier(self):
        toks = self._all_toks()
        for e in ENGS:
            waits = {}
            for t in toks:
                if t[0] == ("c", e):
                    continue
                self._need(e, waits, t)
            for sid, val in waits.items():
                self.waited[e][sid] = val
            if waits:
                self.ops[e].append((list(waits.items()), None, None, 0))

    def final_wait(self, eng="sp"):
        waits = {}
        for t in self._all_toks():
            if t[0] == ("c", eng):
                continue
            self._need(eng, waits, t)
        self.ops[eng].append((list(waits.items()), None, None, 0))

    def build(self):
        nc = self.nc
        with nc.Block() as block:
            def mk(e):
                def body(h):
                    for waits, fn, tok, amt in self.ops[e]:
                        for sid, val in waits:
                            h.wait_ge(self.semobj[sid], val)
                        if fn is None:
                            continue
                        ins = fn(h)
                        ins.then_inc(self.semobj[tok[0]], amt)
                return body
            block.tensor(mk("pe"))
            block.scalar(mk("act"))
            block.vector(mk("dve"))
            block.gpsimd(mk("pool"))
            block.sync(mk("sp"))


def MM(lst):
    def fn(e):
        ins = None
        for (out, lhsT, rhs, st, sp) in lst:
            ins = e.matmul(out, lhsT=lhsT, rhs=rhs, start=st, stop=sp)
        return ins
    return fn


def ACTF(out, in_, func, bias=None, scale=None):
    def fn(e):
        kw = {}
        if bias is not None:
            kw["bias"] = bias
        if scale is not None:
            kw["scale"] = scale
        return e.activation(out=out, in_=in_, func=func, **kw)
    return fn


def TT(out, in0, in1, op):
    return lambda e: e.tensor_tensor(out=out, in0=in0, in1=in1, op=op)


def TS(out, in0, s1, s2, op0, op1=None):
    if op1 is None:
        return lambda e: e.tensor_scalar(out=out, in0=in0, scalar1=s1, scalar2=None, op0=op0)
    return lambda e: e.tensor_scalar(out=out, in0=in0, scalar1=s1, scalar2=s2, op0=op0, op1=op1)


def STT(out, in0, scalar, in1, op0, op1):
    return lambda e: e.scalar_tensor_tensor(out=out, in0=in0, scalar=scalar, in1=in1, op0=op0, op1=op1)


def CP(out, in_):
    return lambda e: e.tensor_copy(out=out, in_=in_)


def ACP(out, in_):
    return lambda e: e.activation(out=out, in_=in_, func=AF.Copy)


def DMA(out, in_):
    return lambda e: e.dma_start(out=out, in_=in_)


def SCAN(out, d0, d1, init):
    return lambda e: e.tensor_tensor_scan(out=out, data0=d0, data1=d1, initial=init, op0=ALU.mult, op1=ALU.add)


def RECIP(out, in_):
    return lambda e: e.reciprocal(out=out, in_=in_)


def MEMSET(ap, v):
    return lambda e: e.memset(ap, v)


def TRANSP(out, in_, ident):
    return lambda e: e.transpose(out=out, in_=in_, identity=ident)


def xap(base, start, dims):
    return bass.AP(base.tensor, base.offset + start, [list(base.ap[0])] + [list(d) for d in dims])


def build_nc(NB):
    nc = bass.Bass("TRN2", target_bir_lowering=False)

    def din(name, shape, dtype=F32):
        return nc.dram_tensor(name, list(shape), dtype, kind="ExternalInput").ap()

    xT = din("xT", [NB, D, S])
    cT = din("cT", [128, 8, NB])
    w_ada = din("w_ada", [D, 6 * D])
    w_in = din("w_in_p", [D, 3840])
    vec = din("vec", [128, NV])
    bre_x = din("bre_x", [128, 8, 128])
    bim_x = din("bim_x", [128, 8, 128])
    cre_x = din("cre_x", [128, 8, 128])
    cim_x = din("cim_x", [128, 8, 128])
    d_x = din("d_x", [128, 2, 128])
    ident_d = din("ident", [128, 128])
    iota_d = din("iota", [128, 512])
    E_d = din("E", [128, 48, 128])
    w_glu = din("w_glu", [256, 256])
    w_pa = din("w_proj_att", [512, D])
    w_ps = din("w_proj_ssm", [256, D])
    w_out = din("w_out", [D, D])
    w_up = din("w_up", [D, 4096])
    w_down = din("w_down", [2048, D])
    outT = nc.dram_tensor("outT", [NB, D, S], F32, kind="ExternalOutput").ap()
    trig_scr = nc.dram_tensor("trig_scr", [128, 2, 8, 512], F32).ap()
    btab_scr = nc.dram_tensor("btab_scr", [128, 8, 2, 128], BF16).ap()

    with ExitStack() as st:
        P = Prog(nc, st)
        sb = lambda name, shape, dt: st.enter_context(nc.sbuf_tensor(name, list(shape), dt))
        A_h = sb("A_h", [128, 16384], F32)
        A_u = sb("A_u", [128, 16384], BF16)
        A_m = sb("A_m", [128, 16384], BF16)
        oatt = sb("oatt", [128, 4, S], BF16)
        ssmo = sb("ssmo", [128, 2, S], BF16)
        usT = sb("usT", [128, 2, S], BF16)
        NSLOT = 4
        slots = [sb("slot%d" % i, [128, 4096], BF16) for i in range(NSLOT)]
        p3buf = sb("p3buf", [128, 4, 514], F32)
        sm = sb("sm", [128, 768], F32)
        csb = sb("csb", [128, 8, NB], BF16)
        onesb = sb("onesb", [128, 128], BF16)
        ps = [st.enter_context(nc.psum_tensor("ps%d" % i, [128, 512], F32)) for i in range(8)]

        hT = A_h[:, :].rearrange("p (k t) -> p k t", k=8)
        uT = A_u[:, :].rearrange("p (k t) -> p k t", k=8)
        mT = A_m[:, :].rearrange("p (k t) -> p k t", k=8)
        Ahb = A_h[:, :].bitcast(BF16)
        Amf = A_m[:, :].bitcast(F32)

        SM_VEC = 0
        SM_MOD = 192
        SM_GS1 = SM_MOD + 48 * NB
        SM_GS2 = SM_GS1 + 8 * NB
        SM_SSM = SM_GS2 + 8 * NB
        assert SM_SSM + 26 <= 768
        eps_col = sm[:, SM_SSM + 24: SM_SSM + 25]
        nshift_col = sm[:, SM_SSM + 25: SM_SSM + 26]
        smv = lambda off, n=1: sm[:, off:off + n]
        mod_col = lambda chunk, b: sm[:, SM_MOD + chunk * NB + b: SM_MOD + chunk * NB + b + 1]
        gs1_col = lambda k, b: sm[:, SM_GS1 + k * NB + b: SM_GS1 + k * NB + b + 1]
        gs2_col = lambda k, b: sm[:, SM_GS2 + k * NB + b: SM_GS2 + k * NB + b + 1]
        mag_col = lambda sc: sm[:, SM_SSM + sc: SM_SSM + sc + 1]
        carr_col = lambda sc: sm[:, SM_SSM + 8 + sc: SM_SSM + 9 + sc]
        cari_col = lambda sc: sm[:, SM_SSM + 16 + sc: SM_SSM + 17 + sc]

        state = {"bank": 0, "slot": 0, "ev": 0}

        def nb():
            i = state["bank"]
            state["bank"] = (i + 1) % 6
            return i

        def bk(i):
            return ("ps", i)

        def ev():
            state["ev"] ^= 1
            return "act" if state["ev"] else "dve"

        def evac(out, in_, reads, writes):
            P.emit("act", ACP(out, in_), reads=reads, writes=writes)

        def load_w(src2d, nk, ncols, slot=None, off=0):
            if slot is None:
                i = state["slot"]
                state["slot"] = (i + 1) % NSLOT
            else:
                i = slot
            n = nk * ncols
            keys = [("slot", i, h) for h in range(2) if off < (h + 1) * 2048 and off + n > h * 2048]
            view = slots[i][:, off:off + n].rearrange("p (k c) -> p k c", k=nk)
            P.emit("pool", DMA(view, src2d.rearrange("(k p) c -> p k c", p=128)), writes=keys, dma=True)
            return keys, view

        tl = lambda t: slice(t * 512, (t + 1) * 512)

        P.emit("sp", DMA(sm[:, 0:NV], vec[:, :]), writes=["vec"], dma=True, arena=True)
        P.emit("dve", MEMSET(onesb[:, :], 1.0), writes=["ones"])
        P.emit("dve", MEMSET(eps_col, EPS), writes=["consts"])
        P.emit("dve", MEMSET(nshift_col, -SHIFT), writes=["consts"])
        ctmp = A_h[:, 0:8 * NB].rearrange("p (k b) -> p k b", k=8)
        P.emit("sp", DMA(ctmp, cT[:, :, :]), writes=["ctmp"], dma=True, arena=True)
        P.emit("act", ACTF(csb[:, :, :], ctmp, AF.Silu), reads=["ctmp"], writes=["csb"])
        for jb in range(12):
            key, wv = load_w(w_ada[:, jb * 512:(jb + 1) * 512], 8, 512)
            for jj in range(4):
                j = jb * 4 + jj
                b_ = nb()
                P.emit("pe", MM([(ps[b_][:, 0:NB], wv[:, k, jj * 128:(jj + 1) * 128], csb[:, k, :], k == 0, k == 7)
                                 for k in range(8)]), reads=key + ["csb"], writes=[bk(b_)])
                P.emit("dve", TS(sm[:, SM_MOD + j * NB: SM_MOD + (j + 1) * NB], ps[b_][:, 0:NB],
                                 smv(V_BADA + j), None, ALU.add), reads=[bk(b_), "vec"], writes=["mod"])
        for k in range(8):
            P.emit("dve", TS(sm[:, SM_GS1 + k * NB: SM_GS1 + (k + 1) * NB],
                             sm[:, SM_MOD + (8 + k) * NB: SM_MOD + (9 + k) * NB], 1.0, smv(V_GMIX + k), ALU.add, ALU.mult),
                   reads=["mod", "vec"], writes=["gs"])
            P.emit("dve", TS(sm[:, SM_GS2 + k * NB: SM_GS2 + (k + 1) * NB],
                             sm[:, SM_MOD + (32 + k) * NB: SM_MOD + (33 + k) * NB], 1.0, smv(V_GFFN + k), ALU.add, ALU.mult),
                   reads=["mod", "vec"], writes=["gs"])

        pt = lambda i: A_h[:, 512 + 8 * i: 512 + 8 * (i + 1)]
        Xre = A_h[:, 1024:2048].rearrange("p (s c) -> p s c", s=8)
        Xim = A_h[:, 2048:3072].rearrange("p (s c) -> p s c", s=8)
        bre = A_h[:, 3072:4096].rearrange("p (s c) -> p s c", s=8)
        bim = A_h[:, 4096:5120].rearrange("p (s c) -> p s c", s=8)
        iota = A_h[:, 5120:5632]
        ident = A_h[:, 5632:5760]
        P.emit("sp", DMA(bre, bre_x[:, :, :]), writes=["bre"], dma=True, arena=True)
        P.emit("sp", DMA(bim, bim_x[:, :, :]), writes=["bim"], dma=True, arena=True)
        P.emit("sp", DMA(iota, iota_d[:, :]), writes=["iota"], dma=True, arena=True)
        P.emit("sp", DMA(ident, ident_d[:, :]), writes=["ident"], dma=True, arena=True)
        lr, li, ldt = smv(V_ARE, 8), smv(V_AIM, 8), smv(V_LDT, 8)
        T = {}

        def pe_(name, eng, fn, reads):
            P.emit(eng, fn, reads=reads + ["vec"], writes=[name])

        names = ["dt", "lrdt", "th", "ki", "kf", "thr", "sin", "ab", "cos", "abr", "abi", "nr", "t1", "t2", "den",
                 "rden", "u1", "u2", "fre", "fim", "nfim", "u3", "u4"]
        for i, n in enumerate(names):
            T[n] = pt(i)
        T["ki"] = pt(names.index("ki")).bitcast(I32)
        pe_("dt", "act", ACTF(T["dt"], ldt, AF.Exp), [])
        pe_("lrdt", "dve", TT(T["lrdt"], lr, T["dt"], ALU.mult), ["dt"])
        pe_("mag", "act", ACTF(smv(SM_SSM, 8), T["lrdt"], AF.Exp), ["lrdt"])
        pe_("th", "dve", TT(T["th"], li, T["dt"], ALU.mult), ["dt"])
        pe_("ki", "dve", TS(T["ki"], T["th"], 1.0 / TWO_PI, None, ALU.mult), ["th"])
        pe_("kf", "dve", CP(T["kf"], T["ki"]), ["ki"])
        pe_("thr", "dve", STT(T["thr"], T["kf"], -TWO_PI, T["th"], ALU.mult, ALU.add), ["kf", "th"])
        pe_("thr", "dve", TS(T["thr"], T["thr"], PI_LO, -PI_LO, ALU.min, ALU.max), ["thr"])
        pe_("sin", "act", ACTF(T["sin"], T["thr"], AF.Sin), ["thr"])
        pe_("ab", "act", ACTF(T["ab"], T["thr"], AF.Abs), ["thr"])
        pe_("ab2", "dve", TS(T["u4"], T["ab"], -1.0, math.pi / 2, ALU.mult, ALU.add), ["ab"])
        pe_("cos", "act", ACTF(T["cos"], T["u4"], AF.Sin), ["ab2"])
        pe_("abr", "dve", TT(T["abr"], smv(SM_SSM, 8), T["cos"], ALU.mult), ["mag", "cos"])
        pe_("abi", "dve", TT(T["abi"], smv(SM_SSM, 8), T["sin"], ALU.mult), ["mag", "sin"])
        pe_("nr", "dve", TS(T["nr"], T["abr"], -1.0, None, ALU.add), ["abr"])
        pe_("t1", "dve", TT(T["t1"], lr, lr, ALU.mult), [])
        pe_("t2", "dve", TT(T["t2"], li, li, ALU.mult), [])
        pe_("den", "dve", TT(T["den"], T["t1"], T["t2"], ALU.add), ["t1", "t2"])
        pe_("rden", "dve", RECIP(T["rden"], T["den"]), ["den"])
        pe_("u1", "dve", TT(T["u1"], T["nr"], lr, ALU.mult), ["nr"])
        pe_("u2", "dve", TT(T["u2"], T["abi"], li, ALU.mult), ["abi"])
        pe_("u3", "dve", TT(T["u3"], T["u1"], T["u2"], ALU.add), ["u1", "u2"])
        pe_("fre", "dve", TT(T["fre"], T["u3"], T["rden"], ALU.mult), ["u3", "rden"])
        pe_("u1b", "dve", TT(T["u1"], T["abi"], lr, ALU.mult), ["abi", "u3"])
        pe_("u2b", "dve", TT(T["u2"], T["nr"], li, ALU.mult), ["nr", "u3"])
        pe_("u3b", "dve", TT(T["u3"], T["u1"], T["u2"], ALU.subtract), ["u1b", "u2b", "fre"])
        pe_("fim", "dve", TT(T["fim"], T["u3"], T["rden"], ALU.mult), ["u3b", "rden"])
        pe_("nfim", "dve", TS(T["nfim"], T["fim"], -1.0, None, ALU.mult), ["fim"])
        btab = Ahb[:, 2 * 13312: 2 * 13312 + 2048].rearrange("p (s r c) -> p s r c", s=8, r=2)
        for sc in range(8):
            c1 = lambda t, sc=sc: t[:, sc:sc + 1]
            P.emit("dve", TS(Xre[:, sc, :], bre[:, sc, :], c1(T["fre"]), None, ALU.mult), reads=["bre", "fre"], writes=[("Xre", sc)])
            P.emit("dve", STT(Xre[:, sc, :], bim[:, sc, :], c1(T["nfim"]), Xre[:, sc, :], ALU.mult, ALU.add),
                   reads=["bim", "nfim", ("Xre", sc)], writes=[("Xre", sc)])
            P.emit("dve", TS(Xim[:, sc, :], bre[:, sc, :], c1(T["fim"]), None, ALU.mult), reads=["bre", "fim"], writes=[("Xim", sc)])
            P.emit("dve", STT(Xim[:, sc, :], bim[:, sc, :], c1(T["fre"]), Xim[:, sc, :], ALU.mult, ALU.add),
                   reads=["bim", "fre", ("Xim", sc)], writes=[("Xim", sc)])
            for ri, X in enumerate((Xre, Xim)):
                b_ = nb()
                P.emit("pe", TRANSP(ps[b_][:, 0:128], X[:, sc, :], ident), reads=[("Xre" if ri == 0 else "Xim", sc), "ident"],
                       writes=[bk(b_)])
                evac(btab[:, sc, ri, :], ps[b_][:, 0:128], [bk(b_)], [("btab", sc, ri)])
        P.emit("sp", DMA(btab_scr[:, :, :, :], btab), reads=[("btab", sc, ri) for sc in range(8) for ri in range(2)],
               writes=["btab_scr"], dma=True, arena=True)
        for sc in range(8):
            base = 6144 + (sc % 2) * 3072
            arg = A_h[:, base:base + 512]
            ki = A_h[:, base + 512:base + 1024].bitcast(I32)
            kf = A_h[:, base + 1024:base + 1536]
            so = A_h[:, base + 1536:base + 2048]
            ab = A_h[:, base + 2048:base + 2560]
            co = A_h[:, base + 2560:base + 3072]
            kk = lambda n, sc=sc: ("tg", n, sc % 2)
            P.emit("dve", TS(arg, iota, T["thr"][:, sc:sc + 1], None, ALU.mult), reads=["iota", "thr"], writes=[kk("arg")])
            P.emit("dve", TS(ki, arg, 1.0 / TWO_PI, None, ALU.mult), reads=[kk("arg")], writes=[kk("ki")])
            P.emit("dve", CP(kf, ki), reads=[kk("ki")], writes=[kk("kf")])
            P.emit("dve", STT(arg, kf, -TWO_PI, arg, ALU.mult, ALU.add), reads=[kk("kf"), kk("arg")], writes=[kk("arg")])
            P.emit("dve", TS(arg, arg, PI_LO, -PI_LO, ALU.min, ALU.max), reads=[kk("arg")], writes=[kk("arg")])
            P.emit("act", ACTF(so, arg, AF.Sin), reads=[kk("arg")], writes=[kk("so")])
            P.emit("act", ACTF(ab, arg, AF.Abs), reads=[kk("arg")], writes=[kk("ab")])
            P.emit("dve", TS(ab, ab, -1.0, math.pi / 2, ALU.mult, ALU.add), reads=[kk("ab")], writes=[kk("ab")])
            P.emit("act", ACTF(co, ab, AF.Sin), reads=[kk("ab")], writes=[kk("co")])
            P.emit("sp", DMA(trig_scr[:, 0, sc, :], co), reads=[kk("co")], writes=[("trig_scr", 0, sc)], dma=True, arena=True)
            P.emit("sp", DMA(trig_scr[:, 1, sc, :], so), reads=[kk("so")], writes=[("trig_scr", 1, sc)], dma=True, arena=True)

        def mtk(chunks):
            return [("mT", c, tt) for c in chunks for tt in range(NT)]

        def norm_mod(b, gs_col, sh_chunk0, dst):
            for t in range(NT):
                sq = A_m[:, (t % 2) * 4096:(t % 2) * 4096 + 4096].rearrange("p (k c) -> p k c", k=8)
                rstd = Amf[:, 4096 + (t % 2) * 512: 4096 + (t % 2) * 512 + 512]
                sqk = mtk((2 * (t % 2), 2 * (t % 2) + 1))
                P.emit("act", ACTF(sq, hT[:, :, tl(t)], AF.Square), reads=[("hT", k, t) for k in range(8)],
                       writes=[("nsq", t % 2)] + sqk)
                b_ = nb()
                P.emit("pe", MM([(ps[b_][:, :], onesb[:, :], sq[:, k, :], k == 0, k == 7) for k in range(8)]),
                       reads=[("nsq", t % 2), "ones"] + sqk, writes=[bk(b_)])
                P.emit("act", ACTF(rstd, ps[b_][:, :], AF.Ln, bias=eps_col, scale=1.0 / D),
                       reads=[bk(b_), "consts"], writes=[("nrstd", t % 2)] + mtk((4,)))
                P.emit("act", ACTF(rstd, rstd, AF.Exp, scale=-0.5), reads=[("nrstd", t % 2)], writes=[("nrstd", t % 2)] + mtk((4,)))
                for k in range(8):
                    tmp = Amf[:, 5120 + (k % 2) * 512: 5120 + (k % 2) * 512 + 512]
                    P.emit("dve", TT(tmp, hT[:, k, tl(t)], rstd, ALU.mult), reads=[("hT", k, t), ("nrstd", t % 2)] + mtk((4,)),
                           writes=[("ntmp", k % 2)] + mtk((5,)))
                    P.emit("act", ACTF(dst[:, k, tl(t)], tmp, AF.Identity, bias=mod_col(sh_chunk0 + k, b), scale=gs_col(k, b)),
                           reads=[("ntmp", k % 2), "mod", "gs"] + mtk((5,)), writes=[("uT", k, t)])

        Etabs = [Ahb[:, i * 1536:(i + 1) * 1536].rearrange("p (i q) -> p i q", i=12) for i in range(2)]
        qT = Ahb[:, 3072:5120]
        kT = Ahb[:, 5120:7168]
        Vt = [Ahb[:, 7168 + i * 2048: 7168 + (i + 1) * 2048].rearrange("p (b c) -> p b c", b=16) for i in range(3)]
        Pb = [Ahb[:, 13312 + i * 512: 13312 + (i + 1) * 512] for i in range(8)]
        acc_o = A_h[:, 8704:10752]
        acc_d = A_h[:, 10752:12800]

        def load_E(hp):
            P.emit("pool", DMA(Etabs[hp % 2], E_d[:, hp * 12:(hp + 1) * 12, :]), writes=[("Etab", hp % 2)], dma=True, arena=True)

        def tokset(pat, blk):
            if pat == 0:
                return 128 * blk, 1
            if pat == 1:
                return 512 * (blk % 4) + blk // 4, 4
            return blk, 16

        def cs_(ap2d, pat, blk):
            base, stp = tokset(pat, blk)
            return ap2d[:, base: base + 127 * stp + 1: stp]

        def acc_view(acc, pat, grp):
            if pat == 0:
                return acc[:, 512 * grp: 512 * grp + 512].rearrange("p (j a) -> p j a", j=4)
            if pat == 1:
                return xap(acc[:, 0:1], 512 * grp, [[1, 4], [4, 128]])
            return xap(acc[:, 0:1], 4 * grp, [[1, 4], [16, 128]])

        pbi = [0]

        def attention_hp(b, hp, key, wv, tick):
            if hp < 3:
                load_E(hp + 1)
            for which, dst, nm in ((0, qT, "qT"), (1, kT, "kT")):
                for t in range(NT):
                    b_ = nb()
                    P.emit("pe", MM([(ps[b_][:, :], wv[:, k, which * 128:(which + 1) * 128], uT[:, k, tl(t)], k == 0, k == 7)
                                     for k in range(8)]), reads=key + [("uT", k, t) for k in range(8)], writes=[bk(b_)])
                    evac(dst[:, tl(t)], ps[b_][:, :], [bk(b_)], [(nm, t)])
                    tick(force=True)
            for pat in range(3):
                for b4 in range(4):
                    b_ = nb()
                    lst = []
                    for jj in range(4):
                        blk = b4 * 4 + jj
                        for k in range(8):
                            lst.append((ps[b_][:, jj * 128:(jj + 1) * 128], cs_(uT[:, k, :], pat, blk), wv[:, k, 256:384],
                                        k == 0, k == 7))
                    P.emit("pe", MM(lst), reads=key + [("uT", k, t) for k in range(8) for t in range(NT)], writes=[bk(b_)])
                    evac(Vt[pat][:, b4 * 4:(b4 + 1) * 4, :], ps[b_][:, :].rearrange("p (j c) -> p j c", j=4), [bk(b_)],
                         [("V", pat, b4)])
                    tick(force=True)
            units = [(pat, grp, hd) for pat in range(3) for grp in range(4) for hd in range(2)]
            pend = None
            obank = {}

            def emit_qk(u):
                pat, grp, hd = u
                h = hp * 2 + hd
                rows = slice(hd * 64, hd * 64 + 64)
                info = {"cur": None, "prev": None, "mask": []}
                for typ in ("cur", "prev"):
                    lst = []
                    js = []
                    for j in range(4):
                        if pat == 0:
                            qb = 4 * grp + j
                            kb = qb if typ == "cur" else qb - 1
                            ok = kb >= 0
                        elif pat == 1:
                            qb = j * 4 + grp
                            kb = qb if typ == "cur" else qb - 1
                            ok = (typ == "cur") or grp >= 1
                        else:
                            qb = 4 * grp + j
                            kb = qb
                            ok = typ == "cur"
                        if ok:
                            js.append((j, qb, kb))
                    if not js:
                        continue
                    b_ = nb()
                    for (j, qb, kb) in js:
                        lst.append((ps[b_][:, j * 128:(j + 1) * 128], cs_(kT[rows, :], pat, kb), cs_(qT[rows, :], pat, qb), True, True))
                    P.emit("pe", MM(lst), reads=[("qT", t) for t in range(NT)] + [("kT", t) for t in range(NT)], writes=[bk(b_)])
                    j0 = js[0][0]
                    pi = pbi[0]
                    pbi[0] = (pi + 1) % 8
                    pv = Pb[pi][:, j0 * 128:512]
                    P.emit("act", ACTF(pv, ps[b_][:, j0 * 128:512], AF.Exp, bias=nshift_col, scale=0.125),
                           reads=[bk(b_), "consts"], writes=[("Pb", pi)])
                    ei = (hd * 3 + pat) * 2 + (0 if typ == "cur" else 1)
                    nj = 4 - j0
                    ebc = xap(Etabs[hp % 2][:, ei, 0:1], 0, [[0, nj], [1, 128]])
                    pv3 = pv.rearrange("p (j a) -> p j a", j=nj)
                    info["mask"].append((pv3, ebc, pi))
                    info[typ] = (pi, js)
                return info

            def emit_mask(info):
                for (pv3, ebc, pi) in info["mask"]:
                    P.emit("dve", TT(pv3, pv3, ebc, ALU.mult), reads=[("Pb", pi), ("Etab", hp % 2)], writes=[("Pb", pi)])

            def emit_pv(u, info):
                pat, grp, hd = u
                rows = slice(hd * 64, hd * 64 + 64)
                if hd == 0:
                    obank[(pat, grp)] = (6, 7)
                bo, bd = obank[(pat, grp)]
                lst = []
                reads = [("V", pat, b4) for b4 in range(4)] + ["ones"]
                pcur, jcur = info["cur"]
                reads.append(("Pb", pcur))
                prevmap = {}
                if info["prev"] is not None:
                    pprev, jprev = info["prev"]
                    reads.append(("Pb", pprev))
                    prevmap = {j: kb for (j, qb, kb) in jprev}
                for (j, qb, kb) in jcur:
                    oc = slice(j * 128, (j + 1) * 128)
                    hasp = j in prevmap
                    for dst, isden in ((ps[bo], False), (ps[bd], True)):
                        if hasp:
                            lw = onesb[:, 0:64] if isden else Vt[pat][:, prevmap[j], rows]
                            lst.append((dst[rows, oc], lw, Pb[pprev][:, oc], True, False))
                        lw = onesb[:, 0:64] if isden else Vt[pat][:, kb, rows]
                        lst.append((dst[rows, oc], lw, Pb[pcur][:, oc], not hasp, True))
                P.emit("pe", MM(lst), reads=reads, writes=[bk(bo), bk(bd)])
                if hd == 1:
                    gk = [grp] if pat < 2 else [0, 1, 2, 3]
                    for acc, bb, nm in ((acc_o, bo, "acc_o"), (acc_d, bd, "acc_d")):
                        av = acc_view(acc, pat, grp)
                        src = ps[bb][:, :].rearrange("p (j a) -> p j a", j=4)
                        if pat == 0:
                            evac(av, src, [bk(bb)], [(nm, g) for g in gk])
                        else:
                            P.emit("dve", TT(av, av, src, ALU.add), reads=[bk(bb)] + [(nm, g) for g in gk],
                                   writes=[(nm, g) for g in gk])

            infos = []
            for ui, u in enumerate(units):
                infos.append(emit_qk(u))
                tick()
                if ui >= 1:
                    emit_mask(infos[ui - 1])
                if ui >= 2:
                    emit_pv(units[ui - 2], infos[ui - 2])
                tick()
            nU = len(units)
            emit_mask(infos[nU - 1])
            emit_pv(units[nU - 2], infos[nU - 2])
            emit_pv(units[nU - 1], infos[nU - 1])
            P.emit("act", ACTF(acc_d, acc_d, AF.Ln), reads=[("acc_d", g) for g in range(4)], writes=[("acc_d", g) for g in range(4)])
            P.emit("act", ACTF(acc_d, acc_d, AF.Exp, scale=-1.0), reads=[("acc_d", g) for g in range(4)], writes=[("acc_d", g) for g in range(4)])
            P.emit("dve", TT(oatt[:, hp, :], acc_o, acc_d, ALU.mult), reads=[("acc_o", g) for g in range(4)] + [("acc_d", g) for g in range(4)],
                   writes=[("oatt", hp)])

        cosT = slots[3][:, :].bitcast(F32).rearrange("p (s c) -> p s c", s=4)
        sinT = p3buf[:, :, 0:512]
        wk = lambda i: A_h[:, 12800 + i * 512: 12800 + (i + 1) * 512]
        xb = lambda i: A_m[:, 12288 + i * 512: 12288 + (i + 1) * 512]
        Btab = A_m[:, 0:2048].rearrange("p (s r c) -> p s r c", s=8, r=2)
        Cre = A_m[:, 2048:3072].rearrange("p (s c) -> p s c", s=8)
        Cim = A_m[:, 3072:4096].rearrange("p (s c) -> p s c", s=8)
        Dx = A_m[:, 4096:4352].rearrange("p (s c) -> p s c", s=2)
        yg = A_m[:, 4352:4352 + 4096].rearrange("p (h t) -> p h t", h=2)
        ysb = Amf[:, 4608:5120]
        yt = Amf[:, 5120:5632]
        ysig = Amf[:, 5632:6144]

        def ssm_tables():
            P.emit("sp", DMA(Btab, btab_scr[:, :, :, :]), reads=["btab_scr"], writes=["Btab"], dma=True, arena=True)
            P.emit("pool", DMA(Cre, cre_x[:, :, :]), writes=["Cre"], dma=True, arena=True)
            P.emit("pool", DMA(Cim, cim_x[:, :, :]), writes=["Cim"], dma=True, arena=True)
            P.emit("pool", DMA(Dx, d_x[:, :, :]), writes=["Dx"], dma=True, arena=True)

        def us_proj(b, key, wv):
            for c in range(2):
                for t in range(NT):
                    b_ = nb()
                    P.emit("pe", MM([(ps[b_][:, :], wv[:, k, c * 128:(c + 1) * 128], uT[:, k, tl(t)], k == 0, k == 7) for k in range(8)]),
                           reads=key + [("uT", k, t) for k in range(8)], writes=[bk(b_)])
                    evac(usT[:, c, tl(t)], ps[b_][:, :], [bk(b_)], [("usT", c, t)])

        def ssm_gen(b, kglu, wglu):
            for h in range(2):
                P.emit("sp", DMA(cosT, trig_scr[:, 0, 4 * h:4 * h + 4, :]), reads=[("trig_scr", 0, sc) for sc in range(8)],
                       writes=[("slot", 3, 0), ("slot", 3, 1)], dma=True, arena=True)
                P.emit("sp", DMA(sinT, trig_scr[:, 1, 4 * h:4 * h + 4, :]), reads=[("trig_scr", 1, sc) for sc in range(8)],
                       writes=["sinT"], dma=True, arena=True)
                ck = [("slot", 3, 0), ("slot", 3, 1)]
                for tc in range(NT):
                    xbufs = []
                    for s4 in range(4):
                        sc = h * 4 + s4
                        br_, bi_ = nb(), nb()
                        P.emit("pe", MM([(ps[br_][:, :], Btab[:, sc, 0, :], usT[:, h, tl(tc)], True, True)]),
                               reads=["Btab", ("usT", h, tc)], writes=[bk(br_)])
                        P.emit("pe", MM([(ps[bi_][:, :], Btab[:, sc, 1, :], usT[:, h, tl(tc)], True, True)]),
                               reads=["Btab", ("usT", h, tc)], writes=[bk(bi_)])
                        c_, s_ = cosT[:, s4, :], sinT[:, s4, :]
                        W0, W1, W2, W3, W4, W5 = [wk(i) for i in range(6)]
                        K = lambda n: ("wk", n)
                        PR, PI = ps[br_][:, :], ps[bi_][:, :]
                        d = lambda fn, r, w: P.emit("dve", fn, reads=r, writes=w)
                        d(TT(W0, PR, c_, ALU.mult), [bk(br_)] + ck, [K(0)])
                        d(TT(W1, PI, s_, ALU.mult), [bk(bi_), "sinT"], [K(1)])
                        d(TT(W0, W0, W1, ALU.add), [K(0), K(1)], [K(0)])
                        d(TT(W1, PI, c_, ALU.mult), [bk(bi_)] + ck, [K(1)])
                        d(TT(W2, PR, s_, ALU.mult), [bk(br_), "sinT"], [K(2)])
                        d(TT(W1, W1, W2, ALU.subtract), [K(1), K(2)], [K(1)])
                        yield 1
                        magb = xap(mag_col(sc), 0, [[0, 512]])
                        ir = 0.0 if tc == 0 else carr_col(sc)
                        ii = 0.0 if tc == 0 else cari_col(sc)
                        d(SCAN(W3, magb, W0, ir), [K(0), "mag", ("car", sc)], [K(3)])
                        d(SCAN(W4, magb, W1, ii), [K(1), "mag", ("car", sc)], [K(4)])
                        yield 1
                        xr_b, xi_b = xb(2 * s4), xb(2 * s4 + 1)
                        kxr, kxi = ("xb", 2 * s4), ("xb", 2 * s4 + 1)
                        d(TT(W0, c_, W3, ALU.mult), [K(3)] + ck, [K(0)])
                        d(TT(W2, s_, W4, ALU.mult), [K(4), "sinT"], [K(2)])
                        d(TT(xr_b, W0, W2, ALU.subtract), [K(0), K(2)], [kxr])
                        d(TT(carr_col(sc), W0[:, 511:512], W2[:, 511:512], ALU.subtract), [K(0), K(2)], [("car", sc)])
                        yield 1
                        d(TT(W5, s_, W3, ALU.mult), [K(3), "sinT"], [K(5)])
                        d(TT(W2, c_, W4, ALU.mult), [K(4)] + ck, [K(2)])
                        d(STT(xi_b, W5, -1.0, W2, ALU.mult, ALU.subtract), [K(5), K(2)], [kxi])
                        d(TT(cari_col(sc), W5[:, 511:512], W2[:, 511:512], ALU.add), [K(5), K(2)], [("car", sc)])
                        xbufs.append((sc, xr_b, xi_b, kxr, kxi))
                        yield 1
                    by = nb()
                    lst = []
                    reads = ["Cre", "Cim", "Dx", ("usT", h, tc)]
                    for i, (sc, xr_b, xi_b, kxr, kxi) in enumerate(xbufs):
                        lst.append((ps[by][:, :], Cre[:, sc, :], xr_b, i == 0, False))
                        lst.append((ps[by][:, :], Cim[:, sc, :], xi_b, False, False))
                        reads += [kxr, kxi]
                    lst.append((ps[by][:, :], Dx[:, h, :], usT[:, h, tl(tc)], False, True))
                    P.emit("pe", MM(lst), reads=reads, writes=[bk(by)])
                    P.emit("act", ACP(ysb, ps[by][:, :]), reads=[bk(by)], writes=["ysb"])
                    P.emit("dve", TT(yt, ysb, ysb, ALU.mult), reads=["ysb"], writes=["yt"])
                    P.emit("dve", TS(yt, yt, 0.044715, 1.0, ALU.mult, ALU.add), reads=["yt"], writes=["yt"])
                    P.emit("dve", TT(yt, yt, ysb, ALU.mult), reads=["yt", "ysb"], writes=["yt"])
                    P.emit("act", ACTF(ysig, yt, AF.Sigmoid, scale=1.5957691216057308), reads=["yt"], writes=["ysig"])
                    P.emit("dve", TT(yg[:, h, tl(tc)], ysb, ysig, ALU.mult), reads=["ysb", "ysig"], writes=[("yg", h, tc)])
            for c in range(2):
                for t in range(NT):
                    b_ = nb()
                    P.emit("pe", MM([(ps[b_][:, :], wglu[:, hh, c * 128:(c + 1) * 128], yg[:, hh, tl(t)], hh == 0, hh == 1) for hh in range(2)]),
                           reads=kglu + [("yg", 0, t), ("yg", 1, t)], writes=[bk(b_)])
                    P.emit("act", ACTF(ysig, ps[b_][:, :], AF.Sigmoid, bias=smv(V_BGLU + c)), reads=[bk(b_), "vec"], writes=["ysig"])
                    P.emit("dve", TT(ssmo[:, c, tl(t)], yg[:, c, tl(t)], ysig, ALU.mult), reads=["ysig", ("yg", c, t)],
                           writes=[("ssmo", c, t)])

        def load_x(b):
            for t in range(NT):
                for k in range(8):
                    P.emit("sp", DMA(hT[:, k, tl(t)], xT[b, k * 128:(k + 1) * 128, tl(t)]), writes=[("hT", k, t)], dma=True, arena=True)

        def mixer_q(b, q, kpa, wpa, kps, wps, kg, wga, wgs):
            for jj in range(2):
                j = q * 2 + jj
                jc = slice(j * 128, (j + 1) * 128)
                jjc = slice(jj * 128, (jj + 1) * 128)
                for t in range(NT):
                    b1, b2, b3, b4 = nb(), nb(), nb(), nb()
                    P.emit("pe", MM([(ps[b3][:, :], wga[:, k, jjc], uT[:, k, tl(t)], k == 0, k == 7) for k in range(8)]),
                           reads=kg + [("uT", k, t) for k in range(8)], writes=[bk(b3)])
                    P.emit("pe", MM([(ps[b4][:, :], wgs[:, k, jjc], uT[:, k, tl(t)], k == 0, k == 7) for k in range(8)]),
                           reads=kg + [("uT", k, t) for k in range(8)], writes=[bk(b4)])
                    P.emit("pe", MM([(ps[b1][:, :], wpa[:, k, jc], oatt[:, k, tl(t)], k == 0, k == 3) for k in range(4)]),
                           reads=kpa + [("oatt", k) for k in range(4)], writes=[bk(b1)])
                    P.emit("pe", MM([(ps[b2][:, :], wps[:, k, jc], ssmo[:, k, tl(t)], k == 0, k == 1) for k in range(2)]),
                           reads=kps + [("ssmo", 0, t), ("ssmo", 1, t)], writes=[bk(b2)])
                    sa, ss_, m1, m2 = (p3buf[:, i, 0:512] for i in range(4))
                    P.emit("act", ACTF(sa, ps[b3][:, :], AF.Sigmoid, bias=smv(V_BGATE + j)), reads=[bk(b3), "vec"], writes=["sa"])
                    P.emit("act", ACTF(ss_, ps[b4][:, :], AF.Sigmoid, bias=smv(V_BGATE + 8 + j)), reads=[bk(b4), "vec"], writes=["ss"])
                    P.emit("dve", TT(m1, sa, ps[b1][:, :], ALU.mult), reads=["sa", bk(b1)], writes=["m1"])
                    P.emit("dve", TT(m2, ss_, ps[b2][:, :], ALU.mult), reads=["ss", bk(b2)], writes=["m2"])
                    P.emit("dve", TT(mT[:, j, tl(t)], m1, m2, ALU.add), reads=["m1", "m2"], writes=[("mT", j, t)])

        def wout_half(b, half, kwo, wo):
            for jj in range(4):
                j = half * 4 + jj
                for t in range(NT):
                    b_ = nb()
                    P.emit("pe", MM([(ps[b_][:, :], wo[:, k, jj * 128:(jj + 1) * 128], mT[:, k, tl(t)], k == 0, k == 7) for k in range(8)]),
                           reads=kwo + [("mT", k, t) for k in range(8)], writes=[bk(b_)])
                    P.emit("dve", STT(hT[:, j, tl(t)], ps[b_][:, :], mod_col(16 + j, b), hT[:, j, tl(t)], ALU.mult, ALU.add),
                           reads=[bk(b_), "mod", ("hT", j, t)], writes=[("hT", j, t)])

        gT = mT

        def ffn_up(b, hf, fb, kwa, wa, kwv, wvv):
            for ff in range(4):
                f = hf * 8 + fb * 4 + ff
                fl = fb * 4 + ff
                fc = slice(ff * 128, (ff + 1) * 128)
                for t in range(NT):
                    ba, bv = nb(), nb()
                    P.emit("pe", MM([(ps[ba][:, :], wa[:, k, fc], uT[:, k, tl(t)], k == 0, k == 7) for k in range(8)]),
                           reads=kwa + [("uT", k, t) for k in range(8)], writes=[bk(ba)])
                    P.emit("pe", MM([(ps[bv][:, :], wvv[:, k, fc], uT[:, k, tl(t)], k == 0, k == 7) for k in range(8)]),
                           reads=kwv + [("uT", k, t) for k in range(8)], writes=[bk(bv)])
                    ab_ = p3buf[:, t % 2, :]
                    pvb = p3buf[:, (t + 1) % 2, :]
                    if t == 0:
                        P.emit("dve", MEMSET(ab_[:, 0:2], 0.0), writes=[("asbh", t % 2)])
                    else:
                        P.emit("act", ACP(ab_[:, 0:2], pvb[:, 512:514]), reads=[("asb", (t + 1) % 2)], writes=[("asbh", t % 2)])
                    P.emit("act", ACP(ab_[:, 2:514], ps[ba][:, :]), reads=[bk(ba)], writes=[("asb", t % 2)])
                    cv = p3buf[:, 2, 0:512]
                    sl = p3buf[:, 3, 0:512]
                    wc = lambda jtap, f=f: smv(V_WCONV + f * 3 + jtap)
                    rk = [("asb", t % 2), ("asbh", t % 2), "vec"]
                    P.emit("dve", TS(cv, ab_[:, 2:514], wc(0), smv(V_BCONV + f), ALU.mult, ALU.add), reads=rk, writes=["cv"])
                    P.emit("dve", STT(cv, ab_[:, 1:513], wc(1), cv, ALU.mult, ALU.add), reads=rk + ["cv"], writes=["cv"])
                    P.emit("dve", STT(cv, ab_[:, 0:512], wc(2), cv, ALU.mult, ALU.add), reads=rk + ["cv"], writes=["cv"])
                    P.emit("act", ACTF(sl, cv, AF.Silu), reads=["cv"], writes=["sl"])
                    P.emit("dve", TT(gT[:, fl, tl(t)], sl, ps[bv][:, :], ALU.mult), reads=["sl", bk(bv)], writes=[("mT", fl, t)])

        def ffn_down(b, hf, cb, kwd, wd):
            for jj in range(4):
                j = cb * 4 + jj
                for t in range(NT):
                    b_ = nb()
                    P.emit("pe", MM([(ps[b_][:, :], wd[:, k, jj * 128:(jj + 1) * 128], gT[:, k, tl(t)], k == 0, k == 7) for k in range(8)]),
                           reads=kwd + [("mT", k, t) for k in range(8)], writes=[bk(b_)])
                    P.emit("dve", STT(hT[:, j, tl(t)], ps[b_][:, :], mod_col(40 + j, b), hT[:, j, tl(t)], ALU.mult, ALU.add),
                           reads=[bk(b_), "mod", ("hT", j, t)], writes=[("hT", j, t)])

        def final_norm(b):
            for t in range(NT):
                sq = A_m[:, (t % 2) * 4096:(t % 2) * 4096 + 4096].rearrange("p (k c) -> p k c", k=8)
                rstd = Amf[:, 4096 + (t % 2) * 512: 4096 + (t % 2) * 512 + 512]
                P.emit("act", ACTF(sq, hT[:, :, tl(t)], AF.Square), reads=[("hT", k, t) for k in range(8)], writes=[("nsq", t % 2)])
                b_ = nb()
                P.emit("pe", MM([(ps[b_][:, :], onesb[:, :], sq[:, k, :], k == 0, k == 7) for k in range(8)]),
                       reads=[("nsq", t % 2), "ones"], writes=[bk(b_)])
                P.emit("act", ACTF(rstd, ps[b_][:, :], AF.Ln, bias=eps_col, scale=1.0 / D), reads=[bk(b_), "consts"], writes=[("nrstd", t % 2)])
                P.emit("act", ACTF(rstd, rstd, AF.Exp, scale=-0.5), reads=[("nrstd", t % 2)], writes=[("nrstd", t % 2)])
                for k in range(8):
                    si = (t * 8 + k) % 4
                    stg = Amf[:, 6144 + si * 512: 6144 + (si + 1) * 512]
                    P.emit("dve", STT(stg, hT[:, k, tl(t)], smv(V_GFIN + k), rstd, ALU.mult, ALU.mult),
                           reads=[("hT", k, t), ("nrstd", t % 2), "vec"], writes=[("ostg", si)])
                    P.emit("sp", DMA(outT[b, k * 128:(k + 1) * 128, tl(t)], stg),
                           reads=[("ostg", si)], writes=[("outT", b, k, t)], dma=True, arena=True)

        stages = []

        def gq_src(q):
            return [("ga", w_in[:, 1792 + q * 256: 1792 + (q + 1) * 256], 8, 256, 2 if q % 2 == 0 else 3, 0),
                    ("gs", w_in[:, 2816 + q * 256: 2816 + (q + 1) * 256], 8, 256, 2 if q % 2 == 0 else 3, 2048)]

        gen_state = {}

        for b in range(NB):
            def c_us(W, b=b):
                load_x(b)
                norm_mod(b, gs1_col, 0, uT)
                P.barrier()
                ssm_tables()
                load_E(0)
                us_proj(b, *W["us"])
                gen_state["gen"] = ssm_gen(b, *W["glu"])
            stages.append(([("us", w_in[:, 1536:1792], 8, 256, 2, 0), ("glu", w_glu[:, :], 2, 256, 1, 3072)], c_us))

            def tick(force=False):
                gen_state["n"] = gen_state.get("n", 0) + 1
                if not force and gen_state["n"] % 4 != 0:
                    return
                g = gen_state.get("gen")
                if g is not None:
                    try:
                        next(g)
                    except StopIteration:
                        gen_state["gen"] = None

            for hp in range(4):
                def c_att(W, b=b, hp=hp, tick=tick):
                    attention_hp(b, hp, *W["w"], tick)
                stages.append(([("w", w_in[:, hp * 384:(hp + 1) * 384], 8, 384, hp % 2, 0)], c_att))

            def c_tail(W, tick=tick):
                while gen_state.get("gen") is not None:
                    tick(force=True)
            stages.append(([], c_tail))
            for q in range(4):
                def c_mq(W, b=b, q=q, st_=state):
                    if q == 0:
                        st_["pa"] = W["pa"]
                        st_["ps"] = W["ps"]
                        P.barrier()
                        load_x(b)
                    kg = W["ga"][0] + W["gs"][0]
                    mixer_q(b, q, st_["pa"][0], st_["pa"][1], st_["ps"][0], st_["ps"][1], kg, W["ga"][1], W["gs"][1])
                lds = gq_src(q)
                if q == 0:
                    lds = [("pa", w_pa[:, :], 4, 1024, 0, 0), ("ps", w_ps[:, :], 2, 1024, 1, 0)] + lds
                stages.append((lds, c_mq))
            for half in range(2):
                def c_wo(W, b=b, half=half):
                    wout_half(b, half, *W["wo"])
                stages.append(([("wo", w_out[:, half * 512:(half + 1) * 512], 8, 512, 2 if half == 0 else 3, 0)], c_wo))
            fslots = [(0, 1), (2, 3), (0,), (1,), (2, 3), (0, 1), (2,), (3,)]
            fi = 0
            for hf in range(2):
                for fb in range(2):
                    c0 = hf * 1024 + fb * 512

                    def c_up(W, b=b, hf=hf, fb=fb):
                        if hf == 0 and fb == 0:
                            P.barrier()
                            norm_mod(b, gs2_col, 24, uT)
                        ffn_up(b, hf, fb, W["wa"][0], W["wa"][1], W["wv"][0], W["wv"][1])
                    stages.append(([("wa", w_up[:, c0:c0 + 512], 8, 512, fslots[fi][0], 0),
                                    ("wv", w_up[:, 2048 + c0:2048 + c0 + 512], 8, 512, fslots[fi][1], 0)], c_up))
                    fi += 1
                for cb in range(2):
                    def c_dn(W, b=b, hf=hf, cb=cb):
                        ffn_down(b, hf, cb, *W["wd"])
                        if hf == 1 and cb == 1:
                            P.barrier()
                            final_norm(b)
                    stages.append(([("wd", w_down[hf * 1024:(hf + 1) * 1024, cb * 512:(cb + 1) * 512], 8, 512, fslots[fi][0], 0)], c_dn))
                    fi += 1

        def do_loads(lds):
            return {nm: load_w(src, nk, ncols, slot=sl_, off=off) for (nm, src, nk, ncols, sl_, off) in lds}

        Wn = do_loads(stages[0][0])
        P.barrier()
        for i, (lds, comp) in enumerate(stages):
            Wc = Wn
            if i + 1 < len(stages):
                Wn = do_loads(stages[i + 1][0])
            comp(Wc)
        P.final_wait("sp")
        P.build()
    return nc


def _host_layouts(inp, NB):
    f32 = np.float32
    x = np.asarray(inp["x"], f32)
    c = np.asarray(inp["c"], f32)
    col8 = lambda v: np.ascontiguousarray(np.asarray(v, f32).reshape(-1, 128).T)
    w_in = np.asarray(inp["w_in"][0], f32)
    perm = []
    for hp in range(4):
        for sec in range(3):
            perm += list(range(sec * 512 + hp * 128, sec * 512 + (hp + 1) * 128))
    perm += list(range(1536, 3840))
    w_in_p = np.ascontiguousarray(w_in[:, perm])
    a_re, a_im, ldt = inp["a_re"][0], inp["a_im"][0], inp["log_dt"][0]
    st_major = lambda a: np.ascontiguousarray(np.asarray(a, f32).reshape(8, 128).T)
    ldt_l = np.ascontiguousarray(np.repeat(np.asarray(ldt, f32), 64).reshape(8, 128).T)
    wconv = np.asarray(inp["w_conv"][0], f32)
    wconv_l = np.ascontiguousarray(wconv.reshape(3, 16, 128).transpose(2, 1, 0).reshape(128, 48))
    vec = np.concatenate([
        col8(inp["g_mix"][0]), col8(inp["g_ffn"][0]), col8(inp["g_final"]), col8(inp["b_gate"][0]),
        col8(inp["b_glu"][0]), wconv_l, col8(inp["b_conv"][0]), st_major(a_re), st_major(a_im), ldt_l,
        col8(inp["b_ada"][0])], axis=1).astype(f32)
    assert vec.shape == (128, NV), vec.shape

    def expand_b(bmat):
        out = np.zeros((128, 8, 128), f32)
        for g in range(16):
            sc, gl = g // 2, g % 2
            c0 = 32 * (sc % 4) + 16 * gl
            out[gl * 64:(gl + 1) * 64, sc, c0:c0 + 16] = bmat[g]
        return out

    def expand_c(cmat):
        out = np.zeros((128, 8, 128), f32)
        for g in range(16):
            sc, gl = g // 2, g % 2
            c0 = 32 * (sc % 4) + 16 * gl
            out[gl * 64:(gl + 1) * 64, sc, c0:c0 + 16] = cmat[g].T
        return out

    d = np.asarray(inp["d_skip"][0], f32)
    d_x = np.zeros((128, 2, 128), f32)
    for h in range(2):
        d_x[np.arange(128), h, np.arange(128)] = d[h * 128:(h + 1) * 128]
    ak = np.arange(128, dtype=np.float64)[:, None]
    aq = np.arange(128, dtype=np.float64)[None, :]
    E = np.zeros((128, 48, 128), f32)
    for h in range(8):
        slope = 2.0 ** (-8.0 * (h + 1) / 8)
        for p, dil in enumerate((1, 4, 16)):
            cur = np.where(ak <= aq, np.exp(-slope * dil * (aq - ak) - 0.0), 0.0)
            prv = np.where(ak >= aq, np.exp(-slope * dil * (128 + aq - ak)), 0.0)
            E[:, (h * 3 + p) * 2 + 0, :] = cur
            E[:, (h * 3 + p) * 2 + 1, :] = prv
    common = dict(
        w_ada=np.ascontiguousarray(inp["w_ada"][0], f32), w_in_p=w_in_p, vec=vec,
        bre_x=expand_b(np.asarray(inp["b_re"][0], f32)), bim_x=expand_b(np.asarray(inp["b_im"][0], f32)),
        cre_x=expand_c(np.asarray(inp["c_re"][0], f32)), cim_x=expand_c(np.asarray(inp["c_im"][0], f32)),
        d_x=d_x, ident=np.eye(128, dtype=f32), iota=np.tile(np.arange(1, 513, dtype=f32)[None, :], (128, 1)), E=E,
        w_glu=np.ascontiguousarray(inp["w_glu"][0], f32), w_proj_att=np.ascontiguousarray(inp["w_proj_att"][0], f32),
        w_proj_ssm=np.ascontiguousarray(inp["w_proj_ssm"][0], f32), w_out=np.ascontiguousarray(inp["w_out"][0], f32),
        w_up=np.ascontiguousarray(inp["w_up"][0], f32), w_down=np.ascontiguousarray(inp["w_down"][0], f32))
    maps = []
    for core in range(NCORES):
        bs = slice(core * NB, (core + 1) * NB)
        m = dict(common)
        m["xT"] = np.ascontiguousarray(x[bs].transpose(0, 2, 1))
        m["cT"] = np.ascontiguousarray(c[bs].reshape(NB, 8, 128).transpose(2, 1, 0))
        maps.append(m)
    return maps


def kernel(**inputs):
    B = inputs["x"].shape[0]
    NB = B // NCORES
    maps = _host_layouts(inputs, NB)
    nc = build_nc(NB)
    res = run_bass_kernel_spmd(nc, maps, core_ids=list(range(NCORES)))
    outs = [np.asarray(r["outT"]).transpose(0, 2, 1) for r in res.results]
    return np.ascontiguousarray(np.concatenate(outs, axis=0).astype(np.float32))
```

```python
import math
from contextlib import ExitStack

import numpy as np
import concourse.bass as bass
import concourse.mybir as mybir
from concourse.bass_utils import run_bass_kernel_spmd

F32 = mybir.dt.float32
BF16 = mybir.dt.bfloat16
I32 = mybir.dt.int32
AF = mybir.ActivationFunctionType
ALU = mybir.AluOpType

NCORES = 8
D = 1024
S = 2048
NT = 4
EPS = 1e-6
SHIFT = 8.0
TWO_PI = 2.0 * math.pi
PI_LO = 3.1415925
ENGS = ("pe", "act", "dve", "pool", "sp")

V_GMIX, V_GFFN, V_GFIN, V_BGATE, V_BGLU, V_WCONV, V_BCONV, V_ARE, V_AIM, V_LDT, V_BADA = (
    0, 8, 16, 24, 40, 42, 90, 106, 114, 122, 130)
NV = 178


class Prog:
    def __init__(self, nc, stack, n_dma_sems=8):
        self.nc = nc
        self.ops = {e: [] for e in ENGS}
        self.cnt = {e: 0 for e in ENGS}
        self.semobj = {}
        for e in ENGS:
            self.semobj[("c", e)] = stack.enter_context(nc.semaphore("c_" + e))
        self.dval = {q: [0] * n_dma_sems for q in ("sp", "pool")}
        self.dnext = {q: 0 for q in ("sp", "pool")}
        for q in ("sp", "pool"):
            for i in range(n_dma_sems):
                self.semobj[("d", q, i)] = stack.enter_context(nc.semaphore("d_%s%d" % (q, i)))
        self.waited = {e: {} for e in ENGS}
        self.regions = {}
        self.fence = []
        self.arena_toks = {}

    def _need(self, eng, waits, tok):
        if tok is None:
            return
        sid, val = tok
        if self.waited[eng].get(sid, 0) >= val:
            return
        if waits.get(sid, 0) < val:
            waits[sid] = val

    def emit(self, eng, fn, reads=(), writes=(), dma=False, arena=False):
        waits = {}
        own = ("c", eng)
        if dma and arena:
            for t in self.fence:
                self._need(eng, waits, t)
        for key in reads:
            r = self.regions.get(key)
            if r is not None:
                self._need(eng, waits, r["w"])
        for key in writes:
            r = self.regions.get(key)
            if r is not None:
                w = r["w"]
                if w is not None and not (w[0] == own and not dma):
                    self._need(eng, waits, w)
                for sid, val in r["r"].items():
                    if sid == own and not dma:
                        continue
                    self._need(eng, waits, (sid, val))
        if dma:
            i = self.dnext[eng]
            self.dnext[eng] = (i + 1) % len(self.dval[eng])
            sid = ("d", eng, i)
            if self.dval[eng][i] > 0:
                self._need(eng, waits, (sid, self.dval[eng][i]))
            self.dval[eng][i] += 16
            tok = (sid, self.dval[eng][i])
            amt = 16
            if arena:
                self.arena_toks[sid] = tok[1]
        else:
            self.cnt[eng] += 1
            tok = (own, self.cnt[eng])
            amt = 1
        for sid, val in waits.items():
            self.waited[eng][sid] = val
        for key in reads:
            r = self.regions.setdefault(key, {"w": None, "r": {}})
            if r["r"].get(tok[0], 0) < tok[1]:
                r["r"][tok[0]] = tok[1]
        for key in writes:
            self.regions[key] = {"w": tok, "r": {}}
        self.ops[eng].append((list(waits.items()), fn, tok, amt))
        return tok

    def _all_toks(self):
        toks = []
        for e in ENGS:
            if self.cnt[e] > 0:
                toks.append((("c", e), self.cnt[e]))
        for q in ("sp", "pool"):
            for i, v in enumerate(self.dval[q]):
                if v > 0:
                    toks.append((("d", q, i), v))
        return toks

    def barrier(self):
        toks = self._all_toks()
        for e in ENGS:
            waits = {}
            for t in toks:
                if t[0] == ("c", e):
                    continue
                self._need(e, waits, t)
            for sid, val in waits.items():
                self.waited[e][sid] = val
            if waits:
                self.ops[e].append((list(waits.items()), None, None, 0))

    def final_wait(self, eng="sp"):
        waits = {}
        for t in self._all_toks():
            if t[0] == ("c", eng):
                continue
            self._need(eng, waits, t)
        self.ops[eng].append((list(waits.items()), None, None, 0))

    def build(self):
        nc = self.nc
        with nc.Block() as block:
            def mk(e):
                def body(h):
                    for waits, fn, tok, amt in self.ops[e]:
                        for sid, val in waits:
                            h.wait_ge(self.semobj[sid], val)
                        if fn is None:
                            continue
                        ins = fn(h)
                        ins.then_inc(self.semobj[tok[0]], amt)
                return body
            block.tensor(mk("pe"))
            block.scalar(mk("act"))
            block.vector(mk("dve"))
            block.gpsimd(mk("pool"))
            block.sync(mk("sp"))


def MM(lst):
    def fn(e):
        ins = None
        for (out, lhsT, rhs, st, sp) in lst:
            ins = e.matmul(out, lhsT=lhsT, rhs=rhs, start=st, stop=sp)
        return ins
    return fn


def ACTF(out, in_, func, bias=None, scale=None):
    def fn(e):
        kw = {}
        if bias is not None:
            kw["bias"] = bias
        if scale is not None:
            kw["scale"] = scale
        return e.activation(out=out, in_=in_, func=func, **kw)
    return fn


def TT(out, in0, in1, op):
    return lambda e: e.tensor_tensor(out=out, in0=in0, in1=in1, op=op)


def TS(out, in0, s1, s2, op0, op1=None):
    if op1 is None:
        return lambda e: e.tensor_scalar(out=out, in0=in0, scalar1=s1, scalar2=None, op0=op0)
    return lambda e: e.tensor_scalar(out=out, in0=in0, scalar1=s1, scalar2=s2, op0=op0, op1=op1)


def STT(out, in0, scalar, in1, op0, op1):
    return lambda e: e.scalar_tensor_tensor(out=out, in0=in0, scalar=scalar, in1=in1, op0=op0, op1=op1)


def CP(out, in_):
    return lambda e: e.tensor_copy(out=out, in_=in_)


def ACP(out, in_):
    return lambda e: e.activation(out=out, in_=in_, func=AF.Copy)


def DMA(out, in_):
    return lambda e: e.dma_start(out=out, in_=in_)


def SCAN(out, d0, d1, init):
    return lambda e: e.tensor_tensor_scan(out=out, data0=d0, data1=d1, initial=init, op0=ALU.mult, op1=ALU.add)


def RECIP(out, in_):
    return lambda e: e.reciprocal(out=out, in_=in_)


def MEMSET(ap, v):
    return lambda e: e.memset(ap, v)


def TRANSP(out, in_, ident):
    return lambda e: e.transpose(out=out, in_=in_, identity=ident)


def xap(base, start, dims):
    return bass.AP(base.tensor, base.offset + start, [list(base.ap[0])] + [list(d) for d in dims])


def build_nc(NB):
    nc = bass.Bass("TRN2", target_bir_lowering=False)

    def din(name, shape, dtype=F32):
        return nc.dram_tensor(name, list(shape), dtype, kind="ExternalInput").ap()

    xT = din("xT", [NB, D, S])
    cT = din("cT", [128, 8, NB])
    w_ada = din("w_ada", [D, 6 * D])
    w_in = din("w_in_p", [D, 3840])
    vec = din("vec", [128, NV])
    bre_x = din("bre_x", [128, 8, 128])
    bim_x = din("bim_x", [128, 8, 128])
    cre_x = din("cre_x", [128, 8, 128])
    cim_x = din("cim_x", [128, 8, 128])
    d_x = din("d_x", [128, 2, 128])
    ident_d = din("ident", [128, 128])
    iota_d = din("iota", [128, 512])
    E_d = din("E", [128, 48, 128])
    w_glu = din("w_glu", [256, 256])
    w_pa = din("w_proj_att", [512, D])
    w_ps = din("w_proj_ssm", [256, D])
    w_out = din("w_out", [D, D])
    w_up = din("w_up", [D, 4096])
    w_down = din("w_down", [2048, D])
    outT = nc.dram_tensor("outT", [NB, D, S], F32, kind="ExternalOutput").ap()
    trig_scr = nc.dram_tensor("trig_scr", [128, 2, 8, 512], F32).ap()
    btab_scr = nc.dram_tensor("btab_scr", [128, 8, 2, 128], BF16).ap()

    with ExitStack() as st:
        P = Prog(nc, st)
        sb = lambda name, shape, dt: st.enter_context(nc.sbuf_tensor(name, list(shape), dt))
        A_h = sb("A_h", [128, 16384], F32)
        A_u = sb("A_u", [128, 16384], BF16)
        A_m = sb("A_m", [128, 16384], BF16)
        oatt = sb("oatt", [128, 4, S], BF16)
        ssmo = sb("ssmo", [128, 2, S], BF16)
        usT = sb("usT", [128, 2, S], BF16)
        NSLOT = 4
        slots = [sb("slot%d" % i, [128, 4096], BF16) for i in range(NSLOT)]
        p3buf = sb("p3buf", [128, 4, 514], F32)
        sm = sb("sm", [128, 768], F32)
        csb = sb("csb", [128, 8, NB], BF16)
        onesb = sb("onesb", [128, 128], BF16)
        ps = [st.enter_context(nc.psum_tensor("ps%d" % i, [128, 512], F32)) for i in range(8)]

        hT = A_h[:, :].rearrange("p (k t) -> p k t", k=8)
        uT = A_u[:, :].rearrange("p (k t) -> p k t", k=8)
        mT = A_m[:, :].rearrange("p (k t) -> p k t", k=8)
        Ahb = A_h[:, :].bitcast(BF16)
        Amf = A_m[:, :].bitcast(F32)

        SM_VEC = 0
        SM_MOD = 192
        SM_GS1 = SM_MOD + 48 * NB
        SM_GS2 = SM_GS1 + 8 * NB
        SM_SSM = SM_GS2 + 8 * NB
        assert SM_SSM + 26 <= 768
        eps_col = sm[:, SM_SSM + 24: SM_SSM + 25]
        nshift_col = sm[:, SM_SSM + 25: SM_SSM + 26]
        smv = lambda off, n=1: sm[:, off:off + n]
        mod_col = lambda chunk, b: sm[:, SM_MOD + chunk * NB + b: SM_MOD + chunk * NB + b + 1]
        gs1_col = lambda k, b: sm[:, SM_GS1 + k * NB + b: SM_GS1 + k * NB + b + 1]
        gs2_col = lambda k, b: sm[:, SM_GS2 + k * NB + b: SM_GS2 + k * NB + b + 1]
        mag_col = lambda sc: sm[:, SM_SSM + sc: SM_SSM + sc + 1]
        carr_col = lambda sc: sm[:, SM_SSM + 8 + sc: SM_SSM + 9 + sc]
        cari_col = lambda sc: sm[:, SM_SSM + 16 + sc: SM_SSM + 17 + sc]

        state = {"bank": 0, "slot": 0, "ev": 0}

        def nb():
            i = state["bank"]
            state["bank"] = (i + 1) % 6
            return i

        def bk(i):
            return ("ps", i)

        def ev():
            state["ev"] ^= 1
            return "act" if state["ev"] else "dve"

        def evac(out, in_, reads, writes):
            P.emit("act", ACP(out, in_), reads=reads, writes=writes)

        def load_w(src2d, nk, ncols, slot=None, off=0):
            if slot is None:
                i = state["slot"]
                state["slot"] = (i + 1) % NSLOT
            else:
                i = slot
            n = nk * ncols
            keys = [("slot", i, h) for h in range(2) if off < (h + 1) * 2048 and off + n > h * 2048]
            view = slots[i][:, off:off + n].rearrange("p (k c) -> p k c", k=nk)
            P.emit("pool", DMA(view, src2d.rearrange("(k p) c -> p k c", p=128)), writes=keys, dma=True)
            return keys, view

        tl = lambda t: slice(t * 512, (t + 1) * 512)

        P.emit("sp", DMA(sm[:, 0:NV], vec[:, :]), writes=["vec"], dma=True, arena=True)
        P.emit("dve", MEMSET(onesb[:, :], 1.0), writes=["ones"])
        P.emit("dve", MEMSET(eps_col, EPS), writes=["consts"])
        P.emit("dve", MEMSET(nshift_col, -SHIFT), writes=["consts"])
        ctmp = A_h[:, 0:8 * NB].rearrange("p (k b) -> p k b", k=8)
        P.emit("sp", DMA(ctmp, cT[:, :, :]), writes=["ctmp"], dma=True, arena=True)
        P.emit("act", ACTF(csb[:, :, :], ctmp, AF.Silu), reads=["ctmp"], writes=["csb"])
        for jb in range(12):
            key, wv = load_w(w_ada[:, jb * 512:(jb + 1) * 512], 8, 512)
            for jj in range(4):
                j = jb * 4 + jj
                b_ = nb()
                P.emit("pe", MM([(ps[b_][:, 0:NB], wv[:, k, jj * 128:(jj + 1) * 128], csb[:, k, :], k == 0, k == 7)
                                 for k in range(8)]), reads=key + ["csb"], writes=[bk(b_)])
                P.emit("dve", TS(sm[:, SM_MOD + j * NB: SM_MOD + (j + 1) * NB], ps[b_][:, 0:NB],
                                 smv(V_BADA + j), None, ALU.add), reads=[bk(b_), "vec"], writes=["mod"])
        for k in range(8):
            P.emit("dve", TS(sm[:, SM_GS1 + k * NB: SM_GS1 + (k + 1) * NB],
                             sm[:, SM_MOD + (8 + k) * NB: SM_MOD + (9 + k) * NB], 1.0, smv(V_GMIX + k), ALU.add, ALU.mult),
                   reads=["mod", "vec"], writes=["gs"])
            P.emit("dve", TS(sm[:, SM_GS2 + k * NB: SM_GS2 + (k + 1) * NB],
                             sm[:, SM_MOD + (32 + k) * NB: SM_MOD + (33 + k) * NB], 1.0, smv(V_GFFN + k), ALU.add, ALU.mult),
                   reads=["mod", "vec"], writes=["gs"])

        pt = lambda i: A_h[:, 512 + 8 * i: 512 + 8 * (i + 1)]
        Xre = A_h[:, 1024:2048].rearrange("p (s c) -> p s c", s=8)
        Xim = A_h[:, 2048:3072].rearrange("p (s c) -> p s c", s=8)
        bre = A_h[:, 3072:4096].rearrange("p (s c) -> p s c", s=8)
        bim = A_h[:, 4096:5120].rearrange("p (s c) -> p s c", s=8)
        iota = A_h[:, 5120:5632]
        ident = A_h[:, 5632:5760]
        P.emit("sp", DMA(bre, bre_x[:, :, :]), writes=["bre"], dma=True, arena=True)
        P.emit("sp", DMA(bim, bim_x[:, :, :]), writes=["bim"], dma=True, arena=True)
        P.emit("sp", DMA(iota, iota_d[:, :]), writes=["iota"], dma=True, arena=True)
        P.emit("sp", DMA(ident, ident_d[:, :]), writes=["ident"], dma=True, arena=True)
        lr, li, ldt = smv(V_ARE, 8), smv(V_AIM, 8), smv(V_LDT, 8)
        T = {}

        def pe_(name, eng, fn, reads):
            P.emit(eng, fn, reads=reads + ["vec"], writes=[name])

        names = ["dt", "lrdt", "th", "ki", "kf", "thr", "sin", "ab", "cos", "abr", "abi", "nr", "t1", "t2", "den",
                 "rden", "u1", "u2", "fre", "fim", "nfim", "u3", "u4"]
        for i, n in enumerate(names):
            T[n] = pt(i)
        T["ki"] = pt(names.index("ki")).bitcast(I32)
        pe_("dt", "act", ACTF(T["dt"], ldt, AF.Exp), [])
        pe_("lrdt", "dve", TT(T["lrdt"], lr, T["dt"], ALU.mult), ["dt"])
        pe_("mag", "act", ACTF(smv(SM_SSM, 8), T["lrdt"], AF.Exp), ["lrdt"])
        pe_("th", "dve", TT(T["th"], li, T["dt"], ALU.mult), ["dt"])
        pe_("ki", "dve", TS(T["ki"], T["th"], 1.0 / TWO_PI, None, ALU.mult), ["th"])
        pe_("kf", "dve", CP(T["kf"], T["ki"]), ["ki"])
        pe_("thr", "dve", STT(T["thr"], T["kf"], -TWO_PI, T["th"], ALU.mult, ALU.add), ["kf", "th"])
        pe_("thr", "dve", TS(T["thr"], T["thr"], PI_LO, -PI_LO, ALU.min, ALU.max), ["thr"])
        pe_("sin", "act", ACTF(T["sin"], T["thr"], AF.Sin), ["thr"])
        pe_("ab", "act", ACTF(T["ab"], T["thr"], AF.Abs), ["thr"])
        pe_("ab2", "dve", TS(T["u4"], T["ab"], -1.0, math.pi / 2, ALU.mult, ALU.add), ["ab"])
        pe_("cos", "act", ACTF(T["cos"], T["u4"], AF.Sin), ["ab2"])
        pe_("abr", "dve", TT(T["abr"], smv(SM_SSM, 8), T["cos"], ALU.mult), ["mag", "cos"])
        pe_("abi", "dve", TT(T["abi"], smv(SM_SSM, 8), T["sin"], ALU.mult), ["mag", "sin"])
        pe_("nr", "dve", TS(T["nr"], T["abr"], -1.0, None, ALU.add), ["abr"])
        pe_("t1", "dve", TT(T["t1"], lr, lr, ALU.mult), [])
        pe_("t2", "dve", TT(T["t2"], li, li, ALU.mult), [])
        pe_("den", "dve", TT(T["den"], T["t1"], T["t2"], ALU.add), ["t1", "t2"])
        pe_("rden", "dve", RECIP(T["rden"], T["den"]), ["den"])
        pe_("u1", "dve", TT(T["u1"], T["nr"], lr, ALU.mult), ["nr"])
        pe_("u2", "dve", TT(T["u2"], T["abi"], li, ALU.mult), ["abi"])
        pe_("u3", "dve", TT(T["u3"], T["u1"], T["u2"], ALU.add), ["u1", "u2"])
        pe_("fre", "dve", TT(T["fre"], T["u3"], T["rden"], ALU.mult), ["u3", "rden"])
        pe_("u1b", "dve", TT(T["u1"], T["abi"], lr, ALU.mult), ["abi", "u3"])
        pe_("u2b", "dve", TT(T["u2"], T["nr"], li, ALU.mult), ["nr", "u3"])
        pe_("u3b", "dve", TT(T["u3"], T["u1"], T["u2"], ALU.subtract), ["u1b", "u2b", "fre"])
        pe_("fim", "dve", TT(T["fim"], T["u3"], T["rden"], ALU.mult), ["u3b", "rden"])
        pe_("nfim", "dve", TS(T["nfim"], T["fim"], -1.0, None, ALU.mult), ["fim"])
        btab = Ahb[:, 2 * 13312: 2 * 13312 + 2048].rearrange("p (s r c) -> p s r c", s=8, r=2)
        for sc in range(8):
            c1 = lambda t, sc=sc: t[:, sc:sc + 1]
            P.emit("dve", TS(Xre[:, sc, :], bre[:, sc, :], c1(T["fre"]), None, ALU.mult), reads=["bre", "fre"], writes=[("Xre", sc)])
            P.emit("dve", STT(Xre[:, sc, :], bim[:, sc, :], c1(T["nfim"]), Xre[:, sc, :], ALU.mult, ALU.add),
                   reads=["bim", "nfim", ("Xre", sc)], writes=[("Xre", sc)])
            P.emit("dve", TS(Xim[:, sc, :], bre[:, sc, :], c1(T["fim"]), None, ALU.mult), reads=["bre", "fim"], writes=[("Xim", sc)])
            P.emit("dve", STT(Xim[:, sc, :], bim[:, sc, :], c1(T["fre"]), Xim[:, sc, :], ALU.mult, ALU.add),
                   reads=["bim", "fre", ("Xim", sc)], writes=[("Xim", sc)])
            for ri, X in enumerate((Xre, Xim)):
                b_ = nb()
                P.emit("pe", TRANSP(ps[b_][:, 0:128], X[:, sc, :], ident), reads=[("Xre" if ri == 0 else "Xim", sc), "ident"],
                       writes=[bk(b_)])
                evac(btab[:, sc, ri, :], ps[b_][:, 0:128], [bk(b_)], [("btab", sc, ri)])
        P.emit("sp", DMA(btab_scr[:, :, :, :], btab), reads=[("btab", sc, ri) for sc in range(8) for ri in range(2)],
               writes=["btab_scr"], dma=True, arena=True)
        for sc in range(8):
            base = 6144 + (sc % 2) * 3072
            arg = A_h[:, base:base + 512]
            ki = A_h[:, base + 512:base + 1024].bitcast(I32)
            kf = A_h[:, base + 1024:base + 1536]
            so = A_h[:, base + 1536:base + 2048]
            ab = A_h[:, base + 2048:base + 2560]
            co = A_h[:, base + 2560:base + 3072]
            kk = lambda n, sc=sc: ("tg", n, sc % 2)
            P.emit("dve", TS(arg, iota, T["thr"][:, sc:sc + 1], None, ALU.mult), reads=["iota", "thr"], writes=[kk("arg")])
            P.emit("dve", TS(ki, arg, 1.0 / TWO_PI, None, ALU.mult), reads=[kk("arg")], writes=[kk("ki")])
            P.emit("dve", CP(kf, ki), reads=[kk("ki")], writes=[kk("kf")])
            P.emit("dve", STT(arg, kf, -TWO_PI, arg, ALU.mult, ALU.add), reads=[kk("kf"), kk("arg")], writes=[kk("arg")])
            P.emit("dve", TS(arg, arg, PI_LO, -PI_LO, ALU.min, ALU.max), reads=[kk("arg")], writes=[kk("arg")])
            P.emit("act", ACTF(so, arg, AF.Sin), reads=[kk("arg")], writes=[kk("so")])
            P.emit("act", ACTF(ab, arg, AF.Abs), reads=[kk("arg")], writes=[kk("ab")])
            P.emit("dve", TS(ab, ab, -1.0, math.pi / 2, ALU.mult, ALU.add), reads=[kk("ab")], writes=[kk("ab")])
            P.emit("act", ACTF(co, ab, AF.Sin), reads=[kk("ab")], writes=[kk("co")])
            P.emit("sp", DMA(trig_scr[:, 0, sc, :], co), reads=[kk("co")], writes=[("trig_scr", 0, sc)], dma=True, arena=True)
            P.emit("sp", DMA(trig_scr[:, 1, sc, :], so), reads=[kk("so")], writes=[("trig_scr", 1, sc)], dma=True, arena=True)

        def mtk(chunks):
            return [("mT", c, tt) for c in chunks for tt in range(NT)]

        def norm_mod(b, gs_col, sh_chunk0, dst):
            def stats(t):
                sq = A_m[:, (t % 2) * 4096:(t % 2) * 4096 + 4096].rearrange("p (k c) -> p k c", k=8)
                rstd = Amf[:, 4096 + (t % 2) * 512: 4096 + (t % 2) * 512 + 512]
                sqk = mtk((2 * (t % 2), 2 * (t % 2) + 1))
                P.emit("act", ACTF(sq, hT[:, :, tl(t)], AF.Square), reads=[("hT", k, t) for k in range(8)],
                       writes=[("nsq", t % 2)] + sqk)
                b_ = nb()
                P.emit("pe", MM([(ps[b_][:, :], onesb[:, :], sq[:, k, :], k == 0, k == 7) for k in range(8)]),
                       reads=[("nsq", t % 2), "ones"] + sqk, writes=[bk(b_)])
                P.emit("act", ACTF(rstd, ps[b_][:, :], AF.Ln, bias=eps_col, scale=1.0 / D),
                       reads=[bk(b_), "consts"], writes=[("nrstd", t % 2)] + mtk((4,)))
                P.emit("act", ACTF(rstd, rstd, AF.Exp, scale=-0.5), reads=[("nrstd", t % 2)], writes=[("nrstd", t % 2)] + mtk((4,)))

            def apply(t):
                rstd = Amf[:, 4096 + (t % 2) * 512: 4096 + (t % 2) * 512 + 512]
                for k in range(8):
                    tmp = Amf[:, 5120 + (k % 2) * 512: 5120 + (k % 2) * 512 + 512]
                    P.emit("dve", TT(tmp, hT[:, k, tl(t)], rstd, ALU.mult), reads=[("hT", k, t), ("nrstd", t % 2)] + mtk((4,)),
                           writes=[("ntmp", k % 2)] + mtk((5,)))
                    if k % 2 == 0:
                        P.emit("act", ACTF(dst[:, k, tl(t)], tmp, AF.Identity, bias=mod_col(sh_chunk0 + k, b), scale=gs_col(k, b)),
                               reads=[("ntmp", k % 2), "mod", "gs"] + mtk((5,)), writes=[("uT", k, t)])
                    else:
                        P.emit("dve", TS(dst[:, k, tl(t)], tmp, gs_col(k, b), mod_col(sh_chunk0 + k, b), ALU.mult, ALU.add),
                               reads=[("ntmp", k % 2), "mod", "gs"], writes=[("uT", k, t)])

            stats(0)
            for t in range(NT):
                if t + 1 < NT:
                    stats(t + 1)
                apply(t)

        Etabs = [Ahb[:, i * 1536:(i + 1) * 1536].rearrange("p (i q) -> p i q", i=12) for i in range(2)]
        qT = Ahb[:, 3072:5120]
        kT = Ahb[:, 5120:7168]
        Vt = [Ahb[:, 7168 + i * 2048: 7168 + (i + 1) * 2048].rearrange("p (b c) -> p b c", b=16) for i in range(3)]
        Pb = [Ahb[:, 13312 + i * 512: 13312 + (i + 1) * 512] for i in range(8)]
        acc_o = A_h[:, 8704:10752]
        acc_d = A_h[:, 10752:12800]

        def load_E(hp):
            P.emit("pool", DMA(Etabs[hp % 2], E_d[:, hp * 12:(hp + 1) * 12, :]), writes=[("Etab", hp % 2)], dma=True, arena=True)

        def tokset(pat, blk):
            if pat == 0:
                return 128 * blk, 1
            if pat == 1:
                return 512 * (blk % 4) + blk // 4, 4
            return blk, 16

        def cs_(ap2d, pat, blk):
            base, stp = tokset(pat, blk)
            return ap2d[:, base: base + 127 * stp + 1: stp]

        def acc_view(acc, pat, grp):
            if pat == 0:
                return acc[:, 512 * grp: 512 * grp + 512].rearrange("p (j a) -> p j a", j=4)
            if pat == 1:
                return xap(acc[:, 0:1], 512 * grp, [[1, 4], [4, 128]])
            return xap(acc[:, 0:1], 4 * grp, [[1, 4], [16, 128]])

        pbi = [0]

        def attention_hp(b, hp, key, wv, tick):
            if hp < 3:
                load_E(hp + 1)
            for which, dst, nm in ((0, qT, "qT"), (1, kT, "kT")):
                for t in range(NT):
                    b_ = nb()
                    P.emit("pe", MM([(ps[b_][:, :], wv[:, k, which * 128:(which + 1) * 128], uT[:, k, tl(t)], k == 0, k == 7)
                                     for k in range(8)]), reads=key + [("uT", k, t) for k in range(8)], writes=[bk(b_)])
                    evac(dst[:, tl(t)], ps[b_][:, :], [bk(b_)], [(nm, t)])
                    tick(force=True)
            for pat in range(3):
                for b4 in range(4):
                    b_ = nb()
                    lst = []
                    for jj in range(4):
                        blk = b4 * 4 + jj
                        for k in range(8):
                            lst.append((ps[b_][:, jj * 128:(jj + 1) * 128], cs_(uT[:, k, :], pat, blk), wv[:, k, 256:384],
                                        k == 0, k == 7))
                    P.emit("pe", MM(lst), reads=key + [("uT", k, t) for k in range(8) for t in range(NT)], writes=[bk(b_)])
                    evac(Vt[pat][:, b4 * 4:(b4 + 1) * 4, :], ps[b_][:, :].rearrange("p (j c) -> p j c", j=4), [bk(b_)],
                         [("V", pat, b4)])
                    tick(force=True)
            units = [(pat, grp, hd) for pat in range(3) for grp in range(4) for hd in range(2)]
            pend = None
            obank = {}

            def emit_qk(u):
                pat, grp, hd = u
                h = hp * 2 + hd
                rows = slice(hd * 64, hd * 64 + 64)
                info = {"cur": None, "prev": None, "mask": []}
                for typ in ("cur", "prev"):
                    lst = []
                    js = []
                    for j in range(4):
                        if pat == 0:
                            qb = 4 * grp + j
                            kb = qb if typ == "cur" else qb - 1
                            ok = kb >= 0
                        elif pat == 1:
                            qb = j * 4 + grp
                            kb = qb if typ == "cur" else qb - 1
                            ok = (typ == "cur") or grp >= 1
                        else:
                            qb = 4 * grp + j
                            kb = qb
                            ok = typ == "cur"
                        if ok:
                            js.append((j, qb, kb))
                    if not js:
                        continue
                    b_ = nb()
                    for (j, qb, kb) in js:
                        lst.append((ps[b_][:, j * 128:(j + 1) * 128], cs_(kT[rows, :], pat, kb), cs_(qT[rows, :], pat, qb), True, True))
                    P.emit("pe", MM(lst), reads=[("qT", t) for t in range(NT)] + [("kT", t) for t in range(NT)], writes=[bk(b_)])
                    j0 = js[0][0]
                    pi = pbi[0]
                    pbi[0] = (pi + 1) % 8
                    pv = Pb[pi][:, j0 * 128:512]
                    P.emit("act", ACTF(pv, ps[b_][:, j0 * 128:512], AF.Exp, bias=nshift_col, scale=0.125),
                           reads=[bk(b_), "consts"], writes=[("Pb", pi)])
                    ei = (hd * 3 + pat) * 2 + (0 if typ == "cur" else 1)
                    nj = 4 - j0
                    ebc = xap(Etabs[hp % 2][:, ei, 0:1], 0, [[0, nj], [1, 128]])
                    pv3 = pv.rearrange("p (j a) -> p j a", j=nj)
                    info["mask"].append((pv3, ebc, pi))
                    info[typ] = (pi, js)
                return info

            def emit_mask(info):
                for (pv3, ebc, pi) in info["mask"]:
                    P.emit("dve", TT(pv3, pv3, ebc, ALU.mult), reads=[("Pb", pi), ("Etab", hp % 2)], writes=[("Pb", pi)])

            def emit_pv(u, info):
                pat, grp, hd = u
                rows = slice(hd * 64, hd * 64 + 64)
                if hd == 0:
                    obank[(pat, grp)] = (6, 7)
                bo, bd = obank[(pat, grp)]
                lst = []
                reads = [("V", pat, b4) for b4 in range(4)] + ["ones"]
                pcur, jcur = info["cur"]
                reads.append(("Pb", pcur))
                prevmap = {}
                if info["prev"] is not None:
                    pprev, jprev = info["prev"]
                    reads.append(("Pb", pprev))
                    prevmap = {j: kb for (j, qb, kb) in jprev}
                for (j, qb, kb) in jcur:
                    oc = slice(j * 128, (j + 1) * 128)
                    hasp = j in prevmap
                    for dst, isden in ((ps[bo], False), (ps[bd], True)):
                        if hasp:
                            lw = onesb[:, 0:64] if isden else Vt[pat][:, prevmap[j], rows]
                            lst.append((dst[rows, oc], lw, Pb[pprev][:, oc], True, False))
                        lw = onesb[:, 0:64] if isden else Vt[pat][:, kb, rows]
                        lst.append((dst[rows, oc], lw, Pb[pcur][:, oc], not hasp, True))
                P.emit("pe", MM(lst), reads=reads, writes=[bk(bo), bk(bd)])
                if hd == 1:
                    gk = [grp] if pat < 2 else [0, 1, 2, 3]
                    for acc, bb, nm in ((acc_o, bo, "acc_o"), (acc_d, bd, "acc_d")):
                        av = acc_view(acc, pat, grp)
                        src = ps[bb][:, :].rearrange("p (j a) -> p j a", j=4)
                        if pat == 0:
                            evac(av, src, [bk(bb)], [(nm, g) for g in gk])
                        else:
                            P.emit("dve", TT(av, av, src, ALU.add), reads=[bk(bb)] + [(nm, g) for g in gk],
                                   writes=[(nm, g) for g in gk])

            infos = []
            for ui, u in enumerate(units):
                infos.append(emit_qk(u))
                tick()
                if ui >= 1:
                    emit_mask(infos[ui - 1])
                if ui >= 2:
                    emit_pv(units[ui - 2], infos[ui - 2])
                tick()
            nU = len(units)
            emit_mask(infos[nU - 1])
            emit_pv(units[nU - 2], infos[nU - 2])
            emit_pv(units[nU - 1], infos[nU - 1])
            P.emit("act", ACTF(acc_d, acc_d, AF.Ln), reads=[("acc_d", g) for g in range(4)], writes=[("acc_d", g) for g in range(4)])
            P.emit("act", ACTF(acc_d, acc_d, AF.Exp, scale=-1.0), reads=[("acc_d", g) for g in range(4)], writes=[("acc_d", g) for g in range(4)])
            P.emit("dve", TT(oatt[:, hp, :], acc_o, acc_d, ALU.mult), reads=[("acc_o", g) for g in range(4)] + [("acc_d", g) for g in range(4)],
                   writes=[("oatt", hp)])

        cosT = slots[3][:, :].bitcast(F32).rearrange("p (s c) -> p s c", s=4)
        sinT = p3buf[:, :, 0:512]
        wk = lambda i: A_h[:, 12800 + i * 512: 12800 + (i + 1) * 512]
        xb = lambda i: A_m[:, 12288 + i * 512: 12288 + (i + 1) * 512]
        Btab = A_m[:, 0:2048].rearrange("p (s r c) -> p s r c", s=8, r=2)
        Cre = A_m[:, 2048:3072].rearrange("p (s c) -> p s c", s=8)
        Cim = A_m[:, 3072:4096].rearrange("p (s c) -> p s c", s=8)
        Dx = A_m[:, 4096:4352].rearrange("p (s c) -> p s c", s=2)
        yg = A_m[:, 4352:4352 + 4096].rearrange("p (h t) -> p h t", h=2)
        ysb = Amf[:, 4608:5120]
        yt = Amf[:, 5120:5632]
        ysig = Amf[:, 5632:6144]

        def ssm_tables():
            P.emit("sp", DMA(Btab, btab_scr[:, :, :, :]), reads=["btab_scr"], writes=["Btab"], dma=True, arena=True)
            P.emit("pool", DMA(Cre, cre_x[:, :, :]), writes=["Cre"], dma=True, arena=True)
            P.emit("pool", DMA(Cim, cim_x[:, :, :]), writes=["Cim"], dma=True, arena=True)
            P.emit("pool", DMA(Dx, d_x[:, :, :]), writes=["Dx"], dma=True, arena=True)

        def us_proj(b, key, wv):
            for c in range(2):
                for t in range(NT):
                    b_ = nb()
                    P.emit("pe", MM([(ps[b_][:, :], wv[:, k, c * 128:(c + 1) * 128], uT[:, k, tl(t)], k == 0, k == 7) for k in range(8)]),
                           reads=key + [("uT", k, t) for k in range(8)], writes=[bk(b_)])
                    evac(usT[:, c, tl(t)], ps[b_][:, :], [bk(b_)], [("usT", c, t)])

        def ssm_gen(b, kglu, wglu):
            for h in range(2):
                P.emit("sp", DMA(cosT, trig_scr[:, 0, 4 * h:4 * h + 4, :]), reads=[("trig_scr", 0, sc) for sc in range(8)],
                       writes=[("slot", 3, 0), ("slot", 3, 1)], dma=True, arena=True)
                P.emit("sp", DMA(sinT, trig_scr[:, 1, 4 * h:4 * h + 4, :]), reads=[("trig_scr", 1, sc) for sc in range(8)],
                       writes=["sinT"], dma=True, arena=True)
                ck = [("slot", 3, 0), ("slot", 3, 1)]
                for tc in range(NT):
                    xbufs = []
                    for s4 in range(4):
                        sc = h * 4 + s4
                        br_, bi_ = nb(), nb()
                        P.emit("pe", MM([(ps[br_][:, :], Btab[:, sc, 0, :], usT[:, h, tl(tc)], True, True)]),
                               reads=["Btab", ("usT", h, tc)], writes=[bk(br_)])
                        P.emit("pe", MM([(ps[bi_][:, :], Btab[:, sc, 1, :], usT[:, h, tl(tc)], True, True)]),
                               reads=["Btab", ("usT", h, tc)], writes=[bk(bi_)])
                        c_, s_ = cosT[:, s4, :], sinT[:, s4, :]
                        W0, W1, W2, W3, W4, W5 = [wk(i) for i in range(6)]
                        K = lambda n: ("wk", n)
                        PR, PI = ps[br_][:, :], ps[bi_][:, :]
                        d = lambda fn, r, w: P.emit("dve", fn, reads=r, writes=w)
                        d(TT(W0, PR, c_, ALU.mult), [bk(br_)] + ck, [K(0)])
                        d(TT(W1, PI, s_, ALU.mult), [bk(bi_), "sinT"], [K(1)])
                        d(TT(W0, W0, W1, ALU.add), [K(0), K(1)], [K(0)])
                        d(TT(W1, PI, c_, ALU.mult), [bk(bi_)] + ck, [K(1)])
                        d(TT(W2, PR, s_, ALU.mult), [bk(br_), "sinT"], [K(2)])
                        d(TT(W1, W1, W2, ALU.subtract), [K(1), K(2)], [K(1)])
                        yield 1
                        magb = xap(mag_col(sc), 0, [[0, 512]])
                        ir = 0.0 if tc == 0 else carr_col(sc)
                        ii = 0.0 if tc == 0 else cari_col(sc)
                        d(SCAN(W3, magb, W0, ir), [K(0), "mag", ("car", sc)], [K(3)])
                        d(SCAN(W4, magb, W1, ii), [K(1), "mag", ("car", sc)], [K(4)])
                        yield 1
                        xr_b, xi_b = xb(2 * s4), xb(2 * s4 + 1)
                        kxr, kxi = ("xb", 2 * s4), ("xb", 2 * s4 + 1)
                        d(TT(W0, c_, W3, ALU.mult), [K(3)] + ck, [K(0)])
                        d(TT(W2, s_, W4, ALU.mult), [K(4), "sinT"], [K(2)])
                        d(TT(xr_b, W0, W2, ALU.subtract), [K(0), K(2)], [kxr])
                        d(TT(carr_col(sc), W0[:, 511:512], W2[:, 511:512], ALU.subtract), [K(0), K(2)], [("car", sc)])
                        yield 1
                        d(TT(W5, s_, W3, ALU.mult), [K(3), "sinT"], [K(5)])
                        d(TT(W2, c_, W4, ALU.mult), [K(4)] + ck, [K(2)])
                        d(STT(xi_b, W5, -1.0, W2, ALU.mult, ALU.subtract), [K(5), K(2)], [kxi])
                        d(TT(cari_col(sc), W5[:, 511:512], W2[:, 511:512], ALU.add), [K(5), K(2)], [("car", sc)])
                        xbufs.append((sc, xr_b, xi_b, kxr, kxi))
                        yield 1
                    by = nb()
                    lst = []
                    reads = ["Cre", "Cim", "Dx", ("usT", h, tc)]
                    for i, (sc, xr_b, xi_b, kxr, kxi) in enumerate(xbufs):
                        lst.append((ps[by][:, :], Cre[:, sc, :], xr_b, i == 0, False))
                        lst.append((ps[by][:, :], Cim[:, sc, :], xi_b, False, False))
                        reads += [kxr, kxi]
                    lst.append((ps[by][:, :], Dx[:, h, :], usT[:, h, tl(tc)], False, True))
                    P.emit("pe", MM(lst), reads=reads, writes=[bk(by)])
                    P.emit("act", ACP(ysb, ps[by][:, :]), reads=[bk(by)], writes=["ysb"])
                    P.emit("dve", TT(yt, ysb, ysb, ALU.mult), reads=["ysb"], writes=["yt"])
                    P.emit("dve", TS(yt, yt, 0.044715, 1.0, ALU.mult, ALU.add), reads=["yt"], writes=["yt"])
                    P.emit("dve", TT(yt, yt, ysb, ALU.mult), reads=["yt", "ysb"], writes=["yt"])
                    P.emit("act", ACTF(ysig, yt, AF.Sigmoid, scale=1.5957691216057308), reads=["yt"], writes=["ysig"])
                    P.emit("dve", TT(yg[:, h, tl(tc)], ysb, ysig, ALU.mult), reads=["ysb", "ysig"], writes=[("yg", h, tc)])
            for c in range(2):
                for t in range(NT):
                    b_ = nb()
                    P.emit("pe", MM([(ps[b_][:, :], wglu[:, hh, c * 128:(c + 1) * 128], yg[:, hh, tl(t)], hh == 0, hh == 1) for hh in range(2)]),
                           reads=kglu + [("yg", 0, t), ("yg", 1, t)], writes=[bk(b_)])
                    P.emit("act", ACTF(ysig, ps[b_][:, :], AF.Sigmoid, bias=smv(V_BGLU + c)), reads=[bk(b_), "vec"], writes=["ysig"])
                    P.emit("dve", TT(ssmo[:, c, tl(t)], yg[:, c, tl(t)], ysig, ALU.mult), reads=["ysig", ("yg", c, t)],
                           writes=[("ssmo", c, t)])

        def load_x(b):
            for t in range(NT):
                for k in range(8):
                    P.emit("sp", DMA(hT[:, k, tl(t)], xT[b, k * 128:(k + 1) * 128, tl(t)]), writes=[("hT", k, t)], dma=True, arena=True)

        def mixer_q(b, q, kpa, wpa, kps, wps, kg, wga, wgs):
            for jj in range(2):
                j = q * 2 + jj
                jc = slice(j * 128, (j + 1) * 128)
                jjc = slice(jj * 128, (jj + 1) * 128)
                for t in range(NT):
                    b1, b2, b3, b4 = nb(), nb(), nb(), nb()
                    P.emit("pe", MM([(ps[b3][:, :], wga[:, k, jjc], uT[:, k, tl(t)], k == 0, k == 7) for k in range(8)]),
                           reads=kg + [("uT", k, t) for k in range(8)], writes=[bk(b3)])
                    P.emit("pe", MM([(ps[b4][:, :], wgs[:, k, jjc], uT[:, k, tl(t)], k == 0, k == 7) for k in range(8)]),
                           reads=kg + [("uT", k, t) for k in range(8)], writes=[bk(b4)])
                    P.emit("pe", MM([(ps[b1][:, :], wpa[:, k, jc], oatt[:, k, tl(t)], k == 0, k == 3) for k in range(4)]),
                           reads=kpa + [("oatt", k) for k in range(4)], writes=[bk(b1)])
                    P.emit("pe", MM([(ps[b2][:, :], wps[:, k, jc], ssmo[:, k, tl(t)], k == 0, k == 1) for k in range(2)]),
                           reads=kps + [("ssmo", 0, t), ("ssmo", 1, t)], writes=[bk(b2)])
                    sa, ss_, m1, m2 = (p3buf[:, i, 0:512] for i in range(4))
                    P.emit("act", ACTF(sa, ps[b3][:, :], AF.Sigmoid, bias=smv(V_BGATE + j)), reads=[bk(b3), "vec"], writes=["sa"])
                    P.emit("act", ACTF(ss_, ps[b4][:, :], AF.Sigmoid, bias=smv(V_BGATE + 8 + j)), reads=[bk(b4), "vec"], writes=["ss"])
                    P.emit("dve", TT(m1, sa, ps[b1][:, :], ALU.mult), reads=["sa", bk(b1)], writes=["m1"])
                    P.emit("dve", TT(m2, ss_, ps[b2][:, :], ALU.mult), reads=["ss", bk(b2)], writes=["m2"])
                    P.emit("dve", TT(mT[:, j, tl(t)], m1, m2, ALU.add), reads=["m1", "m2"], writes=[("mT", j, t)])

        def wout_half(b, half, kwo, wo):
            for jj in range(4):
                j = half * 4 + jj
                for t in range(NT):
                    b_ = nb()
                    P.emit("pe", MM([(ps[b_][:, :], wo[:, k, jj * 128:(jj + 1) * 128], mT[:, k, tl(t)], k == 0, k == 7) for k in range(8)]),
                           reads=kwo + [("mT", k, t) for k in range(8)], writes=[bk(b_)])
                    P.emit("dve", STT(hT[:, j, tl(t)], ps[b_][:, :], mod_col(16 + j, b), hT[:, j, tl(t)], ALU.mult, ALU.add),
                           reads=[bk(b_), "mod", ("hT", j, t)], writes=[("hT", j, t)])

        gT = mT

        def ffn_up(b, hf, fb, kwa, wa, kwv, wvv):
            for ff in range(4):
                f = hf * 8 + fb * 4 + ff
                fl = fb * 4 + ff
                fc = slice(ff * 128, (ff + 1) * 128)
                for t in range(NT):
                    ba, bv = nb(), nb()
                    P.emit("pe", MM([(ps[ba][:, :], wa[:, k, fc], uT[:, k, tl(t)], k == 0, k == 7) for k in range(8)]),
                           reads=kwa + [("uT", k, t) for k in range(8)], writes=[bk(ba)])
                    P.emit("pe", MM([(ps[bv][:, :], wvv[:, k, fc], uT[:, k, tl(t)], k == 0, k == 7) for k in range(8)]),
                           reads=kwv + [("uT", k, t) for k in range(8)], writes=[bk(bv)])
                    ab_ = p3buf[:, t % 2, :]
                    pvb = p3buf[:, (t + 1) % 2, :]
                    if t == 0:
                        P.emit("dve", MEMSET(ab_[:, 0:2], 0.0), writes=[("asbh", t % 2)])
                    else:
                        P.emit("act", ACP(ab_[:, 0:2], pvb[:, 512:514]), reads=[("asb", (t + 1) % 2)], writes=[("asbh", t % 2)])
                    P.emit("act", ACP(ab_[:, 2:514], ps[ba][:, :]), reads=[bk(ba)], writes=[("asb", t % 2)])
                    cv = p3buf[:, 2, 0:512]
                    sl = p3buf[:, 3, 0:512]
                    wc = lambda jtap, f=f: smv(V_WCONV + f * 3 + jtap)
                    rk = [("asb", t % 2), ("asbh", t % 2), "vec"]
                    P.emit("dve", TS(cv, ab_[:, 2:514], wc(0), smv(V_BCONV + f), ALU.mult, ALU.add), reads=rk, writes=["cv"])
                    P.emit("dve", STT(cv, ab_[:, 1:513], wc(1), cv, ALU.mult, ALU.add), reads=rk + ["cv"], writes=["cv"])
                    P.emit("dve", STT(cv, ab_[:, 0:512], wc(2), cv, ALU.mult, ALU.add), reads=rk + ["cv"], writes=["cv"])
                    P.emit("act", ACTF(sl, cv, AF.Silu), reads=["cv"], writes=["sl"])
                    P.emit("dve", TT(gT[:, fl, tl(t)], sl, ps[bv][:, :], ALU.mult), reads=["sl", bk(bv)], writes=[("mT", fl, t)])

        def ffn_down(b, hf, cb, kwd, wd):
            for jj in range(4):
                j = cb * 4 + jj
                for t in range(NT):
                    b_ = nb()
                    P.emit("pe", MM([(ps[b_][:, :], wd[:, k, jj * 128:(jj + 1) * 128], gT[:, k, tl(t)], k == 0, k == 7) for k in range(8)]),
                           reads=kwd + [("mT", k, t) for k in range(8)], writes=[bk(b_)])
                    P.emit("dve", STT(hT[:, j, tl(t)], ps[b_][:, :], mod_col(40 + j, b), hT[:, j, tl(t)], ALU.mult, ALU.add),
                           reads=[bk(b_), "mod", ("hT", j, t)], writes=[("hT", j, t)])

        def final_norm(b):
            for t in range(NT):
                sq = A_m[:, (t % 2) * 4096:(t % 2) * 4096 + 4096].rearrange("p (k c) -> p k c", k=8)
                rstd = Amf[:, 4096 + (t % 2) * 512: 4096 + (t % 2) * 512 + 512]
                P.emit("act", ACTF(sq, hT[:, :, tl(t)], AF.Square), reads=[("hT", k, t) for k in range(8)], writes=[("nsq", t % 2)])
                b_ = nb()
                P.emit("pe", MM([(ps[b_][:, :], onesb[:, :], sq[:, k, :], k == 0, k == 7) for k in range(8)]),
                       reads=[("nsq", t % 2), "ones"], writes=[bk(b_)])
                P.emit("act", ACTF(rstd, ps[b_][:, :], AF.Ln, bias=eps_col, scale=1.0 / D), reads=[bk(b_), "consts"], writes=[("nrstd", t % 2)])
                P.emit("act", ACTF(rstd, rstd, AF.Exp, scale=-0.5), reads=[("nrstd", t % 2)], writes=[("nrstd", t % 2)])
                for k in range(8):
                    si = (t * 8 + k) % 4
                    stg = Amf[:, 6144 + si * 512: 6144 + (si + 1) * 512]
                    P.emit("dve", STT(stg, hT[:, k, tl(t)], smv(V_GFIN + k), rstd, ALU.mult, ALU.mult),
                           reads=[("hT", k, t), ("nrstd", t % 2), "vec"], writes=[("ostg", si)])
                    P.emit("sp", DMA(outT[b, k * 128:(k + 1) * 128, tl(t)], stg),
                           reads=[("ostg", si)], writes=[("outT", b, k, t)], dma=True, arena=True)

        stages = []

        def gq_src(q):
            return [("ga", w_in[:, 1792 + q * 256: 1792 + (q + 1) * 256], 8, 256, 2 if q % 2 == 0 else 3, 0),
                    ("gs", w_in[:, 2816 + q * 256: 2816 + (q + 1) * 256], 8, 256, 2 if q % 2 == 0 else 3, 2048)]

        gen_state = {}

        for b in range(NB):
            def c_us(W, b=b):
                load_x(b)
                norm_mod(b, gs1_col, 0, uT)
                P.barrier()
                ssm_tables()
                load_E(0)
                us_proj(b, *W["us"])
                gen_state["gen"] = ssm_gen(b, *W["glu"])
            stages.append(([("us", w_in[:, 1536:1792], 8, 256, 2, 0), ("glu", w_glu[:, :], 2, 256, 1, 3072)], c_us))

            def tick(force=False):
                gen_state["n"] = gen_state.get("n", 0) + 1
                if not force and gen_state["n"] % 4 != 0:
                    return
                g = gen_state.get("gen")
                if g is not None:
                    try:
                        next(g)
                    except StopIteration:
                        gen_state["gen"] = None

            for hp in range(4):
                def c_att(W, b=b, hp=hp, tick=tick):
                    attention_hp(b, hp, *W["w"], tick)
                stages.append(([("w", w_in[:, hp * 384:(hp + 1) * 384], 8, 384, hp % 2, 0)], c_att))

            def c_tail(W, tick=tick):
                while gen_state.get("gen") is not None:
                    tick(force=True)
            stages.append(([], c_tail))
            for q in range(4):
                def c_mq(W, b=b, q=q, st_=state):
                    if q == 0:
                        st_["pa"] = W["pa"]
                        st_["ps"] = W["ps"]
                        P.barrier()
                        load_x(b)
                    kg = W["ga"][0] + W["gs"][0]
                    mixer_q(b, q, st_["pa"][0], st_["pa"][1], st_["ps"][0], st_["ps"][1], kg, W["ga"][1], W["gs"][1])
                lds = gq_src(q)
                if q == 0:
                    lds = [("pa", w_pa[:, :], 4, 1024, 0, 0), ("ps", w_ps[:, :], 2, 1024, 1, 0)] + lds
                stages.append((lds, c_mq))
            for half in range(2):
                def c_wo(W, b=b, half=half):
                    wout_half(b, half, *W["wo"])
                stages.append(([("wo", w_out[:, half * 512:(half + 1) * 512], 8, 512, 2 if half == 0 else 3, 0)], c_wo))
            fslots = [(0, 1), (2, 3), (0,), (1,), (2, 3), (0, 1), (2,), (3,)]
            fi = 0
            for hf in range(2):
                for fb in range(2):
                    c0 = hf * 1024 + fb * 512

                    def c_up(W, b=b, hf=hf, fb=fb):
                        if hf == 0 and fb == 0:
                            P.barrier()
                            norm_mod(b, gs2_col, 24, uT)
                        ffn_up(b, hf, fb, W["wa"][0], W["wa"][1], W["wv"][0], W["wv"][1])
                    stages.append(([("wa", w_up[:, c0:c0 + 512], 8, 512, fslots[fi][0], 0),
                                    ("wv", w_up[:, 2048 + c0:2048 + c0 + 512], 8, 512, fslots[fi][1], 0)], c_up))
                    fi += 1
                for cb in range(2):
                    def c_dn(W, b=b, hf=hf, cb=cb):
                        ffn_down(b, hf, cb, *W["wd"])
                        if hf == 1 and cb == 1:
                            P.barrier()
                            final_norm(b)
                    stages.append(([("wd", w_down[hf * 1024:(hf + 1) * 1024, cb * 512:(cb + 1) * 512], 8, 512, fslots[fi][0], 0)], c_dn))
                    fi += 1

        def do_loads(lds):
            return {nm: load_w(src, nk, ncols, slot=sl_, off=off) for (nm, src, nk, ncols, sl_, off) in lds}

        Wn = do_loads(stages[0][0])
        P.barrier()
        for i, (lds, comp) in enumerate(stages):
            Wc = Wn
            if i + 1 < len(stages):
                Wn = do_loads(stages[i + 1][0])
            comp(Wc)
        P.final_wait("sp")
        P.build()
    return nc


def _host_layouts(inp, NB):
    f32 = np.float32
    x = np.asarray(inp["x"], f32)
    c = np.asarray(inp["c"], f32)
    col8 = lambda v: np.ascontiguousarray(np.asarray(v, f32).reshape(-1, 128).T)
    w_in = np.asarray(inp["w_in"][0], f32)
    perm = []
    for hp in range(4):
        for sec in range(3):
            perm += list(range(sec * 512 + hp * 128, sec * 512 + (hp + 1) * 128))
    perm += list(range(1536, 3840))
    w_in_p = np.ascontiguousarray(w_in[:, perm])
    a_re, a_im, ldt = inp["a_re"][0], inp["a_im"][0], inp["log_dt"][0]
    st_major = lambda a: np.ascontiguousarray(np.asarray(a, f32).reshape(8, 128).T)
    ldt_l = np.ascontiguousarray(np.repeat(np.asarray(ldt, f32), 64).reshape(8, 128).T)
    wconv = np.asarray(inp["w_conv"][0], f32)
    wconv_l = np.ascontiguousarray(wconv.reshape(3, 16, 128).transpose(2, 1, 0).reshape(128, 48))
    vec = np.concatenate([
        col8(inp["g_mix"][0]), col8(inp["g_ffn"][0]), col8(inp["g_final"]), col8(inp["b_gate"][0]),
        col8(inp["b_glu"][0]), wconv_l, col8(inp["b_conv"][0]), st_major(a_re), st_major(a_im), ldt_l,
        col8(inp["b_ada"][0])], axis=1).astype(f32)
    assert vec.shape == (128, NV), vec.shape

    def expand_b(bmat):
        out = np.zeros((128, 8, 128), f32)
        for g in range(16):
            sc, gl = g // 2, g % 2
            c0 = 32 * (sc % 4) + 16 * gl
            out[gl * 64:(gl + 1) * 64, sc, c0:c0 + 16] = bmat[g]
        return out

    def expand_c(cmat):
        out = np.zeros((128, 8, 128), f32)
        for g in range(16):
            sc, gl = g // 2, g % 2
            c0 = 32 * (sc % 4) + 16 * gl
            out[gl * 64:(gl + 1) * 64, sc, c0:c0 + 16] = cmat[g].T
        return out

    d = np.asarray(inp["d_skip"][0], f32)
    d_x = np.zeros((128, 2, 128), f32)
    for h in range(2):
        d_x[np.arange(128), h, np.arange(128)] = d[h * 128:(h + 1) * 128]
    ak = np.arange(128, dtype=np.float64)[:, None]
    aq = np.arange(128, dtype=np.float64)[None, :]
    E = np.zeros((128, 48, 128), f32)
    for h in range(8):
        slope = 2.0 ** (-8.0 * (h + 1) / 8)
        for p, dil in enumerate((1, 4, 16)):
            cur = np.where(ak <= aq, np.exp(-slope * dil * (aq - ak) - 0.0), 0.0)
            prv = np.where(ak >= aq, np.exp(-slope * dil * (128 + aq - ak)), 0.0)
            E[:, (h * 3 + p) * 2 + 0, :] = cur
            E[:, (h * 3 + p) * 2 + 1, :] = prv
    common = dict(
        w_ada=np.ascontiguousarray(inp["w_ada"][0], f32), w_in_p=w_in_p, vec=vec,
        bre_x=expand_b(np.asarray(inp["b_re"][0], f32)), bim_x=expand_b(np.asarray(inp["b_im"][0], f32)),
        cre_x=expand_c(np.asarray(inp["c_re"][0], f32)), cim_x=expand_c(np.asarray(inp["c_im"][0], f32)),
        d_x=d_x, ident=np.eye(128, dtype=f32), iota=np.tile(np.arange(1, 513, dtype=f32)[None, :], (128, 1)), E=E,
        w_glu=np.ascontiguousarray(inp["w_glu"][0], f32), w_proj_att=np.ascontiguousarray(inp["w_proj_att"][0], f32),
        w_proj_ssm=np.ascontiguousarray(inp["w_proj_ssm"][0], f32), w_out=np.ascontiguousarray(inp["w_out"][0], f32),
        w_up=np.ascontiguousarray(inp["w_up"][0], f32), w_down=np.ascontiguousarray(inp["w_down"][0], f32))
    maps = []
    for core in range(NCORES):
        bs = slice(core * NB, (core + 1) * NB)
        m = dict(common)
        m["xT"] = np.ascontiguousarray(x[bs].transpose(0, 2, 1))
        m["cT"] = np.ascontiguousarray(c[bs].reshape(NB, 8, 128).transpose(2, 1, 0))
        maps.append(m)
    return maps


def kernel(**inputs):
    B = inputs["x"].shape[0]
    NB = B // NCORES
    maps = _host_layouts(inputs, NB)
    nc = build_nc(NB)
    res = run_bass_kernel_spmd(nc, maps, core_ids=list(range(NCORES)))
    outs = [np.asarray(r["outT"]).transpose(0, 2, 1) for r in res.results]
    return np.ascontiguousarray(np.concatenate(outs, axis=0).astype(np.float32))
```
